# Optimizing a Trainium2 kernel written in Bass

```python
import math
import jax, jax.numpy as jnp
from jax import lax
import numpy as np

D_MODEL = 1024
BATCH = 4
SEQ = 8192
DEPTH = 1

ATTN_HEADS = 8
ATTN_KV_HEADS = 2
ATTN_GROUP = ATTN_HEADS // ATTN_KV_HEADS
HEAD_DIM = 64
WINDOW = 128
BLOCK = 128
NUM_BUCKETS = 32
MAX_DISTANCE = 128
GLA_HEADS = 4
GLA_KEY_DIM = D_MODEL // 2
GLA_VAL_DIM = D_MODEL
GLA_HEAD_K = GLA_KEY_DIM // GLA_HEADS
GLA_HEAD_V = GLA_VAL_DIM // GLA_HEADS
GLA_GATE_RANK = 16
GLA_GATE_TAU = 16.0
GLA_CHUNK = 64
D_FF = -(-8 * D_MODEL // (3 * 256)) * 256
PLE_DIM = 256
EPS = 1e-6

SPLITS = (ATTN_HEADS * HEAD_DIM, ATTN_KV_HEADS * HEAD_DIM, ATTN_KV_HEADS * HEAD_DIM,
          GLA_KEY_DIM, GLA_KEY_DIM, GLA_VAL_DIM, GLA_GATE_RANK, GLA_VAL_DIM,
          D_MODEL, D_MODEL)
D_IN = (ATTN_HEADS * HEAD_DIM + 2 * ATTN_KV_HEADS * HEAD_DIM + 2 * GLA_KEY_DIM
        + 2 * GLA_VAL_DIM + GLA_GATE_RANK + 2 * D_MODEL)

kernel_name = "hybrid_swa_gla_gated_merge"


def rms_norm(x, gain):
    xf = x.astype(jnp.float32)
    y = xf * lax.rsqrt(jnp.mean(xf * xf, axis=-1, keepdims=True) + EPS)
    return (y * gain.astype(jnp.float32)).astype(x.dtype)


def t5_bucket(dist):
    max_exact = NUM_BUCKETS // 2
    d_f = jnp.maximum(dist, max_exact).astype(jnp.float32)
    large = max_exact + (jnp.log(d_f / max_exact) / math.log(MAX_DISTANCE / max_exact)
                         * (NUM_BUCKETS - max_exact)).astype(jnp.int32)
    large = jnp.minimum(large, NUM_BUCKETS - 1)
    return jnp.where(dist < max_exact, dist, large)


def band_relative_bias(rel_table):
    q_loc = jnp.arange(BLOCK)[:, None] + BLOCK
    k_loc = jnp.arange(2 * BLOCK)[None, :]
    bucket = t5_bucket(jnp.maximum(q_loc - k_loc, 0))
    bias = rel_table.astype(jnp.float32)[bucket]
    return jnp.transpose(bias, (2, 0, 1)).reshape(ATTN_KV_HEADS, ATTN_GROUP, BLOCK, 2 * BLOCK)


def sliding_window_attention(q, k, v, sinks, rel_table):
    B, S = q.shape[0], q.shape[1]
    nb = S // BLOCK
    qb = q.astype(jnp.float32).reshape(B, nb, BLOCK, ATTN_KV_HEADS, ATTN_GROUP, HEAD_DIM)

    def band(t):
        t = jnp.pad(t.astype(jnp.float32), ((0, 0), (BLOCK, 0), (0, 0), (0, 0)))
        t = t.reshape(B, nb + 1, BLOCK, ATTN_KV_HEADS, HEAD_DIM)
        return jnp.concatenate([t[:, :-1], t[:, 1:]], axis=2)

    kb, vb = band(k), band(v)
    s = jnp.einsum('bnqhgd,bnkhd->bnhgqk', qb, kb) * (HEAD_DIM ** -0.5)
    s = s + band_relative_bias(rel_table)
    dist = (jnp.arange(BLOCK)[:, None] + BLOCK) - jnp.arange(2 * BLOCK)[None, :]
    in_window = (dist >= 0) & (dist < WINDOW)
    k_pos = jnp.arange(nb)[:, None] * BLOCK - BLOCK + jnp.arange(2 * BLOCK)[None, :]
    valid = in_window[None] & (k_pos >= 0)[:, None, :]
    s = jnp.where(valid[None, :, None, None], s, -1e30)
    sink = sinks.astype(jnp.float32).reshape(ATTN_KV_HEADS, ATTN_GROUP)[:, :, None, None]
    m = jnp.maximum(jnp.max(s, axis=-1, keepdims=True), sink)
    pr = jnp.exp(s - m)
    denom = jnp.sum(pr, axis=-1, keepdims=True) + jnp.exp(sink - m)
    o = jnp.einsum('bnhgqk,bnkhd->bnqhgd', pr / denom, vb)
    return o.reshape(B, S, ATTN_HEADS * HEAD_DIM)


def gla_chunked(q, k, v, log_a):
    B, S = q.shape[0], q.shape[1]
    nc = S // GLA_CHUNK

    def chunks(t, d):
        t = t.astype(jnp.float32).reshape(B, nc, GLA_CHUNK, GLA_HEADS, d)
        return t.transpose(1, 0, 3, 2, 4)

    qc = chunks(q, GLA_HEAD_K) * (GLA_HEAD_K ** -0.5)
    kc = chunks(k, GLA_HEAD_K)
    vc = chunks(v, GLA_HEAD_V)
    gc = jnp.cumsum(chunks(log_a, GLA_HEAD_K), axis=-2)
    causal = jnp.tril(jnp.ones((GLA_CHUNK, GLA_CHUNK), dtype=bool))[:, :, None]

    def step(state, inp):
        qi, ki, vi, gi = inp
        o_inter = jnp.einsum('bhtd,bhde->bhte', qi * jnp.exp(gi), state)
        decay = jnp.exp(jnp.where(causal, gi[:, :, :, None, :] - gi[:, :, None, :, :], -jnp.inf))
        scores = jnp.einsum('bhtsd,bhsd->bhts', qi[:, :, :, None, :] * decay, ki)
        o_intra = jnp.einsum('bhts,bhse->bhte', scores, vi)
        g_last = gi[:, :, -1:, :]
        k_dec = ki * jnp.exp(g_last - gi)
        state = state * jnp.exp(g_last[:, :, 0, :, None]) + jnp.einsum('bhsd,bhse->bhde', k_dec, vi)
        return state, o_inter + o_intra

    state0 = jnp.zeros((B, GLA_HEADS, GLA_HEAD_K, GLA_HEAD_V), jnp.float32)
    _, o = lax.scan(step, state0, (qc, kc, vc, gc))
    return o.transpose(1, 0, 3, 2, 4).reshape(B, S, GLA_HEADS, GLA_HEAD_V)


def hybrid_layer(x, p_i, w_in, w_gk2, b_gk, sinks, rel_table, w_proj_attn, w_proj_gla,
                 gla_norm, w_out, norm_mix, norm_ffn, w_ffn_in, w_ffn_out, norm_ple,
                 w_ple_gate, w_ple):
    B, S, _ = x.shape
    h = rms_norm(x, norm_mix)
    proj = h @ w_in
    points = np.cumsum(SPLITS)[:-1].tolist()
    q_a, k_a, v_a, q_g, k_g, v_g, gk_low, og, gate_a, gate_g = jnp.split(proj, points, axis=-1)

    o_a = sliding_window_attention(q_a.reshape(B, S, ATTN_HEADS, HEAD_DIM),
                                   k_a.reshape(B, S, ATTN_KV_HEADS, HEAD_DIM),
                                   v_a.reshape(B, S, ATTN_KV_HEADS, HEAD_DIM),
                                   sinks, rel_table)
    y_a = o_a.astype(x.dtype) @ w_proj_attn

    log_a = jax.nn.log_sigmoid((gk_low @ w_gk2 + b_gk).astype(jnp.float32)) / GLA_GATE_TAU
    o_g = gla_chunked(q_g.reshape(B, S, GLA_HEADS, GLA_HEAD_K),
                      k_g.reshape(B, S, GLA_HEADS, GLA_HEAD_K),
                      v_g.reshape(B, S, GLA_HEADS, GLA_HEAD_V),
                      log_a.reshape(B, S, GLA_HEADS, GLA_HEAD_K))
    o_g = rms_norm(o_g, gla_norm).reshape(B, S, GLA_VAL_DIM) * jax.nn.silu(og.astype(jnp.float32))
    y_g = o_g.astype(x.dtype) @ w_proj_gla

    mixed = jax.nn.sigmoid(gate_a) * y_a + jax.nn.sigmoid(gate_g) * y_g
    x = x + mixed @ w_out

    h = rms_norm(x, norm_ffn)
    g, u = jnp.split(h @ w_ffn_in, 2, axis=-1)
    x = x + (jax.nn.silu(g) * u) @ w_ffn_out

    h = rms_norm(x, norm_ple)
    x = x + jax.nn.sigmoid(h @ w_ple_gate) * (p_i.astype(x.dtype) @ w_ple)
    return x


def setup_inputs(seed: int = 0) -> dict:
    key = jax.random.key(seed)
    ks = jax.random.split(key, 20)
    f32 = jnp.float32

    def nrm(k, shape, scale):
        return jax.random.normal(k, shape, f32) * scale

    def gain(k, shape):
        return 1.0 + 0.01 * jax.random.normal(k, shape, f32)

    L = DEPTH
    return {
        "x": nrm(ks[0], (BATCH, SEQ, D_MODEL), 1.0),
        "p": nrm(ks[1], (DEPTH, BATCH, SEQ, PLE_DIM), 1.0),
        "w_in": nrm(ks[2], (L, D_MODEL, D_IN), D_MODEL ** -0.5),
        "w_gk2": nrm(ks[3], (L, GLA_GATE_RANK, GLA_KEY_DIM), GLA_GATE_RANK ** -0.5),
        "b_gk": nrm(ks[4], (L, GLA_KEY_DIM), 0.1),
        "sinks": nrm(ks[5], (L, ATTN_HEADS), 1.0),
        "rel_table": nrm(ks[6], (NUM_BUCKETS, ATTN_HEADS), 0.5),
        "w_proj_attn": nrm(ks[7], (L, ATTN_HEADS * HEAD_DIM, D_MODEL), (ATTN_HEADS * HEAD_DIM) ** -0.5),
        "w_proj_gla": nrm(ks[8], (L, GLA_VAL_DIM, D_MODEL), GLA_VAL_DIM ** -0.5),
        "gla_norm": gain(ks[9], (L, GLA_HEAD_V)),
        "w_out": nrm(ks[10], (L, D_MODEL, D_MODEL), D_MODEL ** -0.5),
        "norm_mix": gain(ks[11], (L, D_MODEL)),
        "norm_ffn": gain(ks[12], (L, D_MODEL)),
        "w_ffn_in": nrm(ks[13], (L, D_MODEL, 2 * D_FF), D_MODEL ** -0.5),
        "w_ffn_out": nrm(ks[14], (L, D_FF, D_MODEL), D_FF ** -0.5),
        "norm_ple": gain(ks[15], (L, D_MODEL)),
        "w_ple_gate": nrm(ks[16], (L, D_MODEL, D_MODEL), D_MODEL ** -0.5),
        "w_ple": nrm(ks[17], (L, PLE_DIM, D_MODEL), PLE_DIM ** -0.5),
        "norm_final": gain(ks[18], (D_MODEL,)),
    }


def reference(x, p, w_in, w_gk2, b_gk, sinks, rel_table, w_proj_attn, w_proj_gla, gla_norm,
              w_out, norm_mix, norm_ffn, w_ffn_in, w_ffn_out, norm_ple, w_ple_gate, w_ple,
              norm_final):
    for i in range(DEPTH):
        x = hybrid_layer(x, p[i], w_in[i], w_gk2[i], b_gk[i], sinks[i], rel_table,
                         w_proj_attn[i], w_proj_gla[i], gla_norm[i], w_out[i], norm_mix[i],
                         norm_ffn[i], w_ffn_in[i], w_ffn_out[i], norm_ple[i], w_ple_gate[i],
                         w_ple[i])
    return rms_norm(x, norm_final)
```

```python
import math
import numpy as np
import concourse.bass as bass
import concourse.mybir as mybir
from concourse.bass_utils import run_bass_kernel_spmd

F32 = mybir.dt.float32
BF16 = mybir.dt.bfloat16
AF = mybir.ActivationFunctionType
ALU = mybir.AluOpType

D = 1024
D_IN = 5904
D_FF = 2816
EPS = 1e-6
NEG = -30000.0
C_QA, C_KA, C_VA, C_QG, C_KG, C_VG, C_GK, C_OG, C_GA, C_GG = 0, 512, 640, 768, 1280, 1792, 2816, 2832, 3856, 4880


class Buf:
    __slots__ = ("name", "last_w", "readers")

    def __init__(self, name):
        self.name = name
        self.last_w = None
        self.readers = []


class Op:
    __slots__ = ("eng", "fns", "deps", "sem", "val", "signal", "is_dma")

    def __init__(self, eng, fns):
        self.eng = eng
        self.fns = fns
        self.deps = []
        self.sem = None
        self.val = 0
        self.signal = False
        self.is_dma = False


class Prog:
    ENGS = ("pe", "act", "dve", "pool", "sp")

    def __init__(self, nc):
        self.nc = nc
        self.ops = {e: [] for e in self.ENGS}
        self.esem = {e: nc.alloc_semaphore("es_" + e) for e in self.ENGS}
        self.dma_cnt = {}
        self.nsem = 0
        self.final = []

    def buf(self, name):
        return Buf(name)

    def new_sem(self):
        self.nsem += 1
        return self.nc.alloc_semaphore(f"ds_{self.nsem}")

    def _track(self, op, reads, writes):
        deps = []
        for b in reads:
            if b.last_w is not None:
                deps.append(b.last_w)
        for b in writes:
            if b.last_w is not None:
                deps.append(b.last_w)
            deps.extend(b.readers)
        for b in reads:
            b.readers.append(op)
        for b in writes:
            b.last_w = op
            b.readers = []
        seen = set()
        for d in deps:
            if d is op or id(d) in seen:
                continue
            seen.add(id(d))
            if d.eng == "pe" and op.eng == "pe" and not d.is_dma and not op.is_dma:
                continue
            op.deps.append(d)
            d.signal = True

    def op(self, eng, fns, reads=(), writes=()):
        if callable(fns):
            fns = [fns]
        o = Op(eng, list(fns))
        self._track(o, reads, writes)
        self.ops[eng].append(o)
        return o

    def dma(self, eng, fns, sem, reads=(), writes=()):
        if callable(fns):
            fns = [fns]
        o = Op(eng, list(fns))
        o.is_dma = True
        o.sem = sem
        c = self.dma_cnt.get(id(sem), 0) + 16 * len(o.fns)
        self.dma_cnt[id(sem)] = c
        o.val = c
        self._track(o, reads, writes)
        self.ops[eng].append(o)
        return o

    def emit(self):
        nc = self.nc
        for e in self.ENGS:
            c = 0
            for o in self.ops[e]:
                if o.is_dma:
                    continue
                if o.signal:
                    c += 1
                    o.sem = self.esem[e]
                    o.val = c
        handles = {"pe": "tensor", "act": "scalar", "dve": "vector", "pool": "gpsimd", "sp": "sync"}
        final = self.final
        with nc.Block() as block:
            for e in self.ENGS:
                ops = self.ops[e]

                def body(eng, ops=ops, e=e):
                    waited = {}
                    for o in ops:
                        for d in o.deps:
                            k = id(d.sem)
                            if waited.get(k, 0) >= d.val:
                                continue
                            eng.wait_ge(d.sem, d.val)
                            waited[k] = d.val
                        n = len(o.fns)
                        for i, f in enumerate(o.fns):
                            ins = f(eng)
                            if o.is_dma:
                                ins.then_inc(o.sem, 16)
                            elif o.signal and i == n - 1:
                                ins.then_inc(o.sem, 1)
                    if e == "sp":
                        for d in final:
                            eng.wait_ge(d.sem, d.val)

                getattr(block, handles[e])(body)


def build_nc(NT, NTP, dbg=None):
    assert NT % 4 == 0 and NTP % 4 == 0
    nc = bass.Bass("TRN2", target_bir_lowering=False)
    P = Prog(nc)
    T, TP = NT * 128, NTP * 128

    def din(name, shape, dt=F32):
        return nc.dram_tensor(name, list(shape), dt, kind="ExternalInput").ap()

    x_d = din("x", [T, D])
    xp_d = din("xp", [TP, D])
    p_d = din("p", [T, 256])
    w_in_d = din("w_in", [D, D_IN])
    wpa_d = din("w_proj_attn", [512, D])
    wpg_d = din("w_proj_gla", [D, D])
    wout_d = din("w_out", [D, D])
    wfi_d = din("w_ffn_in", [D, 2 * D_FF])
    wfo_d = din("w_ffn_out", [D_FF, D])
    wpgate_d = din("w_ple_gate", [D, D])
    wple_d = din("w_ple", [256, D])
    wgk_d = din("wgk_aug", [17, 512])
    gcols_d = din("gcols", [128, 24])
    gn_d = din("gn_col", [128, 2])
    gfin_d = din("norm_final", [D])
    sinkl_d = din("sink_l", [128, 4])
    biasT_d = din("biasT", [2, 128, 1024])
    maskT_d = din("maskT", [2, 128, 128])
    mask0_d = din("mask0", [128, 128])
    cst_d = din("cst", [128, 386])
    out_d = nc.dram_tensor("out", [T, D], F32, kind="ExternalOutput").ap()

    def dscr(name, shape):
        return nc.dram_tensor(name, list(shape), BF16, kind="Internal").ap()

    wi_s = dscr("wi_s", [D, D_IN])
    wpa_s = dscr("wpa_s", [512, D])
    wpg_s = dscr("wpg_s", [D, D])
    wout_s = dscr("wout_s", [D, D])
    wfi_s = dscr("wfi_s", [D, 2 * D_FF])
    wfo_s = dscr("wfo_s", [D_FF, D])
    wpgate_s = dscr("wpgate_s", [D, D])
    wple_s = dscr("wple_s", [256, D])

    def sb(name, shape, dt=F32):
        return nc.alloc_sbuf_tensor("s_" + name, list(shape), dt)

    xres = [sb(f"xres{t}", [128, D]) for t in range(4)]
    xresB = [P.buf(f"xres{t}") for t in range(4)]
    xn = [sb(f"xn{i}", [128, D], BF16) for i in range(2)]
    xnB = [P.buf(f"xn{i}") for i in range(2)]
    stat = [sb(f"stat{t}", [128, 4]) for t in range(4)]
    statB = [P.buf(f"stat{t}") for t in range(4)]
    hT = sb("hT", [128, 8, 512], BF16)
    hTB = P.buf("hT")
    slab = sb("slab", [128, 40, 512], BF16)
    slabB = [P.buf(f"slab{i}") for i in range(40)]
    S_ACT, S_GSIL, S_OGN, S_MIX, S_QA, S_QG, S_KG, S_OA = 0, 0, 8, 16, 24, 28, 32, 36
    ka_pad = [sb(f"ka_pad{g}", [128, 640], BF16) for g in range(2)]
    kaB = [P.buf(f"ka_s{s}") for s in range(5)]
    va_pad = sb("va_pad", [128, 5, 2, 128], BF16)
    vaB = [P.buf(f"va_s{s}") for s in range(5)]
    kgTM = [sb(f"kgTM{t}", [128, 512], BF16) for t in range(4)]
    kgTMB = [P.buf(f"kgTM{t}") for t in range(4)]
    vgTM = [sb(f"vgTM{t}", [128, 1024], BF16) for t in range(4)]
    vgTMB = [P.buf(f"vgTM{t}") for t in range(4)]
    gkT = sb("gkT", [32, 512])
    gkTB = P.buf("gkT")
    sp_t2 = [sb(f"sp_t{i}", [128, 512]) for i in range(2)]; spB2 = [P.buf(f"sp{i}") for i in range(2)]
    ED2 = [sb(f"ED{i}", [128, 512]) for i in range(2)]; EDB2 = [P.buf(f"ED{i}") for i in range(2)]
    Epos2 = [sb(f"Epos{i}", [128, 512]) for i in range(2)]; EposB2 = [P.buf(f"Epos{i}") for i in range(2)]
    Eneg2 = [sb(f"Eneg{i}", [128, 512]) for i in range(2)]; EnegB2 = [P.buf(f"Eneg{i}") for i in range(2)]
    el2 = [sb(f"el{i}", [128, 8]) for i in range(2)]; elB2 = [P.buf(f"el{i}") for i in range(2)]
    qt2 = [sb(f"qt{i}", [128, 4, 128], BF16) for i in range(2)]; qtB2 = [P.buf(f"qt{i}") for i in range(2)]
    kt2 = [sb(f"kt{i}", [128, 4, 128], BF16) for i in range(2)]; ktB2 = [P.buf(f"kt{i}") for i in range(2)]
    kd2 = [[sb(f"kd{i}_{c}", [128, 512], BF16) for c in range(2)] for i in range(2)]
    kdB2 = [[P.buf(f"kd{i}_{c}") for c in range(2)] for i in range(2)]
    ATs = sb("ATs", [128, 4, 128], BF16); ATsB = P.buf("ATs")
    sq = sb("sq", [128, 8, 128], BF16); sqB = P.buf("sq")
    junk_ap = sq[:].rearrange("p a b -> p (a b)")
    junkB = sqB
    rstd_g = sb("rstd_g", [128, 4, 128]); rstdgB = P.buf("rstd_g")
    otmp = sb("otmp", [128, 8, 128]); otmpB = P.buf("otmp")
    S_st = sb("S_st", [128, 1024]); SB_ = P.buf("S_st")
    Sbf = [sb(f"Sbf{i}", [128, 1024], BF16) for i in range(2)]
    SbfB = [P.buf(f"Sbf{i}") for i in range(2)]
    sT = [sb(f"sT{i}", [128, 512]) for i in range(2)]
    sTB = [P.buf(f"sT{i}") for i in range(2)]
    esT = sb("esT", [128, 4, 512], BF16)
    esTB = [P.buf(f"esT{i}") for i in range(4)]
    biasT8 = sb("biasT8", [128, 2, 1024]); biasB = P.buf("biasT8")
    mask0 = sb("mask0", [128, 128]); mask0B = P.buf("mask0")
    maskT = sb("maskT", [128, 2, 128]); maskTB = P.buf("maskT")
    esink = sb("esink", [128, 4]); esinkB = P.buf("esink")
    ta = sb("ta", [128, 512], BF16); taB = P.buf("ta")
    tg = sb("tg", [128, 512], BF16); tgB = P.buf("tg")
    m1 = sb("m1", [128, 512]); m1B = P.buf("m1")
    m2 = sb("m2", [128, 512]); m2B = P.buf("m2")
    lnv, lnvB = m1, m1B
    dtmp, dtmpB = m2, m2B
    ftt = [sb(f"ftt{i}", [128, 512]) for i in range(2)]
    fttB = [P.buf(f"ftt{i}") for i in range(2)]
    fa = [sb(f"fa{i}", [128, 512]) for i in range(2)]
    faB = [P.buf(f"fa{i}") for i in range(2)]
    p_f = [sb(f"p_f{i}", [128, 256]) for i in range(2)]
    p_fB = [P.buf(f"p_f{i}") for i in range(2)]
    p_bf = [sb(f"p_bf{i}", [128, 256], BF16) for i in range(2)]
    p_bfB = [P.buf(f"p_bf{i}") for i in range(2)]
    pT = sb("pT", [128, 2, 512], BF16); pTB = P.buf("pT")
    tgtB = esTB
    gfin = sb("gfin", [128, D]); gfinB = P.buf("gfin")
    gcols = sb("gcols", [128, 24]); gcolsB = P.buf("gcols")
    gnh = sb("gnh", [128, 2]); gnhB = P.buf("gnh")
    cst = sb("cst", [128, 386]); cstB = P.buf("cst")
    wgk = sb("wgk", [32, 512]); wgkB = P.buf("wgk")
    ident = sb("ident", [128, 128], BF16); identB = P.buf("ident")
    ones_bf = sb("ones_bf", [128, 128], BF16); onesB = P.buf("ones")
    ones_pad = sb("ones_pad", [128, 2, 128], BF16); onespB = P.buf("ones_pad")
    mhalf = sb("mhalf", [128, 1]); mhalfB = P.buf("mhalf")
    NSLOT = 6
    ring = [(sb(f"ws{i}", [128, 2048], BF16), P.buf(f"ws{i}"), P.new_sem()) for i in range(NSLOT)]
    ring_i = [0]
    NBANK = 8
    banks = [(nc.alloc_psum_tensor(f"pb{i}", [128, 512], F32), P.buf(f"pb{i}")) for i in range(NBANK)]
    bank_i = [0]
    marks = []

    def mark(name):
        marks.append((name, sum(len(o.fns) for o in P.ops['pe'])))

    pool_i = {}

    def nb(pool=None):
        if pool is None:
            r = banks[bank_i[0] % NBANK]
            bank_i[0] += 1
            return r
        i = pool_i.get(pool, 0)
        pool_i[pool] = i + 1
        return banks[pool[i % len(pool)]]

    PGLA2, PATT, PPROJ = ((0, 1, 2), (3, 4, 5)), (6, 7), (6, 7)

    U2m = cst[:, 0:128]
    SU2m = cst[:, 128:256]
    ind2m = cst[:, 256:258]
    maskA = cst[:, 258:386]

    sem_c = [P.new_sem() for _ in range(12)]
    P.dma("sp", lambda e: e.dma_start(out=cst[:], in_=cst_d), sem_c[0], writes=[cstB])
    P.dma("sp", lambda e: e.dma_start(out=gcols[:], in_=gcols_d), sem_c[1], writes=[gcolsB])
    P.dma("sp", lambda e: e.dma_start(out=gnh[:], in_=gn_d), sem_c[2], writes=[gnhB])
    P.dma("sp", lambda e: e.dma_start(out=gfin[:], in_=gfin_d.partition_broadcast(128)), sem_c[3], writes=[gfinB])
    P.dma("sp", lambda e: e.dma_start(out=esink[:], in_=sinkl_d), sem_c[4], writes=[esinkB])
    P.dma("sp", lambda e: e.dma_start(out=wgk[0:17, :], in_=wgk_d), sem_c[5], writes=[wgkB])
    P.dma("sp", lambda e: e.dma_start(out=mask0[:], in_=mask0_d), sem_c[6], writes=[mask0B])
    P.dma("sp", lambda e: e.dma_start(out=maskT[:], in_=maskT_d.rearrange("b k q -> k b q")), sem_c[7],
          writes=[maskTB])
    P.dma("sp", lambda e: e.dma_start(out=biasT8[:], in_=biasT_d.rearrange("b k c -> k b c")), sem_c[8],
          writes=[biasB])

    def flat(ap, n):
        return ap.rearrange("a b -> (a b)").rearrange("(n c) -> n c", c=2048)

    castB = {}

    def cast(name, src, dst, nrows_flat, pieces=1):
        b = P.buf("scr_" + name)
        castB[name] = b
        fs, fd = flat(src, nrows_flat), flat(dst, nrows_flat)
        per = nrows_flat // pieces
        fns = []
        for i in range(pieces):
            r0, r1 = i * per, (nrows_flat if i == pieces - 1 else (i + 1) * per)
            fns.append(lambda e, r0=r0, r1=r1: e.dma_start(out=fd[r0:r1, :], in_=fs[r0:r1, :]))
        P.dma("pool", fns, P.new_sem(), writes=[b])

    def cast_cols(name, c0, c1):
        b = P.buf("scr_" + name)
        castB[name] = [b]
        P.dma("pool", lambda e: e.dma_start(out=wi_s[:, c0:c1], in_=w_in_d[:, c0:c1]), P.new_sem(), writes=[b])

    pending_casts = []
    cast_cols("wiA", C_KG, C_OG)
    for nm_, a_, b_ in (("wiB", 0, C_KG), ("wiC", C_OG, C_GG), ("wiD", C_GG, D_IN)):
        castB[nm_] = [P.buf("scr_" + nm_)]
        pending_casts.append(lambda nm_=nm_, a_=a_, b_=b_: P.dma(
            "pool", lambda e: e.dma_start(out=wi_s[:, a_:b_], in_=w_in_d[:, a_:b_]), P.new_sem(),
            writes=castB[nm_]))

    def wi_name(c0):
        return "wiB" if c0 < C_KG else ("wiA" if c0 < C_OG else ("wiC" if c0 < C_GG else "wiD"))
    P.op("pool", lambda e: e.memset(ident[:], 1.0), writes=[identB])
    P.op("pool", lambda e: e.affine_select(out=ident[:], in_=ident[:], pattern=[[-1, 128]], compare_op=ALU.is_equal,
                                           fill=0.0, base=0, channel_multiplier=1), reads=[identB], writes=[identB])
    P.op("pool", lambda e: e.memset(ones_bf[:], 1.0), writes=[onesB])
    P.op("pool", [lambda e: e.memset(ones_pad[:], 0.0)], writes=[onespB])
    P.op("pool", [lambda e: e.memset(ones_pad[:, 0, 0:64], 1.0), lambda e: e.memset(ones_pad[:, 1, 64:128], 1.0)],
         reads=[onespB], writes=[onespB])
    P.op("pool", lambda e: e.memset(mhalf[:], -0.5), writes=[mhalfB])
    P.op("pool", [lambda e: e.memset(ka_pad[0][:], 0.0), lambda e: e.memset(ka_pad[1][:], 0.0),
                  lambda e: e.memset(va_pad[:], 0.0)], writes=kaB + vaB)
    P.op("pool", [lambda e, i=i, c=c: e.memset(kd2[i][c][:], 0.0) for i in range(2) for c in range(2)],
         writes=kdB2[0] + kdB2[1])
    P.op("pool", lambda e: e.memset(gkT[:], 1.0), writes=[gkTB])
    P.op("pool", lambda e: e.memset(S_st[:], 0.0), writes=[SB_])
    P.op("pool", lambda e: e.memset(wgk[:], 0.0), writes=[wgkB]) if False else None
    P.op("act", lambda e: e.activation(out=esink[:], in_=esink[:], func=AF.Exp), reads=[esinkB], writes=[esinkB])
    for kb in range(2):
        P.op("dve", lambda e, kb=kb: e.tensor_tensor(
            out=biasT8[:, kb, :].rearrange("p (h q) -> p h q", h=8),
            in0=biasT8[:, kb, :].rearrange("p (h q) -> p h q", h=8),
            in1=maskT[:, kb, :].unsqueeze(1).broadcast_to([128, 8, 128]), op=ALU.add),
            reads=[biasB, maskTB], writes=[biasB])
    P.op("dve", lambda e: e.tensor_scalar(out=biasT8[:], in0=biasT8[:], scalar1=8.0, scalar2=None, op0=ALU.mult),
         reads=[biasB], writes=[biasB])
    P.op("dve", lambda e: e.tensor_scalar(out=mask0[:], in0=mask0[:], scalar1=8.0, scalar2=None, op0=ALU.mult),
         reads=[mask0B], writes=[mask0B])
    def defer_cast(name, src, dst, nrows_flat, pieces=1):
        castB[name] = [P.buf(f"scr_{name}{i}") for i in range(pieces)]
        fs, fd = flat(src, nrows_flat), flat(dst, nrows_flat)
        per = nrows_flat // pieces
        for i in range(pieces):
            r0, r1 = i * per, (nrows_flat if i == pieces - 1 else (i + 1) * per)
            pending_casts.append(lambda name=name, i=i, r0=r0, r1=r1, fs=fs, fd=fd: P.dma(
                "pool", lambda e: e.dma_start(out=fd[r0:r1, :], in_=fs[r0:r1, :]), P.new_sem(),
                writes=[castB[name][i]]))

    castS = {}
    defer_cast("wpa", wpa_d, wpa_s, 512 * D // 2048)
    defer_cast("wpg", wpg_d, wpg_s, D * D // 2048)
    defer_cast("wout", wout_d, wout_s, D * D // 2048)
    defer_cast("wfi", wfi_d, wfi_s, D * 2 * D_FF // 2048, pieces=3)
    defer_cast("wfo", wfo_d, wfo_s, D_FF * D // 2048, pieces=2)
    defer_cast("wpgate", wpgate_d, wpgate_s, D * D // 2048)
    defer_cast("wple", wple_d, wple_s, 256 * D // 2048)

    def wload(srcs, name, nk, ncols):
        t, b, s = ring[ring_i[0] % NSLOT]
        ring_i[0] += 1
        fns = [(lambda e, dv=dv, sa=sa, t=t: e.dma_start(out=dv(t), in_=sa)) for dv, sa in srcs]
        P.dma("sp", fns, s, reads=castB[name], writes=[b])
        return t, b

    def wload_std(scr, name, r0, nk, c0, ncols):
        src = scr[r0:r0 + nk * 128, c0:c0 + ncols].rearrange("(k p) c -> p k c", p=128)
        t, b = wload([(lambda t: t[:, 0:nk * ncols].rearrange("p (k c) -> p k c", k=nk), src)], name, nk, ncols)
        return t[:, 0:nk * ncols].rearrange("p (k c) -> p k c", k=nk), b

    evac_i = [0]

    def evac_copy(out_ap, in_ap, reads, writes, eng=None):
        if eng is None:
            eng = "act" if evac_i[0] % 2 == 0 else "dve"
            evac_i[0] += 1
        if eng == "act":
            P.op("act", lambda e: e.activation(out=out_ap, in_=in_ap, func=AF.Copy), reads=reads, writes=writes)
        else:
            P.op("dve", lambda e: e.tensor_copy(out=out_ap, in_=in_ap), reads=reads, writes=writes)

    def fm_mm(bank, w, wB, ccol, act_view, actB, nk):
        bt, bb = bank
        fns = [(lambda e, kc=kc: e.matmul(bt[:, :], lhsT=w[:, kc, ccol:ccol + 128], rhs=act_view[:, kc, :],
                                          start=(kc == 0), stop=(kc == nk - 1))) for kc in range(nk)]
        P.op("pe", fns, reads=[wB] + list(actB), writes=[bb])

    def norm_gen(gidx, load_fn, dst=None, pool=None):
        hTc, hTBc = dst if dst is not None else (hT, [hTB])
        for t in range(4):
            if load_fn is not None:
                load_fn(t)
            P.op("act", lambda e, t=t: e.activation(out=junk_ap, in_=xres[t][:], func=AF.Square,
                                                   accum_out=stat[t][:, 0:1]),
                 reads=[xresB[t]], writes=[junkB, statB[t]])
            P.op("pool", lambda e, t=t: e.tensor_scalar(out=stat[t][:, 1:2], in0=stat[t][:, 0:1], scalar1=1.0 / D,
                                                       scalar2=EPS, op0=ALU.mult, op1=ALU.add),
                 reads=[statB[t]], writes=[statB[t]])
            P.op("pool", lambda e, t=t: e.tensor_tensor(out=stat[t][:, 2:3], in0=stat[t][:, 1:2], in1=mhalf[:, 0:1],
                                                       op=ALU.pow),
                 reads=[statB[t], mhalfB], writes=[statB[t]])
            i = t % 2
            P.op("dve", lambda e, t=t, i=i: e.tensor_scalar(out=xn[i][:], in0=xres[t][:], scalar1=stat[t][:, 2:3],
                                                           scalar2=None, op0=ALU.mult),
                 reads=[xresB[t], statB[t]], writes=[xnB[i]])
            bt, bb = nb(pool)
            btb = bt[:].bitcast(BF16)
            P.op("pe", [(lambda e, kc=kc, i=i, btb=btb: e.transpose(out=btb[:, kc * 128:(kc + 1) * 128],
                                                                    in_=xn[i][:, kc * 128:(kc + 1) * 128],
                                                                    identity=ident[:])) for kc in range(8)],
                 reads=[xnB[i], identB], writes=[bb])
            P.op("dve", lambda e, t=t, btb=btb: e.tensor_tensor(
                out=hTc[:, :, t * 128:(t + 1) * 128], in0=btb.rearrange("p (k c) -> p k c", k=8),
                in1=gcols[:, gidx * 8:(gidx + 1) * 8].unsqueeze(2).broadcast_to([128, 8, 128]), op=ALU.mult),
                reads=[bb, gcolsB], writes=hTBc)
            yield

    def norm_to_hT(gidx, load_fn, dst=None):
        for _ in norm_gen(gidx, load_fn, dst):
            pass


    ystage = [sb(f"ystage{i}", [128, D]) for i in range(2)]
    ystageB = [P.buf(f"ystage{i}") for i in range(2)]
    xsem = [P.new_sem() for _ in range(4)]
    osem = [P.new_sem() for _ in range(4)]

    def load_x(src, row0):
        def f(t):
            P.dma("sp", lambda e: e.dma_start(out=xres[t][:], in_=src[row0 + t * 128: row0 + (t + 1) * 128, :]),
                  xsem[t], writes=[xresB[t]])
        return f

    def proj_tm(c0, ncols, dst_fn, dstB_fn):
        w, wB = wload_std(wi_s, wi_name(c0), 0, 8, c0, ncols)
        for t in range(4):
            bt, bb = nb()
            P.op("pe", [(lambda e, kc=kc, t=t, bt=bt: e.matmul(bt[:, 0:ncols], lhsT=hT[:, kc, t * 128:(t + 1) * 128],
                                                               rhs=w[:, kc, :], start=(kc == 0), stop=(kc == 7)))
                        for kc in range(8)], reads=[wB, hTB], writes=[bb])
            evac_copy(dst_fn(t), bt[:, 0:ncols], [bb], [dstB_fn(t)])

    def proj_kv_a(src=None):
        hTc, hTBc = src if src is not None else (hT, [hTB])
        w, wB = wload_std(wi_s, "wiB", 0, 8, C_KA, 256)
        bank = nb()
        fm_mm(bank, w, wB, 0, hTc, hTBc, 8)
        bt, bb = bank
        P.op("act", [lambda e: e.activation(out=ka_pad[0][0:64, 128:640], in_=bt[0:64, :], func=AF.Copy),
                     lambda e: e.activation(out=ka_pad[1][64:128, 128:640], in_=bt[64:128, :], func=AF.Copy)],
             reads=[bb], writes=kaB[1:5])
        for t in range(4):
            bt2, bb2 = nb()
            P.op("pe", [(lambda e, kc=kc, t=t, bt2=bt2: e.matmul(bt2[:, 0:128], lhsT=hTc[:, kc, t * 128:(t + 1) * 128],
                                                                 rhs=w[:, kc, 128:256], start=(kc == 0),
                                                                 stop=(kc == 7))) for kc in range(8)],
                 reads=[wB] + hTBc, writes=[bb2])
            P.op("act", [lambda e, t=t, bt2=bt2: e.activation(out=va_pad[:, t + 1, 0, 0:64], in_=bt2[:, 0:64],
                                                              func=AF.Copy),
                         lambda e, t=t, bt2=bt2: e.activation(out=va_pad[:, t + 1, 1, 64:128], in_=bt2[:, 64:128],
                                                              func=AF.Copy)],
                 reads=[bb2], writes=[vaB[t + 1]])

    def carry_kv():
        P.op("pool", [lambda e: e.tensor_copy(out=ka_pad[0][:, 0:128], in_=ka_pad[0][:, 512:640]),
                      lambda e: e.tensor_copy(out=ka_pad[1][:, 0:128], in_=ka_pad[1][:, 512:640]),
                      lambda e: e.tensor_copy(out=va_pad[:, 0], in_=va_pad[:, 4])],
             reads=[kaB[4], vaB[4]], writes=[kaB[0], vaB[0]])

    def proj_gla_tm():
        for i in range(2):
            proj_tm(C_KG + i * 256, 256, lambda t, i=i: kgTM[t][:, i * 256:(i + 1) * 256], lambda t: kgTMB[t])
        for i in range(4):
            proj_tm(C_VG + i * 256, 256, lambda t, i=i: vgTM[t][:, i * 256:(i + 1) * 256], lambda t: vgTMB[t])

    def proj_gk(src=None):
        hTc, hTBc = src if src is not None else (hT, [hTB])
        w, wB = wload_std(wi_s, "wiA", 0, 8, C_GK, 16)
        bt, bb = nb()
        P.op("pe", [(lambda e, kc=kc: e.matmul(bt[0:16, :], lhsT=w[:, kc, 0:16], rhs=hTc[:, kc, :], start=(kc == 0),
                                               stop=(kc == 7))) for kc in range(8)], reads=[wB] + hTBc, writes=[bb])
        P.op("act", lambda e: e.activation(out=gkT[0:16, :], in_=bt[0:16, :], func=AF.Copy), reads=[bb],
             writes=[gkTB])

    def prefix_proj_gen(src):
        hTc, hTBc = src
        proj_gk(src)
        yield
        groups = []
        for i in range(2):
            groups.append((wload_std(wi_s, "wiA", 0, 8, C_KG + i * 256, 256), 0, i))
        for i in range(4):
            groups.append((wload_std(wi_s, "wiA", 0, 8, C_VG + i * 256, 256), 1, i))
        for t in range(4):
            for (w, wB), kind, i in groups:
                bt, bb = nb(PPROJ)
                P.op("pe", [(lambda e, kc=kc, t=t, bt=bt, w=w: e.matmul(bt[:, 0:256],
                                                                        lhsT=hTc[:, kc, t * 128:(t + 1) * 128],
                                                                        rhs=w[:, kc, :], start=(kc == 0),
                                                                        stop=(kc == 7))) for kc in range(8)],
                     reads=[wB] + hTBc, writes=[bb])
                if kind == 0:
                    evac_copy(kgTM[t][:, i * 256:(i + 1) * 256], bt[:, 0:256], [bb], [kgTMB[t]])
                else:
                    evac_copy(vgTM[t][:, i * 256:(i + 1) * 256], bt[:, 0:256], [bb], [vgTMB[t]])
            yield

    def og_gen():
        for i in range(4):
            w, wB = wload_std(wi_s, "wiC", 0, 8, C_OG + i * 256, 256)
            for cc in range(2):
                idx = i * 2 + cc
                bank = nb()
                fm_mm(bank, w, wB, cc * 128, hT, [hTB], 8)
                bt, bb = bank
                f = idx % 2
                P.op("act", lambda e, bt=bt, f=f: e.activation(out=ftt[f][:], in_=bt[:, :], func=AF.Exp, scale=-1.0),
                     reads=[bb], writes=[fttB[f]])
                P.op("act", lambda e, f=f: e.activation(out=ftt[f][:], in_=ftt[f][:], func=AF.Ln, bias=1.0),
                     reads=[fttB[f]], writes=[fttB[f]])
                P.op("act", lambda e, f=f: e.activation(out=ftt[f][:], in_=ftt[f][:], func=AF.Exp, scale=-1.0),
                     reads=[fttB[f]], writes=[fttB[f]])
                P.op("dve", lambda e, bt=bt, f=f, idx=idx: e.scalar_tensor_tensor(
                    out=slab[:, S_GSIL + idx, :], in0=bt[:, :], scalar=gnh[:, idx % 2:idx % 2 + 1], in1=ftt[f][:],
                    op0=ALU.mult, op1=ALU.mult), reads=[bb, gnhB, fttB[f]], writes=[slabB[S_GSIL + idx]])
                yield

    state_par = [0]

    def gla_tile(t, full):
        tok = slice(t * 128, (t + 1) * 128)
        par = t % 2
        PGLA = PGLA2[par]
        sp_t, spB, ED, EDB = sp_t2[par], spB2[par], ED2[par], EDB2[par]
        Epos, EposB, Eneg, EnegB = Epos2[par], EposB2[par], Eneg2[par], EnegB2[par]
        qt, qtB, kt, ktB, kd, kdB = qt2[par], qtB2[par], kt2[par], ktB2[par], kd2[par], kdB2[par]
        el, elB = el2[par], elB2[par]
        zt, zb = nb(PGLA)
        P.op("pe", lambda e: e.matmul(zt[:, :], lhsT=gkT[0:17, tok], rhs=wgk[0:17, :], start=True, stop=True),
             reads=[gkTB, wgkB], writes=[zb])
        P.op("act", lambda e: e.activation(out=sp_t[:], in_=zt[:, :], func=AF.Exp, scale=-1.0), reads=[zb],
             writes=[spB])
        P.op("act", lambda e: e.activation(out=sp_t[:], in_=sp_t[:], func=AF.Ln, bias=1.0), reads=[spB],
             writes=[spB])
        yield
        dt_, db = nb(PGLA)
        P.op("pe", lambda e: e.matmul(dt_[:, :], lhsT=SU2m, rhs=sp_t[:], start=True, stop=True),
             reads=[cstB, spB], writes=[db])
        P.op("act", lambda e: e.activation(out=ED[:], in_=dt_[:, :], func=AF.Exp), reads=[db], writes=[EDB])
        if not full:
            et, eb = nb(PGLA)
            P.op("pe", [(lambda e, h=h: e.matmul(et[:, h * 2:(h + 1) * 2], lhsT=sp_t[:, h * 128:(h + 1) * 128],
                                                 rhs=ind2m, start=True, stop=True)) for h in range(4)],
                 reads=[cstB, spB], writes=[eb])
            P.op("act", lambda e: e.activation(out=el[:], in_=et[:, 0:8], func=AF.Exp), reads=[eb], writes=[elB])
        if full:
            gt, gb = nb(PGLA)
            P.op("pe", [(lambda e, h=h: e.matmul(gt[:, h * 128:(h + 1) * 128], lhsT=sp_t[:, h * 128:(h + 1) * 128],
                                                 rhs=U2m, start=True, stop=True)) for h in range(4)],
                 reads=[cstB, spB], writes=[gb])
            P.op("act", lambda e: e.activation(out=Epos[:], in_=gt[:, :], func=AF.Exp), reads=[gb], writes=[EposB])
            P.op("act", lambda e: e.activation(out=Eneg[:], in_=gt[:, :], func=AF.Exp, scale=-1.0), reads=[gb],
                 writes=[EnegB])
            yield
            P.op("dve", lambda e: e.scalar_tensor_tensor(
                out=qt[:], in0=slab[:, S_QG:S_QG + 4, tok], scalar=128.0 ** -0.5,
                in1=Epos[:].rearrange("p (h c) -> p h c", h=4), op0=ALU.mult, op1=ALU.mult),
                reads=slabB[S_QG:S_QG + 4] + [EposB], writes=[qtB])
            P.op("dve", lambda e: e.tensor_tensor(
                out=kt[:], in0=slab[:, S_KG:S_KG + 4, tok], in1=Eneg[:].rearrange("p (h c) -> p h c", h=4),
                op=ALU.mult), reads=slabB[S_KG:S_KG + 4] + [EnegB], writes=[ktB])
        P.op("pool", lambda e: e.tensor_tensor(out=kd[0][0:64, :], in0=kgTM[t][0:64, :], in1=ED[0:64, :], op=ALU.mult),
             reads=[kgTMB[t], EDB], writes=[kdB[0]])
        P.op("pool", lambda e: e.tensor_tensor(out=kd[1][64:128, :], in0=kgTM[t][64:128, :], in1=ED[64:128, :],
                                               op=ALU.mult), reads=[kgTMB[t], EDB], writes=[kdB[1]])
        yield
        if full:
            at, ab = nb(PGLA)
            P.op("pe", [(lambda e, h=h: e.matmul(at[:, h * 128:(h + 1) * 128], lhsT=kt[:, h, :], rhs=qt[:, h, :],
                                                 start=True, stop=True)) for h in range(4)],
                 reads=[ktB, qtB], writes=[ab])
            P.op("dve", lambda e: e.tensor_tensor(out=ATs[:], in0=at[:, :].rearrange("p (h c) -> p h c", h=4),
                                                  in1=maskA.unsqueeze(1).broadcast_to([128, 4, 128]), op=ALU.mult),
                 reads=[ab, cstB], writes=[ATsB])
        sps = [None, None]

        def sps_mm(c):
            pair = []
            for hp in range(2):
                st, sbb = nb(PGLA)
                P.op("pe", [(lambda e, hh=hh, hp=hp, c=c, st=st: e.matmul(
                    st[:, hh * 256:(hh + 1) * 256], lhsT=kd[c][:, (hp * 2 + hh) * 128:(hp * 2 + hh + 1) * 128],
                    rhs=vgTM[t][:, (hp * 2 + hh) * 256:(hp * 2 + hh + 1) * 256], start=True, stop=True))
                    for hh in range(2)], reads=[kdB[c], vgTMB[t]], writes=[sbb])
                pair.append((st, sbb))
            sps[c] = pair

        def upd(c):
            for h in range(4):
                st, sbb = sps[c][h // 2]
                if full:
                    ci = h * 128 + c * 64 + 63
                    sc, scB = Epos[:, ci:ci + 1], EposB
                else:
                    sc, scB = el[:, h * 2 + c:h * 2 + c + 1], elB
                P.op("dve", lambda e, h=h, st=st, c=c, sc=sc: e.scalar_tensor_tensor(
                    out=S_st[:, h * 256:(h + 1) * 256], in0=S_st[:, h * 256:(h + 1) * 256],
                    scalar=sc, in1=st[:, (h % 2) * 256:(h % 2 + 1) * 256],
                    op0=ALU.mult, op1=ALU.add), reads=[SB_, scB, sbb], writes=[SB_])

        pa = state_par[0]
        if not full:
            sps_mm(0)
            upd(0)
            sps_mm(1)
            upd(1)
            yield
            return
        sps_mm(0)
        upd(0)
        P.op("act", lambda e: e.activation(out=Sbf[1 - pa][:], in_=S_st[:], func=AF.Copy), reads=[SB_],
             writes=[SbfB[1 - pa]])
        yield
        sps_mm(1)
        upd(1)
        obanks = [nb(PGLA), nb(PGLA)]
        for idx in range(8):
            h, dvc = idx // 2, idx % 2
            ot, ob = obanks[idx // 4]
            oc = (idx % 4) * 128
            col = h * 256 + dvc * 128
            P.op("pe", [
                lambda e, ot=ot, oc=oc, col=col, h=h: e.matmul(ot[:, oc:oc + 128], lhsT=vgTM[t][:, col:col + 128],
                                                               rhs=ATs[:, h, :], start=True, stop=False),
                lambda e, ot=ot, oc=oc, col=col, h=h: e.matmul(ot[:, oc:oc + 64], lhsT=Sbf[pa][:, col:col + 128],
                                                               rhs=qt[:, h, 0:64], start=False, stop=False),
                lambda e, ot=ot, oc=oc, col=col, h=h: e.matmul(ot[:, oc + 64:oc + 128],
                                                               lhsT=Sbf[1 - pa][:, col:col + 128],
                                                               rhs=qt[:, h, 64:128], start=False, stop=True)],
                reads=[vgTMB[t], ATsB, SbfB[0], SbfB[1], qtB], writes=[ob])
        P.op("act", lambda e: e.activation(out=Sbf[pa][:], in_=S_st[:], func=AF.Copy), reads=[SB_],
             writes=[SbfB[pa]])
        for k in range(2):
            ot, ob = obanks[k]
            P.op("act", lambda e, ot=ot, k=k: e.activation(out=sq[:, k * 4:(k + 1) * 4, :],
                                                           in_=ot[:, :].rearrange("p (i c) -> p i c", i=4),
                                                           func=AF.Square), reads=[ob], writes=[sqB])
        yield
        st_, sb_ = nb(PGLA)
        sqv = sq[:].rearrange("p (h v) c -> p h v c", v=2)
        P.op("pe", [(lambda e, v=v: e.matmul(st_[:, :], lhsT=ones_bf[:], rhs=sqv[:, :, v, :], start=(v == 0),
                                             stop=(v == 1))) for v in range(2)], reads=[sqB, onesB], writes=[sb_])
        P.op("act", lambda e: e.activation(out=lnv[:], in_=st_[:, :], func=AF.Ln, scale=1.0 / 256, bias=EPS),
             reads=[sb_], writes=[lnvB])
        P.op("act", lambda e: e.activation(out=rstd_g[:].rearrange("p h c -> p (h c)"), in_=lnv[:], func=AF.Exp,
                                           scale=-0.5), reads=[lnvB], writes=[rstdgB])
        for k in range(2):
            ot, ob = obanks[k]
            P.op("dve", lambda e, ot=ot, k=k: e.tensor_tensor(
                out=otmp[:, k * 4:(k + 1) * 4, :].rearrange("p (h v) c -> p h v c", v=2),
                in0=ot[:, :].rearrange("p (h v c) -> p h v c", h=2, v=2),
                in1=rstd_g[:, k * 2:(k + 1) * 2, :].unsqueeze(2).broadcast_to([128, 2, 2, 128]), op=ALU.mult),
                reads=[ob, rstdgB], writes=[otmpB])
        P.op("pool", lambda e: e.tensor_tensor(out=slab[:, S_OGN:S_OGN + 8, tok], in0=otmp[:],
                                               in1=slab[:, S_GSIL:S_GSIL + 8, tok], op=ALU.mult),
             reads=[otmpB] + slabB[S_GSIL:S_GSIL + 8], writes=slabB[S_OGN:S_OGN + 8])
        yield

    def attn_tile(t, first):
        tok = slice(t * 128, (t + 1) * 128)
        k = 0
        for g in range(2):
            for kb in range(2):
                bt, bb = nb(PATT)
                s_ = t + kb
                P.op("pe", lambda e, bt=bt, g=g, s_=s_: e.matmul(bt[:, :], lhsT=ka_pad[g][:, s_ * 128:(s_ + 1) * 128],
                                                                 rhs=slab[:, S_QA:S_QA + 4, tok], start=True,
                                                                 stop=True),
                     reads=[kaB[s_]] + slabB[S_QA:S_QA + 4], writes=[bb])
                i = k % 2
                P.op("dve", lambda e, bt=bt, g=g, kb=kb, i=i: e.tensor_tensor(
                    out=sT[i][:], in0=bt[:, :], in1=biasT8[:, kb, g * 512:(g + 1) * 512], op=ALU.add),
                    reads=[bb, biasB], writes=[sTB[i]])
                if first and kb == 0:
                    P.op("dve", lambda e, i=i: e.tensor_tensor(
                        out=sT[i][:].rearrange("p (j q) -> p j q", j=4), in0=sT[i][:].rearrange("p (j q) -> p j q", j=4),
                        in1=mask0[:].unsqueeze(1).broadcast_to([128, 4, 128]), op=ALU.add),
                        reads=[sTB[i], mask0B], writes=[sTB[i]])
                P.op("act", lambda e, i=i, k=k: e.activation(out=esT[:, k, :], in_=sT[i][:], func=AF.Exp, scale=0.125),
                     reads=[sTB[i]], writes=[esTB[k]])
                k += 1
                if k == 2:
                    yield
        yield
        ot, ob = nb(PATT)
        dt_, db = nb(PATT)
        fo, fd = [], []
        for k, (g, kb) in enumerate([(0, 0), (0, 1), (1, 0), (1, 1)]):
            s_ = t + kb
            fo.append(lambda e, g=g, s_=s_, k=k: e.matmul(ot[:, :], lhsT=va_pad[:, s_, g, :], rhs=esT[:, k, :],
                                                          start=(k == 0), stop=(k == 3)))
            fd.append(lambda e, g=g, k=k: e.matmul(dt_[:, :], lhsT=ones_pad[:, g, :], rhs=esT[:, k, :],
                                                   start=(k == 0), stop=(k == 3)))
        P.op("pe", fd, reads=[onespB] + esTB, writes=[db])
        P.op("pe", fo, reads=[vaB[t], vaB[t + 1]] + esTB, writes=[ob])
        P.op("dve", lambda e: e.tensor_tensor(out=dtmp[:].rearrange("p (j q) -> p j q", j=4),
                                              in0=dt_[:, :].rearrange("p (j q) -> p j q", j=4),
                                              in1=esink[:].unsqueeze(2).broadcast_to([128, 4, 128]), op=ALU.add),
             reads=[db, esinkB], writes=[dtmpB])
        P.op("act", lambda e: e.activation(out=dtmp[:], in_=dtmp[:], func=AF.Ln), reads=[dtmpB], writes=[dtmpB])
        P.op("act", lambda e: e.activation(out=dtmp[:], in_=dtmp[:], func=AF.Exp, scale=-1.0), reads=[dtmpB],
             writes=[dtmpB])
        yield
        P.op("dve", lambda e: e.tensor_tensor(out=slab[:, S_OA:S_OA + 4, tok],
                                              in0=ot[:, :].rearrange("p (j q) -> p j q", j=4),
                                              in1=dtmp[:].rearrange("p (j q) -> p j q", j=4), op=ALU.mult),
             reads=[ob, dtmpB], writes=slabB[S_OA:S_OA + 4])
        yield

    def run_merged(gens):
        gens = list(gens)
        while gens:
            for g in list(gens):
                try:
                    next(g)
                except StopIteration:
                    gens.remove(g)

    def run_weighted(pairs):
        pairs = [[g, w] for g, w in pairs]
        while pairs:
            for pr in list(pairs):
                for _ in range(pr[1]):
                    try:
                        next(pr[0])
                    except StopIteration:
                        pairs.remove(pr)
                        break

    def run_gla_region(full, extra, delay):
        ge = chain(gla_tile(t, full) for t in (0, 2))
        go = chain(gla_tile(t, full) for t in (1, 3))
        gens = [[ge, 0], [go, delay]] + [[g, 0] for g in extra]
        step = 0
        while gens:
            for pr in list(gens):
                if step < pr[1]:
                    continue
                try:
                    next(pr[0])
                except StopIteration:
                    gens.remove(pr)
            step += 1

    def chain(gs):
        for g in gs:
            yield from g

    NPB = NTP // 4
    hT_alt = (slab[:, 0:8, :], slabB[0:8])

    def hsel(bp):
        return hT_alt if (NPB - 1 - bp) % 2 == 0 else (hT, [hTB])

    mark('P0.norm')
    norm_to_hT(0, load_x(xp_d, 0), hsel(0))
    for bp in range(NPB):
        last = bp == NPB - 1
        src = hsel(bp)
        mark(f'P{bp}.proj')
        ld = load_x(x_d, 0) if last else load_x(xp_d, (bp + 1) * 512)
        for t in range(4):
            ld(t)
        if last:
            proj_kv_a(src)
            carry_kv()
            nxt = norm_gen(0, None, None, PPROJ)
        else:
            nxt = norm_gen(0, None, hsel(bp + 1), PPROJ)
        pg = prefix_proj_gen(src)
        next(pg)
        next(pg)
        next(pg)
        for _ in range(2):
            if pending_casts:
                pending_casts.pop(0)()
        mark(f'P{bp}.gla')
        run_gla_region(False, [pg, nxt], 2)
    while pending_casts:
        pending_casts.pop(0)()
    P.op("act", lambda e: e.activation(out=Sbf[0][:], in_=S_st[:], func=AF.Copy), reads=[SB_], writes=[SbfB[0]])
    state_par[0] = 0

    psem = [P.new_sem() for _ in range(2)]
    for b in range(NT // 4):
        row0 = b * 512
        mark(f'M{b}.norm1')
        if b > 0:
            norm_to_hT(0, None)
        mark(f'M{b}.inproj')
        for j0 in (0, 2):
            srcs = []
            for g in range(2):
                for jj in range(2):
                    c0 = C_QA + g * 256 + (j0 + jj) * 64
                    src = wi_s[:, c0:c0 + 64].rearrange("(k p) c -> p k c", p=128)
                    srcs.append((lambda t, g=g, jj=jj: t[:, 0:2048].rearrange(
                        "p (k j g d) -> p k j g d", k=8, j=2, g=2)[:, :, jj, g, :], src))
            wt, wB = wload(srcs, "wiB", 8, 256)
            wv = wt[:, 0:2048].rearrange("p (k j c) -> p k j c", k=8, j=2)
            for jj in range(2):
                bt, bb = nb()
                P.op("pe", [(lambda e, kc=kc, jj=jj, bt=bt, wv=wv: e.matmul(bt[:, :], lhsT=wv[:, kc, jj, :],
                                                                            rhs=hT[:, kc, :], start=(kc == 0),
                                                                            stop=(kc == 7))) for kc in range(8)],
                     reads=[wB, hTB], writes=[bb])
                evac_copy(slab[:, S_QA + j0 + jj, :], bt[:, :], [bb], [slabB[S_QA + j0 + jj]])
        proj_kv_a()
        for (c_base, s_base) in ((C_QG, S_QG), (C_KG, S_KG)):
            for i in range(2):
                w, wB = wload_std(wi_s, wi_name(c_base + i * 256), 0, 8, c_base + i * 256, 256)
                for cc in range(2):
                    bank = nb()
                    fm_mm(bank, w, wB, cc * 128, hT, [hTB], 8)
                    evac_copy(slab[:, s_base + i * 2 + cc, :], bank[0][:, :], [bank[1]], [slabB[s_base + i * 2 + cc]])
        proj_gk()
        proj_gla_tm()
        mark(f'M{b}.attn')
        for _ in og_gen():
            pass
        run_gla_region(True, [chain(attn_tile(t, first=(b == 0 and t == 0)) for t in range(4))], 3)
        carry_kv()
        mark(f'M{b}.mixed')
        for qq in range(4):
            wga, wgaB = wload_std(wi_s, "wiC", 0, 8, C_GA + qq * 256, 256)
            wgg, wggB = wload_std(wi_s, "wiD", 0, 8, C_GG + qq * 256, 256)
            srcs = []
            for g in range(2):
                src = wpa_s[g * 256:(g + 1) * 256, qq * 256:(qq + 1) * 256].rearrange("(j p) c -> p j c", p=64)
                srcs.append((lambda t, g=g: t[g * 64:(g + 1) * 64, 0:1024].rearrange("p (j c) -> p j c", j=4), src))
            wpat, wpaB = wload(srcs, "wpa", 4, 256)
            wpa = wpat[:, 0:1024].rearrange("p (j c) -> p j c", j=4)
            wpg, wpgB = wload_std(wpg_s, "wpg", 0, 8, qq * 256, 256)
            for mm_ in range(2):
                m = qq * 2 + mm_
                bga, bgg, bya, byg = nb(), nb(), nb(), nb()
                fm_mm(bga, wga, wgaB, mm_ * 128, hT, [hTB], 8)
                P.op("act", lambda e, bt=bga[0]: e.activation(out=ta[:], in_=bt[:, :], func=AF.Tanh, scale=0.5),
                     reads=[bga[1]], writes=[taB])
                fm_mm(bgg, wgg, wggB, mm_ * 128, hT, [hTB], 8)
                P.op("act", lambda e, bt=bgg[0]: e.activation(out=tg[:], in_=bt[:, :], func=AF.Tanh, scale=0.5),
                     reads=[bgg[1]], writes=[tgB])
                fm_mm(bya, wpa, wpaB, mm_ * 128, slab[:, S_OA:S_OA + 4, :], slabB[S_OA:S_OA + 4], 4)
                fm_mm(byg, wpg, wpgB, mm_ * 128, slab[:, S_OGN:S_OGN + 8, :], slabB[S_OGN:S_OGN + 8], 8)
                P.op("dve", lambda e, bt=bya[0]: e.scalar_tensor_tensor(out=m1[:], in0=ta[:], scalar=1.0, in1=bt[:, :],
                                                                        op0=ALU.add, op1=ALU.mult),
                     reads=[taB, bya[1]], writes=[m1B])
                P.op("dve", lambda e, bt=byg[0]: e.scalar_tensor_tensor(out=m2[:], in0=tg[:], scalar=1.0, in1=bt[:, :],
                                                                        op0=ALU.add, op1=ALU.mult),
                     reads=[tgB, byg[1]], writes=[m2B])
                P.op("pool", lambda e, m=m: e.tensor_tensor(out=slab[:, S_MIX + m, :], in0=m1[:], in1=m2[:],
                                                            op=ALU.add),
                     reads=[m1B, m2B], writes=[slabB[S_MIX + m]])

        def proj_res(scr, name, nkc, act_view, actB, epi):
            for half in range(2):
                tb = [nb() for _ in range(4)]
                ngrp = (nkc + 3) // 4
                for kg in range(ngrp):
                    k0 = kg * 4
                    nk = min(4, nkc - k0)
                    w, wB = wload_std(scr, name, k0 * 128, nk, half * 512, 512)
                    for t in range(4):
                        bt, bb = tb[t]
                        P.op("pe", [(lambda e, kk=kk, t=t, bt=bt, w=w, k0=k0: e.matmul(
                            bt[:, :], lhsT=act_view[:, k0 + kk, t * 128:(t + 1) * 128], rhs=w[:, kk, :],
                            start=(k0 + kk == 0), stop=(k0 + kk == nkc - 1))) for kk in range(nk)],
                            reads=[wB] + list(actB), writes=[bb])
                for t in range(4):
                    epi(t, half, tb[t])

        def epi_half(t, half, bank):
            bt, bb = bank
            hs = slice(half * 512, (half + 1) * 512)
            P.op("dve", lambda e: e.scalar_tensor_tensor(out=xres[t][:, hs], in0=bt[:, :], scalar=0.5,
                                                         in1=xres[t][:, hs], op0=ALU.mult, op1=ALU.add),
                 reads=[bb, xresB[t]], writes=[xresB[t]])

        def epi_add(t, half, bank):
            bt, bb = bank
            hs = slice(half * 512, (half + 1) * 512)
            P.op("dve", lambda e: e.tensor_tensor(out=xres[t][:, hs], in0=bt[:, :], in1=xres[t][:, hs], op=ALU.add),
                 reads=[bb, xresB[t]], writes=[xresB[t]])

        mark(f'M{b}.wout')
        proj_res(wout_s, "wout", 8, slab[:, S_MIX:S_MIX + 8, :], slabB[S_MIX:S_MIX + 8], epi_half)

        mark(f'M{b}.norm2')
        norm_to_hT(1, None)
        mark(f'M{b}.ffnin')
        for ii in range(11):
            wg_, wgB_ = wload_std(wfi_s, "wfi", 0, 8, ii * 256, 256)
            wu_, wuB_ = wload_std(wfi_s, "wfi", 0, 8, D_FF + ii * 256, 256)
            for s in range(2):
                c = ii * 2 + s
                bg, bu = nb(), nb()
                fm_mm(bg, wg_, wgB_, s * 128, hT, [hTB], 8)
                fm_mm(bu, wu_, wuB_, s * 128, hT, [hTB], 8)
                f = c % 2
                P.op("act", lambda e, bt=bg[0], f=f: e.activation(out=ftt[f][:], in_=bt[:, :], func=AF.Tanh, scale=0.5),
                     reads=[bg[1]], writes=[fttB[f]])
                P.op("dve", lambda e, bt=bg[0], f=f: e.scalar_tensor_tensor(out=fa[f][:], in0=ftt[f][:], scalar=1.0,
                                                                           in1=bt[:, :], op0=ALU.add, op1=ALU.mult),
                     reads=[fttB[f], bg[1]], writes=[faB[f]])
                P.op("dve", lambda e, bt=bu[0], f=f, c=c: e.scalar_tensor_tensor(
                    out=slab[:, S_ACT + c, :], in0=fa[f][:], scalar=0.5, in1=bt[:, :], op0=ALU.mult, op1=ALU.mult),
                    reads=[faB[f], bu[1]], writes=[slabB[S_ACT + c]])
        mark(f'M{b}.ffnout')
        proj_res(wfo_s, "wfo", 22, slab[:, S_ACT:S_ACT + 22, :], slabB[S_ACT:S_ACT + 22], epi_add)

        mark(f'M{b}.norm3')
        norm_to_hT(2, None)
        mark(f'M{b}.ple')
        for t in range(4):
            i = t % 2
            P.dma("sp", lambda e, t=t, i=i, row0=row0: e.dma_start(out=p_f[i][:], in_=p_d[row0 + t * 128: row0 + (t + 1) * 128, :]),
                  psem[i], writes=[p_fB[i]])
            P.op("pool", lambda e, i=i: e.tensor_copy(out=p_bf[i][:], in_=p_f[i][:]), reads=[p_fB[i]],
                 writes=[p_bfB[i]])
            bt, bb = nb()
            btb = bt[:].bitcast(BF16)
            P.op("pe", [(lambda e, kc=kc, i=i, btb=btb: e.transpose(out=btb[:, kc * 128:(kc + 1) * 128],
                                                                    in_=p_bf[i][:, kc * 128:(kc + 1) * 128],
                                                                    identity=ident[:])) for kc in range(2)],
                 reads=[p_bfB[i], identB], writes=[bb])
            P.op("act", lambda e, t=t, btb=btb: e.activation(out=pT[:, :, t * 128:(t + 1) * 128],
                                                             in_=btb[:, 0:256].rearrange("p (k c) -> p k c", k=2),
                                                             func=AF.Copy), reads=[bb], writes=[pTB])

        def epi_gate(t, half, bank):
            bt, bb = bank
            P.op("act", lambda e: e.activation(out=esT[:, t, :], in_=bt[:, :], func=AF.Tanh, scale=0.5), reads=[bb],
                 writes=[tgtB[t]])

        def epi_ple(t, half, bank):
            bt, bb = bank
            hs = slice(half * 512, (half + 1) * 512)
            P.op("dve", lambda e: e.scalar_tensor_tensor(out=m1[:], in0=esT[:, t, :], scalar=1.0, in1=bt[:, :],
                                                         op0=ALU.add, op1=ALU.mult),
                 reads=[tgtB[t], bb], writes=[m1B])
            P.op("dve", lambda e: e.scalar_tensor_tensor(out=xres[t][:, hs], in0=m1[:], scalar=0.5, in1=xres[t][:, hs],
                                                         op0=ALU.mult, op1=ALU.add),
                 reads=[m1B, xresB[t]], writes=[xresB[t]])

        for half in range(2):
            tb = [nb() for _ in range(4)]
            for kg in range(2):
                w, wB = wload_std(wpgate_s, "wpgate", kg * 512, 4, half * 512, 512)
                for t in range(4):
                    bt, bb = tb[t]
                    P.op("pe", [(lambda e, kk=kk, t=t, bt=bt, w=w, kg=kg: e.matmul(
                        bt[:, :], lhsT=hT[:, kg * 4 + kk, t * 128:(t + 1) * 128], rhs=w[:, kk, :],
                        start=(kg * 4 + kk == 0), stop=(kg * 4 + kk == 7))) for kk in range(4)],
                        reads=[wB, hTB], writes=[bb])
            for t in range(4):
                epi_gate(t, half, tb[t])
            w, wB = wload_std(wple_s, "wple", 0, 2, half * 512, 512)
            for t in range(4):
                bt, bb = nb()
                P.op("pe", [(lambda e, kk=kk, t=t, bt=bt, w=w: e.matmul(bt[:, :], lhsT=pT[:, kk, t * 128:(t + 1) * 128],
                                                                        rhs=w[:, kk, :], start=(kk == 0),
                                                                        stop=(kk == 1))) for kk in range(2)],
                     reads=[wB, pTB], writes=[bb])
                epi_ple(t, half, (bt, bb))

        mark(f'M{b}.final')
        nxt_load = load_x(x_d, row0 + 512) if b + 1 < NT // 4 else None

        def out_dma(t):
            o = P.dma("sp", lambda e, t=t, row0=row0: e.dma_start(out=out_d[row0 + t * 128: row0 + (t + 1) * 128, :],
                                                              in_=ystage[t % 2][:]), osem[t % 2],
                      reads=[ystageB[t % 2]])
            if b == NT // 4 - 1:
                P.final.append(o)

        for t in range(4):
            P.op("act", lambda e, t=t: e.activation(out=junk_ap, in_=xres[t][:], func=AF.Square,
                                                   accum_out=stat[t][:, 0:1]),
                 reads=[xresB[t]], writes=[junkB, statB[t]])
            P.op("pool", lambda e, t=t: e.tensor_scalar(out=stat[t][:, 1:2], in0=stat[t][:, 0:1], scalar1=1.0 / D,
                                                       scalar2=EPS, op0=ALU.mult, op1=ALU.add),
                 reads=[statB[t]], writes=[statB[t]])
            P.op("pool", lambda e, t=t: e.tensor_tensor(out=stat[t][:, 2:3], in0=stat[t][:, 1:2], in1=mhalf[:, 0:1],
                                                       op=ALU.pow), reads=[statB[t], mhalfB], writes=[statB[t]])
            P.op("dve", lambda e, t=t: e.scalar_tensor_tensor(out=ystage[t % 2][:], in0=xres[t][:],
                                                             scalar=stat[t][:, 2:3], in1=gfin[:], op0=ALU.mult,
                                                             op1=ALU.mult),
                 reads=[xresB[t], statB[t], gfinB], writes=[ystageB[t % 2]])
            if nxt_load is not None:
                nxt_load(t)
            if t >= 1:
                out_dma(t - 1)
        out_dma(3)

    mark('end')
    nc._marks = marks
    P.emit()
    return nc


def _t5_bucket(dist):
    max_exact = 16
    d_f = np.maximum(dist, max_exact).astype(np.float32)
    large = max_exact + (np.log(d_f / max_exact) / math.log(128 / max_exact) * (32 - max_exact)).astype(np.int32)
    large = np.minimum(large, 31)
    return np.where(dist < max_exact, dist, large)


def _consts():
    s = np.arange(128)[:, None]
    t = np.arange(128)[None, :]
    same = (s // 64) == (t // 64)
    cst = np.zeros((128, 386), np.float32)
    cst[:, 0:128] = np.where(same & (s <= t), -1.0 / 16, 0.0)
    cst[:, 128:256] = np.where(same & (s > t), -1.0 / 16, 0.0)
    cst[:, 256] = np.where(np.arange(128) < 64, -1.0 / 16, 0.0)
    cst[:, 257] = np.where(np.arange(128) >= 64, -1.0 / 16, 0.0)
    cst[:, 258:386] = np.where(same & (s <= t), 1.0, 0.0)
    k = np.arange(128)[:, None]
    q = np.arange(128)[None, :]
    maskT = np.zeros((2, 128, 128), np.float32)
    maskT[0] = np.where(k > q, 0.0, NEG)
    maskT[1] = np.where(k <= q, 0.0, NEG)
    bucket = np.zeros((2, 128, 128), np.int64)
    for kb in range(2):
        dist = q + 128 - (kb * 128 + k)
        bucket[kb] = _t5_bucket(np.maximum(dist, 0))
    return cst, maskT, bucket


def _host_inputs(x, p, w_in, w_gk2, b_gk, sinks, rel_table, w_proj_attn, w_proj_gla, gla_norm, w_out, norm_mix,
                 norm_ffn, w_ffn_in, w_ffn_out, norm_ple, w_ple_gate, w_ple, norm_final, T):
    f = lambda a: np.ascontiguousarray(np.asarray(a, dtype=np.float32))
    x, p = f(x), f(p)
    B, S, _ = x.shape
    halves = S // T
    cst, maskT, bucket = _consts()
    rel = f(rel_table)
    biasT = np.ascontiguousarray(np.transpose(rel[bucket], (0, 1, 3, 2))).reshape(2, 128, 1024)
    sk = f(sinks)[0]
    sink_l = np.zeros((128, 4), np.float32)
    sink_l[0:64, :] = sk[0:4][None, :]
    sink_l[64:128, :] = sk[4:8][None, :]
    gcols = np.concatenate([f(norm_mix)[0].reshape(8, 128).T, f(norm_ffn)[0].reshape(8, 128).T,
                            f(norm_ple)[0].reshape(8, 128).T], axis=1)
    gn_col = np.ascontiguousarray(f(gla_norm)[0].reshape(2, 128).T)
    wgk_aug = np.concatenate([f(w_gk2)[0], f(b_gk)[0][None, :]], axis=0)
    shared = {
        "w_in": f(w_in)[0], "w_proj_attn": f(w_proj_attn)[0], "w_proj_gla": f(w_proj_gla)[0], "w_out": f(w_out)[0],
        "w_ffn_in": f(w_ffn_in)[0], "w_ffn_out": f(w_ffn_out)[0], "w_ple_gate": f(w_ple_gate)[0],
        "w_ple": f(w_ple)[0], "wgk_aug": np.ascontiguousarray(wgk_aug), "gcols": np.ascontiguousarray(gcols),
        "gn_col": gn_col, "norm_final": f(norm_final), "sink_l": sink_l, "biasT": biasT, "maskT": maskT, "cst": cst,
    }
    maps = []
    for b in range(B):
        for h in range(halves):
            m = dict(shared)
            m["x"] = np.ascontiguousarray(x[b, h * T:(h + 1) * T])
            m["p"] = np.ascontiguousarray(p[0, b, h * T:(h + 1) * T])
            if h == 0:
                m["xp"] = np.zeros((T, D), np.float32)
                m["mask0"] = np.full((128, 128), NEG, np.float32)
            else:
                m["xp"] = np.ascontiguousarray(x[b, (h - 1) * T:h * T])
                m["mask0"] = np.zeros((128, 128), np.float32)
            maps.append(m)
    return maps, B, halves


def run(inputs, T):
    maps, B, halves = _host_inputs(T=T, **inputs)
    nc = build_nc(T // 128, T // 128)
    n = len(maps)
    res = run_bass_kernel_spmd(nc, maps, core_ids=list(range(n)))
    out = np.zeros((B, halves * T, D), np.float32)
    for i, r in enumerate(res.results):
        b, h = i // halves, i % halves
        out[b, h * T:(h + 1) * T] = r["out"]
    return out


def kernel(**inputs):
    return run(inputs, 4096)
```

```python
import math
import numpy as np
import concourse.bass as bass
import concourse.mybir as mybir
from concourse.bass_utils import run_bass_kernel_spmd

F32 = mybir.dt.float32
BF16 = mybir.dt.bfloat16
AF = mybir.ActivationFunctionType
ALU = mybir.AluOpType

D = 1024
D_IN = 5904
D_FF = 2816
EPS = 1e-6
NEG = -30000.0
C_QA, C_KA, C_VA, C_QG, C_KG, C_VG, C_GK, C_OG, C_GA, C_GG = 0, 512, 640, 768, 1280, 1792, 2816, 2832, 3856, 4880


class Buf:
    __slots__ = ("name", "last_w", "readers")

    def __init__(self, name):
        self.name = name
        self.last_w = None
        self.readers = []


class Op:
    __slots__ = ("eng", "fns", "deps", "sem", "val", "signal", "is_dma")

    def __init__(self, eng, fns):
        self.eng = eng
        self.fns = fns
        self.deps = []
        self.sem = None
        self.val = 0
        self.signal = False
        self.is_dma = False


class Prog:
    ENGS = ("pe", "act", "dve", "pool", "sp")

    def __init__(self, nc):
        self.nc = nc
        self.ops = {e: [] for e in self.ENGS}
        self.esem = {e: nc.alloc_semaphore("es_" + e) for e in self.ENGS}
        self.dma_cnt = {}
        self.nsem = 0
        self.final = []

    def buf(self, name):
        return Buf(name)

    def new_sem(self):
        self.nsem += 1
        return self.nc.alloc_semaphore(f"ds_{self.nsem}")

    def _track(self, op, reads, writes):
        deps = []
        for b in reads:
            if b.last_w is not None:
                deps.append(b.last_w)
        for b in writes:
            if b.last_w is not None:
                deps.append(b.last_w)
            deps.extend(b.readers)
        for b in reads:
            b.readers.append(op)
        for b in writes:
            b.last_w = op
            b.readers = []
        seen = set()
        for d in deps:
            if d is op or id(d) in seen:
                continue
            seen.add(id(d))
            if d.eng == "pe" and op.eng == "pe" and not d.is_dma and not op.is_dma:
                continue
            op.deps.append(d)
            d.signal = True

    def op(self, eng, fns, reads=(), writes=()):
        if callable(fns):
            fns = [fns]
        o = Op(eng, list(fns))
        self._track(o, reads, writes)
        self.ops[eng].append(o)
        return o

    def dma(self, eng, fns, sem, reads=(), writes=()):
        if callable(fns):
            fns = [fns]
        o = Op(eng, list(fns))
        o.is_dma = True
        o.sem = sem
        c = self.dma_cnt.get(id(sem), 0) + 16 * len(o.fns)
        self.dma_cnt[id(sem)] = c
        o.val = c
        self._track(o, reads, writes)
        self.ops[eng].append(o)
        return o

    def emit(self):
        nc = self.nc
        for e in self.ENGS:
            c = 0
            for o in self.ops[e]:
                if o.is_dma:
                    continue
                if o.signal:
                    c += 1
                    o.sem = self.esem[e]
                    o.val = c
        handles = {"pe": "tensor", "act": "scalar", "dve": "vector", "pool": "gpsimd", "sp": "sync"}
        final = self.final
        with nc.Block() as block:
            for e in self.ENGS:
                ops = self.ops[e]

                def body(eng, ops=ops, e=e):
                    waited = {}
                    for o in ops:
                        for d in o.deps:
                            k = id(d.sem)
                            if waited.get(k, 0) >= d.val:
                                continue
                            eng.wait_ge(d.sem, d.val)
                            waited[k] = d.val
                        n = len(o.fns)
                        for i, f in enumerate(o.fns):
                            ins = f(eng)
                            if o.is_dma:
                                ins.then_inc(o.sem, 16)
                            elif o.signal and i == n - 1:
                                ins.then_inc(o.sem, 1)
                    if e == "sp":
                        for d in final:
                            eng.wait_ge(d.sem, d.val)

                getattr(block, handles[e])(body)


def build_nc(NT, NTP, dbg=None):
    assert NT % 4 == 0 and NTP % 4 == 0
    nc = bass.Bass("TRN2", target_bir_lowering=False)
    P = Prog(nc)
    T, TP = NT * 128, NTP * 128

    def din(name, shape, dt=F32):
        return nc.dram_tensor(name, list(shape), dt, kind="ExternalInput").ap()

    x_d = din("x", [T, D])
    xp_d = din("xp", [TP, D])
    p_d = din("p", [T, 256])
    w_in_d = din("w_in", [D, D_IN])
    wpa_d = din("w_proj_attn", [512, D])
    wpg_d = din("w_proj_gla", [D, D])
    wout_d = din("w_out", [D, D])
    wfi_d = din("w_ffn_in", [D, 2 * D_FF])
    wfo_d = din("w_ffn_out", [D_FF, D])
    wpgate_d = din("w_ple_gate", [D, D])
    wple_d = din("w_ple", [256, D])
    wgk_d = din("wgk_aug", [17, 512])
    gcols_d = din("gcols", [128, 24])
    gn_d = din("gn_col", [128, 2])
    gfin_d = din("norm_final", [D])
    sinkl_d = din("sink_l", [128, 4])
    biasT_d = din("biasT", [2, 128, 1024])
    maskT_d = din("maskT", [2, 128, 128])
    mask0_d = din("mask0", [128, 128])
    cst_d = din("cst", [128, 386])
    out_d = nc.dram_tensor("out", [T, D], F32, kind="ExternalOutput").ap()

    def dscr(name, shape):
        return nc.dram_tensor(name, list(shape), BF16, kind="Internal").ap()

    wi_s = dscr("wi_s", [D, D_IN])
    wpa_s = dscr("wpa_s", [512, D])
    wpg_s = dscr("wpg_s", [D, D])
    wout_s = dscr("wout_s", [D, D])
    wfi_s = dscr("wfi_s", [D, 2 * D_FF])
    wfo_s = dscr("wfo_s", [D_FF, D])
    wpgate_s = dscr("wpgate_s", [D, D])
    wple_s = dscr("wple_s", [256, D])

    def sb(name, shape, dt=F32):
        return nc.alloc_sbuf_tensor("s_" + name, list(shape), dt)

    xres = [sb(f"xres{t}", [128, D]) for t in range(4)]
    xresB = [P.buf(f"xres{t}") for t in range(4)]
    xn = [sb(f"xn{i}", [128, D], BF16) for i in range(2)]
    xnB = [P.buf(f"xn{i}") for i in range(2)]
    stat = [sb(f"stat{t}", [128, 4]) for t in range(4)]
    statB = [P.buf(f"stat{t}") for t in range(4)]
    hT = sb("hT", [128, 8, 512], BF16)
    hTB = P.buf("hT")
    slab = sb("slab", [128, 40, 512], BF16)
    slabB = [P.buf(f"slab{i}") for i in range(40)]
    S_ACT, S_GSIL, S_OGN, S_MIX, S_QA, S_QG, S_KG, S_OA = 0, 0, 8, 16, 24, 28, 32, 36
    ka_pad = [sb(f"ka_pad{g}", [128, 640], BF16) for g in range(2)]
    kaB = [P.buf(f"ka_s{s}") for s in range(5)]
    va_pad = sb("va_pad", [128, 5, 2, 128], BF16)
    vaB = [P.buf(f"va_s{s}") for s in range(5)]
    kgTM = [sb(f"kgTM{t}", [128, 512], BF16) for t in range(4)]
    kgTMB = [P.buf(f"kgTM{t}") for t in range(4)]
    vgTM = [sb(f"vgTM{t}", [128, 1024], BF16) for t in range(4)]
    vgTMB = [P.buf(f"vgTM{t}") for t in range(4)]
    gkT = sb("gkT", [32, 512])
    gkTB = P.buf("gkT")
    sp_t2 = [sb(f"sp_t{i}", [128, 512]) for i in range(2)]; spB2 = [P.buf(f"sp{i}") for i in range(2)]
    ED2 = [sb(f"ED{i}", [128, 512]) for i in range(2)]; EDB2 = [P.buf(f"ED{i}") for i in range(2)]
    Epos2 = [sb(f"Epos{i}", [128, 512]) for i in range(2)]; EposB2 = [P.buf(f"Epos{i}") for i in range(2)]
    Eneg2 = [sb(f"Eneg{i}", [128, 512]) for i in range(2)]; EnegB2 = [P.buf(f"Eneg{i}") for i in range(2)]
    el2 = [sb(f"el{i}", [128, 8]) for i in range(2)]; elB2 = [P.buf(f"el{i}") for i in range(2)]
    qt2 = [sb(f"qt{i}", [128, 4, 128], BF16) for i in range(2)]; qtB2 = [P.buf(f"qt{i}") for i in range(2)]
    kt2 = [sb(f"kt{i}", [128, 4, 128], BF16) for i in range(2)]; ktB2 = [P.buf(f"kt{i}") for i in range(2)]
    kd2 = [[sb(f"kd{i}_{c}", [128, 512], BF16) for c in range(2)] for i in range(2)]
    kdB2 = [[P.buf(f"kd{i}_{c}") for c in range(2)] for i in range(2)]
    ATs = sb("ATs", [128, 4, 128], BF16); ATsB = P.buf("ATs")
    sq = sb("sq", [128, 8, 128], BF16); sqB = P.buf("sq")
    junk_ap = sq[:].rearrange("p a b -> p (a b)")
    junkB = sqB
    rstd_g = sb("rstd_g", [128, 4, 128]); rstdgB = P.buf("rstd_g")
    otmp = sb("otmp", [128, 8, 128]); otmpB = P.buf("otmp")
    S_st = sb("S_st", [128, 1024]); SB_ = P.buf("S_st")
    Sbf = [sb(f"Sbf{i}", [128, 1024], BF16) for i in range(2)]
    SbfB = [P.buf(f"Sbf{i}") for i in range(2)]
    sT = [sb(f"sT{i}", [128, 512]) for i in range(2)]
    sTB = [P.buf(f"sT{i}") for i in range(2)]
    esT = sb("esT", [128, 4, 512], BF16)
    esTB = [P.buf(f"esT{i}") for i in range(4)]
    biasT8 = sb("biasT8", [128, 2, 1024]); biasB = P.buf("biasT8")
    mask0 = sb("mask0", [128, 128]); mask0B = P.buf("mask0")
    maskT = sb("maskT", [128, 2, 128]); maskTB = P.buf("maskT")
    esink = sb("esink", [128, 4]); esinkB = P.buf("esink")
    ta = sb("ta", [128, 512], BF16); taB = P.buf("ta")
    tg = sb("tg", [128, 512], BF16); tgB = P.buf("tg")
    m1 = sb("m1", [128, 512]); m1B = P.buf("m1")
    m2 = sb("m2", [128, 512]); m2B = P.buf("m2")
    lnv, lnvB = m1, m1B
    dtmp, dtmpB = m2, m2B
    ftt = [sb(f"ftt{i}", [128, 512]) for i in range(2)]
    fttB = [P.buf(f"ftt{i}") for i in range(2)]
    fa = [sb(f"fa{i}", [128, 512]) for i in range(2)]
    faB = [P.buf(f"fa{i}") for i in range(2)]
    p_f = [sb(f"p_f{i}", [128, 256]) for i in range(2)]
    p_fB = [P.buf(f"p_f{i}") for i in range(2)]
    p_bf = [sb(f"p_bf{i}", [128, 256], BF16) for i in range(2)]
    p_bfB = [P.buf(f"p_bf{i}") for i in range(2)]
    pT = sb("pT", [128, 2, 512], BF16); pTB = P.buf("pT")
    tgtB = esTB
    gfin = sb("gfin", [128, D]); gfinB = P.buf("gfin")
    gcols = sb("gcols", [128, 24]); gcolsB = P.buf("gcols")
    gnh = sb("gnh", [128, 2]); gnhB = P.buf("gnh")
    cst = sb("cst", [128, 386]); cstB = P.buf("cst")
    wgk = sb("wgk", [32, 512]); wgkB = P.buf("wgk")
    ident = sb("ident", [128, 128], BF16); identB = P.buf("ident")
    ones_bf = sb("ones_bf", [128, 128], BF16); onesB = P.buf("ones")
    ones_pad = sb("ones_pad", [128, 2, 128], BF16); onespB = P.buf("ones_pad")
    mhalf = sb("mhalf", [128, 1]); mhalfB = P.buf("mhalf")
    NSLOT = 6
    ring = [(sb(f"ws{i}", [128, 2048], BF16), P.buf(f"ws{i}"), P.new_sem()) for i in range(NSLOT)]
    ring_i = [0]
    NBANK = 8
    banks = [(nc.alloc_psum_tensor(f"pb{i}", [128, 512], F32), P.buf(f"pb{i}")) for i in range(NBANK)]
    bank_i = [0]
    marks = []

    def mark(name):
        marks.append((name, sum(len(o.fns) for o in P.ops['pe'])))

    pool_i = {}

    def nb(pool=None):
        if pool is None:
            r = banks[bank_i[0] % NBANK]
            bank_i[0] += 1
            return r
        i = pool_i.get(pool, 0)
        pool_i[pool] = i + 1
        return banks[pool[i % len(pool)]]

    PGLA2, PATT, PPROJ = ((0, 1, 2), (3, 4, 5)), (6, 7), (6, 7)

    U2m = cst[:, 0:128]
    SU2m = cst[:, 128:256]
    ind2m = cst[:, 256:258]
    maskA = cst[:, 258:386]

    sem_c = [P.new_sem() for _ in range(12)]
    P.dma("sp", lambda e: e.dma_start(out=cst[:], in_=cst_d), sem_c[0], writes=[cstB])
    P.dma("sp", lambda e: e.dma_start(out=gcols[:], in_=gcols_d), sem_c[1], writes=[gcolsB])
    P.dma("sp", lambda e: e.dma_start(out=gnh[:], in_=gn_d), sem_c[2], writes=[gnhB])
    P.dma("sp", lambda e: e.dma_start(out=gfin[:], in_=gfin_d.partition_broadcast(128)), sem_c[3], writes=[gfinB])
    P.dma("sp", lambda e: e.dma_start(out=esink[:], in_=sinkl_d), sem_c[4], writes=[esinkB])
    P.dma("sp", lambda e: e.dma_start(out=wgk[0:17, :], in_=wgk_d), sem_c[5], writes=[wgkB])
    P.dma("sp", lambda e: e.dma_start(out=mask0[:], in_=mask0_d), sem_c[6], writes=[mask0B])
    P.dma("sp", lambda e: e.dma_start(out=maskT[:], in_=maskT_d.rearrange("b k q -> k b q")), sem_c[7],
          writes=[maskTB])
    P.dma("sp", lambda e: e.dma_start(out=biasT8[:], in_=biasT_d.rearrange("b k c -> k b c")), sem_c[8],
          writes=[biasB])

    def flat(ap, n):
        return ap.rearrange("a b -> (a b)").rearrange("(n c) -> n c", c=2048)

    castB = {}

    def cast(name, src, dst, nrows_flat, pieces=1):
        b = P.buf("scr_" + name)
        castB[name] = b
        fs, fd = flat(src, nrows_flat), flat(dst, nrows_flat)
        per = nrows_flat // pieces
        fns = []
        for i in range(pieces):
            r0, r1 = i * per, (nrows_flat if i == pieces - 1 else (i + 1) * per)
            fns.append(lambda e, r0=r0, r1=r1: e.dma_start(out=fd[r0:r1, :], in_=fs[r0:r1, :]))
        P.dma("pool", fns, P.new_sem(), writes=[b])

    def cast_cols(name, c0, c1):
        b = P.buf("scr_" + name)
        castB[name] = [b]
        P.dma("pool", lambda e: e.dma_start(out=wi_s[:, c0:c1], in_=w_in_d[:, c0:c1]), P.new_sem(), writes=[b])

    pending_casts = []
    cast_cols("wiA", C_KG, C_OG)
    for nm_, a_, b_, np_ in (("wiB", 0, C_KG, 2), ("wiC", C_OG, C_GG, 4), ("wiD", C_GG, D_IN, 2)):
        castB[nm_] = [P.buf(f"scr_{nm_}{i}") for i in range(np_)]
        for i in range(np_):
            r0, r1 = i * (D // np_), (i + 1) * (D // np_)
            pending_casts.append(lambda nm_=nm_, a_=a_, b_=b_, i=i, r0=r0, r1=r1: P.dma(
                "pool", lambda e: e.dma_start(out=wi_s[r0:r1, a_:b_], in_=w_in_d[r0:r1, a_:b_]), P.new_sem(),
                writes=[castB[nm_][i]]))

    def wi_name(c0):
        return "wiB" if c0 < C_KG else ("wiA" if c0 < C_OG else ("wiC" if c0 < C_GG else "wiD"))
    P.op("pool", lambda e: e.memset(ident[:], 1.0), writes=[identB])
    P.op("pool", lambda e: e.affine_select(out=ident[:], in_=ident[:], pattern=[[-1, 128]], compare_op=ALU.is_equal,
                                           fill=0.0, base=0, channel_multiplier=1), reads=[identB], writes=[identB])
    P.op("pool", lambda e: e.memset(ones_bf[:], 1.0), writes=[onesB])
    P.op("pool", [lambda e: e.memset(ones_pad[:], 0.0)], writes=[onespB])
    P.op("pool", [lambda e: e.memset(ones_pad[:, 0, 0:64], 1.0), lambda e: e.memset(ones_pad[:, 1, 64:128], 1.0)],
         reads=[onespB], writes=[onespB])
    P.op("pool", lambda e: e.memset(mhalf[:], -0.5), writes=[mhalfB])
    P.op("pool", [lambda e: e.memset(ka_pad[0][:], 0.0), lambda e: e.memset(ka_pad[1][:], 0.0),
                  lambda e: e.memset(va_pad[:], 0.0)], writes=kaB + vaB)
    P.op("pool", [lambda e, i=i, c=c: e.memset(kd2[i][c][:], 0.0) for i in range(2) for c in range(2)],
         writes=kdB2[0] + kdB2[1])
    P.op("pool", lambda e: e.memset(gkT[:], 1.0), writes=[gkTB])
    P.op("pool", lambda e: e.memset(S_st[:], 0.0), writes=[SB_])
    P.op("pool", lambda e: e.memset(wgk[:], 0.0), writes=[wgkB]) if False else None
    P.op("act", lambda e: e.activation(out=esink[:], in_=esink[:], func=AF.Exp), reads=[esinkB], writes=[esinkB])
    for kb in range(2):
        P.op("dve", lambda e, kb=kb: e.tensor_tensor(
            out=biasT8[:, kb, :].rearrange("p (h q) -> p h q", h=8),
            in0=biasT8[:, kb, :].rearrange("p (h q) -> p h q", h=8),
            in1=maskT[:, kb, :].unsqueeze(1).broadcast_to([128, 8, 128]), op=ALU.add),
            reads=[biasB, maskTB], writes=[biasB])
    P.op("dve", lambda e: e.tensor_scalar(out=biasT8[:], in0=biasT8[:], scalar1=8.0, scalar2=None, op0=ALU.mult),
         reads=[biasB], writes=[biasB])
    P.op("dve", lambda e: e.tensor_scalar(out=mask0[:], in0=mask0[:], scalar1=8.0, scalar2=None, op0=ALU.mult),
         reads=[mask0B], writes=[mask0B])
    def defer_cast(name, src, dst, nrows_flat, pieces=1):
        castB[name] = [P.buf(f"scr_{name}{i}") for i in range(pieces)]
        fs, fd = flat(src, nrows_flat), flat(dst, nrows_flat)
        per = nrows_flat // pieces
        for i in range(pieces):
            r0, r1 = i * per, (nrows_flat if i == pieces - 1 else (i + 1) * per)
            pending_casts.append(lambda name=name, i=i, r0=r0, r1=r1, fs=fs, fd=fd: P.dma(
                "pool", lambda e: e.dma_start(out=fd[r0:r1, :], in_=fs[r0:r1, :]), P.new_sem(),
                writes=[castB[name][i]]))

    castS = {}
    defer_cast("wpa", wpa_d, wpa_s, 512 * D // 2048)
    defer_cast("wpg", wpg_d, wpg_s, D * D // 2048, pieces=2)
    defer_cast("wout", wout_d, wout_s, D * D // 2048, pieces=2)
    defer_cast("wfi", wfi_d, wfi_s, D * 2 * D_FF // 2048, pieces=10)
    defer_cast("wfo", wfo_d, wfo_s, D_FF * D // 2048, pieces=5)
    defer_cast("wpgate", wpgate_d, wpgate_s, D * D // 2048, pieces=2)
    defer_cast("wple", wple_d, wple_s, 256 * D // 2048)

    def wload(srcs, name, nk, ncols):
        t, b, s = ring[ring_i[0] % NSLOT]
        ring_i[0] += 1
        fns = [(lambda e, dv=dv, sa=sa, t=t: e.dma_start(out=dv(t), in_=sa)) for dv, sa in srcs]
        P.dma("sp", fns, s, reads=castB[name], writes=[b])
        return t, b

    def wload_std(scr, name, r0, nk, c0, ncols):
        src = scr[r0:r0 + nk * 128, c0:c0 + ncols].rearrange("(k p) c -> p k c", p=128)
        t, b = wload([(lambda t: t[:, 0:nk * ncols].rearrange("p (k c) -> p k c", k=nk), src)], name, nk, ncols)
        return t[:, 0:nk * ncols].rearrange("p (k c) -> p k c", k=nk), b

    evac_i = [0]

    def evac_copy(out_ap, in_ap, reads, writes, eng=None):
        if eng is None:
            eng = "act" if evac_i[0] % 2 == 0 else "dve"
            evac_i[0] += 1
        if eng == "act":
            P.op("act", lambda e: e.activation(out=out_ap, in_=in_ap, func=AF.Copy), reads=reads, writes=writes)
        else:
            P.op("dve", lambda e: e.tensor_copy(out=out_ap, in_=in_ap), reads=reads, writes=writes)

    def fm_mm(bank, w, wB, ccol, act_view, actB, nk):
        bt, bb = bank
        fns = [(lambda e, kc=kc: e.matmul(bt[:, :], lhsT=w[:, kc, ccol:ccol + 128], rhs=act_view[:, kc, :],
                                          start=(kc == 0), stop=(kc == nk - 1))) for kc in range(nk)]
        P.op("pe", fns, reads=[wB] + list(actB), writes=[bb])

    def norm_gen(gidx, load_fn, dst=None, pool=None):
        hTc, hTBc = dst if dst is not None else (hT, [hTB])
        for t in range(4):
            if load_fn is not None:
                load_fn(t)
            P.op("act", lambda e, t=t: e.activation(out=junk_ap, in_=xres[t][:], func=AF.Square,
                                                   accum_out=stat[t][:, 0:1]),
                 reads=[xresB[t]], writes=[junkB, statB[t]])
            P.op("pool", lambda e, t=t: e.tensor_scalar(out=stat[t][:, 1:2], in0=stat[t][:, 0:1], scalar1=1.0 / D,
                                                       scalar2=EPS, op0=ALU.mult, op1=ALU.add),
                 reads=[statB[t]], writes=[statB[t]])
            P.op("pool", lambda e, t=t: e.tensor_tensor(out=stat[t][:, 2:3], in0=stat[t][:, 1:2], in1=mhalf[:, 0:1],
                                                       op=ALU.pow),
                 reads=[statB[t], mhalfB], writes=[statB[t]])
            i = t % 2
            P.op("dve", lambda e, t=t, i=i: e.tensor_scalar(out=xn[i][:], in0=xres[t][:], scalar1=stat[t][:, 2:3],
                                                           scalar2=None, op0=ALU.mult),
                 reads=[xresB[t], statB[t]], writes=[xnB[i]])
            bt, bb = nb(pool)
            btb = bt[:].bitcast(BF16)
            P.op("pe", [(lambda e, kc=kc, i=i, btb=btb: e.transpose(out=btb[:, kc * 128:(kc + 1) * 128],
                                                                    in_=xn[i][:, kc * 128:(kc + 1) * 128],
                                                                    identity=ident[:])) for kc in range(8)],
                 reads=[xnB[i], identB], writes=[bb])
            P.op("dve", lambda e, t=t, btb=btb: e.tensor_tensor(
                out=hTc[:, :, t * 128:(t + 1) * 128], in0=btb.rearrange("p (k c) -> p k c", k=8),
                in1=gcols[:, gidx * 8:(gidx + 1) * 8].unsqueeze(2).broadcast_to([128, 8, 128]), op=ALU.mult),
                reads=[bb, gcolsB], writes=hTBc)
            yield

    def norm_to_hT(gidx, load_fn, dst=None):
        for _ in norm_gen(gidx, load_fn, dst):
            pass


    ystage = [sb(f"ystage{i}", [128, D]) for i in range(2)]
    ystageB = [P.buf(f"ystage{i}") for i in range(2)]
    xsem = [P.new_sem() for _ in range(4)]
    osem = [P.new_sem() for _ in range(4)]

    def load_x(src, row0):
        def f(t):
            P.dma("sp", lambda e: e.dma_start(out=xres[t][:], in_=src[row0 + t * 128: row0 + (t + 1) * 128, :]),
                  xsem[t], writes=[xresB[t]])
        return f

    def proj_tm(c0, ncols, dst_fn, dstB_fn):
        w, wB = wload_std(wi_s, wi_name(c0), 0, 8, c0, ncols)
        for t in range(4):
            bt, bb = nb()
            P.op("pe", [(lambda e, kc=kc, t=t, bt=bt: e.matmul(bt[:, 0:ncols], lhsT=hT[:, kc, t * 128:(t + 1) * 128],
                                                               rhs=w[:, kc, :], start=(kc == 0), stop=(kc == 7)))
                        for kc in range(8)], reads=[wB, hTB], writes=[bb])
            evac_copy(dst_fn(t), bt[:, 0:ncols], [bb], [dstB_fn(t)])

    def proj_kv_a(src=None):
        hTc, hTBc = src if src is not None else (hT, [hTB])
        w, wB = wload_std(wi_s, "wiB", 0, 8, C_KA, 256)
        bank = nb()
        fm_mm(bank, w, wB, 0, hTc, hTBc, 8)
        bt, bb = bank
        P.op("act", [lambda e: e.activation(out=ka_pad[0][0:64, 128:640], in_=bt[0:64, :], func=AF.Copy),
                     lambda e: e.activation(out=ka_pad[1][64:128, 128:640], in_=bt[64:128, :], func=AF.Copy)],
             reads=[bb], writes=kaB[1:5])
        for t in range(4):
            bt2, bb2 = nb()
            P.op("pe", [(lambda e, kc=kc, t=t, bt2=bt2: e.matmul(bt2[:, 0:128], lhsT=hTc[:, kc, t * 128:(t + 1) * 128],
                                                                 rhs=w[:, kc, 128:256], start=(kc == 0),
                                                                 stop=(kc == 7))) for kc in range(8)],
                 reads=[wB] + hTBc, writes=[bb2])
            P.op("act", [lambda e, t=t, bt2=bt2: e.activation(out=va_pad[:, t + 1, 0, 0:64], in_=bt2[:, 0:64],
                                                              func=AF.Copy),
                         lambda e, t=t, bt2=bt2: e.activation(out=va_pad[:, t + 1, 1, 64:128], in_=bt2[:, 64:128],
                                                              func=AF.Copy)],
                 reads=[bb2], writes=[vaB[t + 1]])

    def carry_kv():
        P.op("pool", [lambda e: e.tensor_copy(out=ka_pad[0][:, 0:128], in_=ka_pad[0][:, 512:640]),
                      lambda e: e.tensor_copy(out=ka_pad[1][:, 0:128], in_=ka_pad[1][:, 512:640]),
                      lambda e: e.tensor_copy(out=va_pad[:, 0], in_=va_pad[:, 4])],
             reads=[kaB[4], vaB[4]], writes=[kaB[0], vaB[0]])

    def proj_gla_tm():
        for i in range(2):
            proj_tm(C_KG + i * 256, 256, lambda t, i=i: kgTM[t][:, i * 256:(i + 1) * 256], lambda t: kgTMB[t])
        for i in range(4):
            proj_tm(C_VG + i * 256, 256, lambda t, i=i: vgTM[t][:, i * 256:(i + 1) * 256], lambda t: vgTMB[t])

    def proj_gk(src=None):
        hTc, hTBc = src if src is not None else (hT, [hTB])
        w, wB = wload_std(wi_s, "wiA", 0, 8, C_GK, 16)
        bt, bb = nb()
        P.op("pe", [(lambda e, kc=kc: e.matmul(bt[0:16, :], lhsT=w[:, kc, 0:16], rhs=hTc[:, kc, :], start=(kc == 0),
                                               stop=(kc == 7))) for kc in range(8)], reads=[wB] + hTBc, writes=[bb])
        P.op("act", lambda e: e.activation(out=gkT[0:16, :], in_=bt[0:16, :], func=AF.Copy), reads=[bb],
             writes=[gkTB])

    def prefix_proj_gen(src):
        hTc, hTBc = src
        proj_gk(src)
        yield
        groups = []
        for i in range(2):
            groups.append((wload_std(wi_s, "wiA", 0, 8, C_KG + i * 256, 256), 0, i))
        for i in range(4):
            groups.append((wload_std(wi_s, "wiA", 0, 8, C_VG + i * 256, 256), 1, i))
        for t in range(4):
            for (w, wB), kind, i in groups:
                bt, bb = nb(PPROJ)
                P.op("pe", [(lambda e, kc=kc, t=t, bt=bt, w=w: e.matmul(bt[:, 0:256],
                                                                        lhsT=hTc[:, kc, t * 128:(t + 1) * 128],
                                                                        rhs=w[:, kc, :], start=(kc == 0),
                                                                        stop=(kc == 7))) for kc in range(8)],
                     reads=[wB] + hTBc, writes=[bb])
                if kind == 0:
                    evac_copy(kgTM[t][:, i * 256:(i + 1) * 256], bt[:, 0:256], [bb], [kgTMB[t]])
                else:
                    evac_copy(vgTM[t][:, i * 256:(i + 1) * 256], bt[:, 0:256], [bb], [vgTMB[t]])
            yield

    def og_gen():
        for i in range(4):
            w, wB = wload_std(wi_s, "wiC", 0, 8, C_OG + i * 256, 256)
            for cc in range(2):
                idx = i * 2 + cc
                bank = nb()
                fm_mm(bank, w, wB, cc * 128, hT, [hTB], 8)
                bt, bb = bank
                f = idx % 2
                P.op("act", lambda e, bt=bt, f=f: e.activation(out=ftt[f][:], in_=bt[:, :], func=AF.Exp, scale=-1.0),
                     reads=[bb], writes=[fttB[f]])
                P.op("act", lambda e, f=f: e.activation(out=ftt[f][:], in_=ftt[f][:], func=AF.Ln, bias=1.0),
                     reads=[fttB[f]], writes=[fttB[f]])
                P.op("act", lambda e, f=f: e.activation(out=ftt[f][:], in_=ftt[f][:], func=AF.Exp, scale=-1.0),
                     reads=[fttB[f]], writes=[fttB[f]])
                P.op("dve", lambda e, bt=bt, f=f, idx=idx: e.scalar_tensor_tensor(
                    out=slab[:, S_GSIL + idx, :], in0=bt[:, :], scalar=gnh[:, idx % 2:idx % 2 + 1], in1=ftt[f][:],
                    op0=ALU.mult, op1=ALU.mult), reads=[bb, gnhB, fttB[f]], writes=[slabB[S_GSIL + idx]])
                yield

    state_par = [0]

    def gla_tile(t, full):
        tok = slice(t * 128, (t + 1) * 128)
        par = t % 2
        PGLA = PGLA2[par]
        sp_t, spB, ED, EDB = sp_t2[par], spB2[par], ED2[par], EDB2[par]
        Epos, EposB, Eneg, EnegB = Epos2[par], EposB2[par], Eneg2[par], EnegB2[par]
        qt, qtB, kt, ktB, kd, kdB = qt2[par], qtB2[par], kt2[par], ktB2[par], kd2[par], kdB2[par]
        el, elB = el2[par], elB2[par]
        zt, zb = nb(PGLA)
        P.op("pe", lambda e: e.matmul(zt[:, :], lhsT=gkT[0:17, tok], rhs=wgk[0:17, :], start=True, stop=True),
             reads=[gkTB, wgkB], writes=[zb])
        P.op("act", lambda e: e.activation(out=sp_t[:], in_=zt[:, :], func=AF.Exp, scale=-1.0), reads=[zb],
             writes=[spB])
        P.op("act", lambda e: e.activation(out=sp_t[:], in_=sp_t[:], func=AF.Ln, bias=1.0), reads=[spB],
             writes=[spB])
        yield
        dt_, db = nb(PGLA)
        P.op("pe", lambda e: e.matmul(dt_[:, :], lhsT=SU2m, rhs=sp_t[:], start=True, stop=True),
             reads=[cstB, spB], writes=[db])
        P.op("act", lambda e: e.activation(out=ED[:], in_=dt_[:, :], func=AF.Exp), reads=[db], writes=[EDB])
        if not full:
            et, eb = nb(PGLA)
            P.op("pe", [(lambda e, h=h: e.matmul(et[:, h * 2:(h + 1) * 2], lhsT=sp_t[:, h * 128:(h + 1) * 128],
                                                 rhs=ind2m, start=True, stop=True)) for h in range(4)],
                 reads=[cstB, spB], writes=[eb])
            P.op("act", lambda e: e.activation(out=el[:], in_=et[:, 0:8], func=AF.Exp), reads=[eb], writes=[elB])
        if full:
            gt, gb = nb(PGLA)
            P.op("pe", [(lambda e, h=h: e.matmul(gt[:, h * 128:(h + 1) * 128], lhsT=sp_t[:, h * 128:(h + 1) * 128],
                                                 rhs=U2m, start=True, stop=True)) for h in range(4)],
                 reads=[cstB, spB], writes=[gb])
            P.op("act", lambda e: e.activation(out=Epos[:], in_=gt[:, :], func=AF.Exp), reads=[gb], writes=[EposB])
            P.op("act", lambda e: e.activation(out=Eneg[:], in_=gt[:, :], func=AF.Exp, scale=-1.0), reads=[gb],
                 writes=[EnegB])
            yield
            P.op("dve", lambda e: e.scalar_tensor_tensor(
                out=qt[:], in0=slab[:, S_QG:S_QG + 4, tok], scalar=128.0 ** -0.5,
                in1=Epos[:].rearrange("p (h c) -> p h c", h=4), op0=ALU.mult, op1=ALU.mult),
                reads=slabB[S_QG:S_QG + 4] + [EposB], writes=[qtB])
            P.op("dve", lambda e: e.tensor_tensor(
                out=kt[:], in0=slab[:, S_KG:S_KG + 4, tok], in1=Eneg[:].rearrange("p (h c) -> p h c", h=4),
                op=ALU.mult), reads=slabB[S_KG:S_KG + 4] + [EnegB], writes=[ktB])
        P.op("pool", lambda e: e.tensor_tensor(out=kd[0][0:64, :], in0=kgTM[t][0:64, :], in1=ED[0:64, :], op=ALU.mult),
             reads=[kgTMB[t], EDB], writes=[kdB[0]])
        P.op("pool", lambda e: e.tensor_tensor(out=kd[1][64:128, :], in0=kgTM[t][64:128, :], in1=ED[64:128, :],
                                               op=ALU.mult), reads=[kgTMB[t], EDB], writes=[kdB[1]])
        if not full and pending_casts:
            pending_casts.pop(0)()
        yield
        if full:
            at, ab = nb(PGLA)
            P.op("pe", [(lambda e, h=h: e.matmul(at[:, h * 128:(h + 1) * 128], lhsT=kt[:, h, :], rhs=qt[:, h, :],
                                                 start=True, stop=True)) for h in range(4)],
                 reads=[ktB, qtB], writes=[ab])
            P.op("dve", lambda e: e.tensor_tensor(out=ATs[:], in0=at[:, :].rearrange("p (h c) -> p h c", h=4),
                                                  in1=maskA.unsqueeze(1).broadcast_to([128, 4, 128]), op=ALU.mult),
                 reads=[ab, cstB], writes=[ATsB])
        sps = [None, None]

        def sps_mm(c):
            pair = []
            for hp in range(2):
                st, sbb = nb(PGLA)
                P.op("pe", [(lambda e, hh=hh, hp=hp, c=c, st=st: e.matmul(
                    st[:, hh * 256:(hh + 1) * 256], lhsT=kd[c][:, (hp * 2 + hh) * 128:(hp * 2 + hh + 1) * 128],
                    rhs=vgTM[t][:, (hp * 2 + hh) * 256:(hp * 2 + hh + 1) * 256], start=True, stop=True))
                    for hh in range(2)], reads=[kdB[c], vgTMB[t]], writes=[sbb])
                pair.append((st, sbb))
            sps[c] = pair

        def upd(c):
            for h in range(4):
                st, sbb = sps[c][h // 2]
                if full:
                    ci = h * 128 + c * 64 + 63
                    sc, scB = Epos[:, ci:ci + 1], EposB
                else:
                    sc, scB = el[:, h * 2 + c:h * 2 + c + 1], elB
                P.op("dve", lambda e, h=h, st=st, c=c, sc=sc: e.scalar_tensor_tensor(
                    out=S_st[:, h * 256:(h + 1) * 256], in0=S_st[:, h * 256:(h + 1) * 256],
                    scalar=sc, in1=st[:, (h % 2) * 256:(h % 2 + 1) * 256],
                    op0=ALU.mult, op1=ALU.add), reads=[SB_, scB, sbb], writes=[SB_])

        pa = state_par[0]
        if not full:
            sps_mm(0)
            upd(0)
            sps_mm(1)
            upd(1)
            yield
            return
        sps_mm(0)
        upd(0)
        P.op("act", lambda e: e.activation(out=Sbf[1 - pa][:], in_=S_st[:], func=AF.Copy), reads=[SB_],
             writes=[SbfB[1 - pa]])
        yield
        sps_mm(1)
        upd(1)
        obanks = [nb(PGLA), nb(PGLA)]
        for idx in range(8):
            h, dvc = idx // 2, idx % 2
            ot, ob = obanks[idx // 4]
            oc = (idx % 4) * 128
            col = h * 256 + dvc * 128
            P.op("pe", [
                lambda e, ot=ot, oc=oc, col=col, h=h: e.matmul(ot[:, oc:oc + 128], lhsT=vgTM[t][:, col:col + 128],
                                                               rhs=ATs[:, h, :], start=True, stop=False),
                lambda e, ot=ot, oc=oc, col=col, h=h: e.matmul(ot[:, oc:oc + 64], lhsT=Sbf[pa][:, col:col + 128],
                                                               rhs=qt[:, h, 0:64], start=False, stop=False),
                lambda e, ot=ot, oc=oc, col=col, h=h: e.matmul(ot[:, oc + 64:oc + 128],
                                                               lhsT=Sbf[1 - pa][:, col:col + 128],
                                                               rhs=qt[:, h, 64:128], start=False, stop=True)],
                reads=[vgTMB[t], ATsB, SbfB[0], SbfB[1], qtB], writes=[ob])
        P.op("act", lambda e: e.activation(out=Sbf[pa][:], in_=S_st[:], func=AF.Copy), reads=[SB_],
             writes=[SbfB[pa]])
        for k in range(2):
            ot, ob = obanks[k]
            P.op("act", lambda e, ot=ot, k=k: e.activation(out=sq[:, k * 4:(k + 1) * 4, :],
                                                           in_=ot[:, :].rearrange("p (i c) -> p i c", i=4),
                                                           func=AF.Square), reads=[ob], writes=[sqB])
        yield
        st_, sb_ = nb(PGLA)
        sqv = sq[:].rearrange("p (h v) c -> p h v c", v=2)
        P.op("pe", [(lambda e, v=v: e.matmul(st_[:, :], lhsT=ones_bf[:], rhs=sqv[:, :, v, :], start=(v == 0),
                                             stop=(v == 1))) for v in range(2)], reads=[sqB, onesB], writes=[sb_])
        P.op("act", lambda e: e.activation(out=lnv[:], in_=st_[:, :], func=AF.Ln, scale=1.0 / 256, bias=EPS),
             reads=[sb_], writes=[lnvB])
        P.op("act", lambda e: e.activation(out=rstd_g[:].rearrange("p h c -> p (h c)"), in_=lnv[:], func=AF.Exp,
                                           scale=-0.5), reads=[lnvB], writes=[rstdgB])
        for k in range(2):
            ot, ob = obanks[k]
            P.op("dve", lambda e, ot=ot, k=k: e.tensor_tensor(
                out=otmp[:, k * 4:(k + 1) * 4, :].rearrange("p (h v) c -> p h v c", v=2),
                in0=ot[:, :].rearrange("p (h v c) -> p h v c", h=2, v=2),
                in1=rstd_g[:, k * 2:(k + 1) * 2, :].unsqueeze(2).broadcast_to([128, 2, 2, 128]), op=ALU.mult),
                reads=[ob, rstdgB], writes=[otmpB])
        P.op("pool", lambda e: e.tensor_tensor(out=slab[:, S_OGN:S_OGN + 8, tok], in0=otmp[:],
                                               in1=slab[:, S_GSIL:S_GSIL + 8, tok], op=ALU.mult),
             reads=[otmpB] + slabB[S_GSIL:S_GSIL + 8], writes=slabB[S_OGN:S_OGN + 8])
        yield

    def attn_tile(t, first):
        tok = slice(t * 128, (t + 1) * 128)
        k = 0
        for g in range(2):
            for kb in range(2):
                bt, bb = nb(PATT)
                s_ = t + kb
                P.op("pe", lambda e, bt=bt, g=g, s_=s_: e.matmul(bt[:, :], lhsT=ka_pad[g][:, s_ * 128:(s_ + 1) * 128],
                                                                 rhs=slab[:, S_QA:S_QA + 4, tok], start=True,
                                                                 stop=True),
                     reads=[kaB[s_]] + slabB[S_QA:S_QA + 4], writes=[bb])
                i = k % 2
                P.op("dve", lambda e, bt=bt, g=g, kb=kb, i=i: e.tensor_tensor(
                    out=sT[i][:], in0=bt[:, :], in1=biasT8[:, kb, g * 512:(g + 1) * 512], op=ALU.add),
                    reads=[bb, biasB], writes=[sTB[i]])
                if first and kb == 0:
                    P.op("dve", lambda e, i=i: e.tensor_tensor(
                        out=sT[i][:].rearrange("p (j q) -> p j q", j=4), in0=sT[i][:].rearrange("p (j q) -> p j q", j=4),
                        in1=mask0[:].unsqueeze(1).broadcast_to([128, 4, 128]), op=ALU.add),
                        reads=[sTB[i], mask0B], writes=[sTB[i]])
                P.op("act", lambda e, i=i, k=k: e.activation(out=esT[:, k, :], in_=sT[i][:], func=AF.Exp, scale=0.125),
                     reads=[sTB[i]], writes=[esTB[k]])
                k += 1
                if k == 2:
                    yield
        yield
        ot, ob = nb(PATT)
        dt_, db = nb(PATT)
        fo, fd = [], []
        for k, (g, kb) in enumerate([(0, 0), (0, 1), (1, 0), (1, 1)]):
            s_ = t + kb
            fo.append(lambda e, g=g, s_=s_, k=k: e.matmul(ot[:, :], lhsT=va_pad[:, s_, g, :], rhs=esT[:, k, :],
                                                          start=(k == 0), stop=(k == 3)))
            fd.append(lambda e, g=g, k=k: e.matmul(dt_[:, :], lhsT=ones_pad[:, g, :], rhs=esT[:, k, :],
                                                   start=(k == 0), stop=(k == 3)))
        P.op("pe", fd, reads=[onespB] + esTB, writes=[db])
        P.op("pe", fo, reads=[vaB[t], vaB[t + 1]] + esTB, writes=[ob])
        P.op("dve", lambda e: e.tensor_tensor(out=dtmp[:].rearrange("p (j q) -> p j q", j=4),
                                              in0=dt_[:, :].rearrange("p (j q) -> p j q", j=4),
                                              in1=esink[:].unsqueeze(2).broadcast_to([128, 4, 128]), op=ALU.add),
             reads=[db, esinkB], writes=[dtmpB])
        P.op("act", lambda e: e.activation(out=dtmp[:], in_=dtmp[:], func=AF.Ln), reads=[dtmpB], writes=[dtmpB])
        P.op("act", lambda e: e.activation(out=dtmp[:], in_=dtmp[:], func=AF.Exp, scale=-1.0), reads=[dtmpB],
             writes=[dtmpB])
        yield
        P.op("dve", lambda e: e.tensor_tensor(out=slab[:, S_OA:S_OA + 4, tok],
                                              in0=ot[:, :].rearrange("p (j q) -> p j q", j=4),
                                              in1=dtmp[:].rearrange("p (j q) -> p j q", j=4), op=ALU.mult),
             reads=[ob, dtmpB], writes=slabB[S_OA:S_OA + 4])
        yield

    def run_merged(gens):
        gens = list(gens)
        while gens:
            for g in list(gens):
                try:
                    next(g)
                except StopIteration:
                    gens.remove(g)

    def run_weighted(pairs):
        pairs = [[g, w] for g, w in pairs]
        while pairs:
            for pr in list(pairs):
                for _ in range(pr[1]):
                    try:
                        next(pr[0])
                    except StopIteration:
                        pairs.remove(pr)
                        break

    def run_gla_region(full, extra, delay):
        ge = chain(gla_tile(t, full) for t in (0, 2))
        go = chain(gla_tile(t, full) for t in (1, 3))
        gens = [[ge, 0], [go, delay]] + [[g, 0] for g in extra]
        step = 0
        while gens:
            for pr in list(gens):
                if step < pr[1]:
                    continue
                try:
                    next(pr[0])
                except StopIteration:
                    gens.remove(pr)
            step += 1

    def chain(gs):
        for g in gs:
            yield from g

    NPB = NTP // 4
    hT_alt = (slab[:, 0:8, :], slabB[0:8])

    def hsel(bp):
        return hT_alt if (NPB - 1 - bp) % 2 == 0 else (hT, [hTB])

    mark('P0.norm')
    norm_to_hT(0, load_x(xp_d, 0), hsel(0))
    for bp in range(NPB):
        last = bp == NPB - 1
        src = hsel(bp)
        mark(f'P{bp}.proj')
        ld = load_x(x_d, 0) if last else load_x(xp_d, (bp + 1) * 512)
        for t in range(4):
            ld(t)
        if last:
            proj_kv_a(src)
            carry_kv()
            nxt = norm_gen(0, None, None, PPROJ)
        else:
            nxt = norm_gen(0, None, hsel(bp + 1), PPROJ)
        pg = prefix_proj_gen(src)
        next(pg)
        next(pg)
        next(pg)
        mark(f'P{bp}.gla')
        run_gla_region(False, [pg, nxt], 2)
    while pending_casts:
        pending_casts.pop(0)()
    P.op("act", lambda e: e.activation(out=Sbf[0][:], in_=S_st[:], func=AF.Copy), reads=[SB_], writes=[SbfB[0]])
    state_par[0] = 0

    psem = [P.new_sem() for _ in range(2)]
    for b in range(NT // 4):
        row0 = b * 512
        mark(f'M{b}.norm1')
        if b > 0:
            norm_to_hT(0, None)
        mark(f'M{b}.inproj')
        for j0 in (0, 2):
            srcs = []
            for g in range(2):
                for jj in range(2):
                    c0 = C_QA + g * 256 + (j0 + jj) * 64
                    src = wi_s[:, c0:c0 + 64].rearrange("(k p) c -> p k c", p=128)
                    srcs.append((lambda t, g=g, jj=jj: t[:, 0:2048].rearrange(
                        "p (k j g d) -> p k j g d", k=8, j=2, g=2)[:, :, jj, g, :], src))
            wt, wB = wload(srcs, "wiB", 8, 256)
            wv = wt[:, 0:2048].rearrange("p (k j c) -> p k j c", k=8, j=2)
            for jj in range(2):
                bt, bb = nb()
                P.op("pe", [(lambda e, kc=kc, jj=jj, bt=bt, wv=wv: e.matmul(bt[:, :], lhsT=wv[:, kc, jj, :],
                                                                            rhs=hT[:, kc, :], start=(kc == 0),
                                                                            stop=(kc == 7))) for kc in range(8)],
                     reads=[wB, hTB], writes=[bb])
                evac_copy(slab[:, S_QA + j0 + jj, :], bt[:, :], [bb], [slabB[S_QA + j0 + jj]])
        proj_kv_a()
        for (c_base, s_base) in ((C_QG, S_QG), (C_KG, S_KG)):
            for i in range(2):
                w, wB = wload_std(wi_s, wi_name(c_base + i * 256), 0, 8, c_base + i * 256, 256)
                for cc in range(2):
                    bank = nb()
                    fm_mm(bank, w, wB, cc * 128, hT, [hTB], 8)
                    evac_copy(slab[:, s_base + i * 2 + cc, :], bank[0][:, :], [bank[1]], [slabB[s_base + i * 2 + cc]])
        proj_gk()
        proj_gla_tm()
        mark(f'M{b}.attn')
        for _ in og_gen():
            pass
        run_gla_region(True, [chain(attn_tile(t, first=(b == 0 and t == 0)) for t in range(4))], 3)
        carry_kv()
        mark(f'M{b}.mixed')
        for qq in range(4):
            wga, wgaB = wload_std(wi_s, "wiC", 0, 8, C_GA + qq * 256, 256)
            wgg, wggB = wload_std(wi_s, "wiD", 0, 8, C_GG + qq * 256, 256)
            srcs = []
            for g in range(2):
                src = wpa_s[g * 256:(g + 1) * 256, qq * 256:(qq + 1) * 256].rearrange("(j p) c -> p j c", p=64)
                srcs.append((lambda t, g=g: t[g * 64:(g + 1) * 64, 0:1024].rearrange("p (j c) -> p j c", j=4), src))
            wpat, wpaB = wload(srcs, "wpa", 4, 256)
            wpa = wpat[:, 0:1024].rearrange("p (j c) -> p j c", j=4)
            wpg, wpgB = wload_std(wpg_s, "wpg", 0, 8, qq * 256, 256)
            for mm_ in range(2):
                m = qq * 2 + mm_
                bga, bgg, bya, byg = nb(), nb(), nb(), nb()
                fm_mm(bga, wga, wgaB, mm_ * 128, hT, [hTB], 8)
                P.op("act", lambda e, bt=bga[0]: e.activation(out=ta[:], in_=bt[:, :], func=AF.Tanh, scale=0.5),
                     reads=[bga[1]], writes=[taB])
                fm_mm(bgg, wgg, wggB, mm_ * 128, hT, [hTB], 8)
                P.op("act", lambda e, bt=bgg[0]: e.activation(out=tg[:], in_=bt[:, :], func=AF.Tanh, scale=0.5),
                     reads=[bgg[1]], writes=[tgB])
                fm_mm(bya, wpa, wpaB, mm_ * 128, slab[:, S_OA:S_OA + 4, :], slabB[S_OA:S_OA + 4], 4)
                fm_mm(byg, wpg, wpgB, mm_ * 128, slab[:, S_OGN:S_OGN + 8, :], slabB[S_OGN:S_OGN + 8], 8)
                P.op("dve", lambda e, bt=bya[0]: e.scalar_tensor_tensor(out=m1[:], in0=ta[:], scalar=1.0, in1=bt[:, :],
                                                                        op0=ALU.add, op1=ALU.mult),
                     reads=[taB, bya[1]], writes=[m1B])
                P.op("dve", lambda e, bt=byg[0]: e.scalar_tensor_tensor(out=m2[:], in0=tg[:], scalar=1.0, in1=bt[:, :],
                                                                        op0=ALU.add, op1=ALU.mult),
                     reads=[tgB, byg[1]], writes=[m2B])
                P.op("pool", lambda e, m=m: e.tensor_tensor(out=slab[:, S_MIX + m, :], in0=m1[:], in1=m2[:],
                                                            op=ALU.add),
                     reads=[m1B, m2B], writes=[slabB[S_MIX + m]])

        def proj_res(scr, name, nkc, act_view, actB, epi):
            for half in range(2):
                tb = [nb() for _ in range(4)]
                ngrp = (nkc + 3) // 4
                for kg in range(ngrp):
                    k0 = kg * 4
                    nk = min(4, nkc - k0)
                    w, wB = wload_std(scr, name, k0 * 128, nk, half * 512, 512)
                    for t in range(4):
                        bt, bb = tb[t]
                        P.op("pe", [(lambda e, kk=kk, t=t, bt=bt, w=w, k0=k0: e.matmul(
                            bt[:, :], lhsT=act_view[:, k0 + kk, t * 128:(t + 1) * 128], rhs=w[:, kk, :],
                            start=(k0 + kk == 0), stop=(k0 + kk == nkc - 1))) for kk in range(nk)],
                            reads=[wB] + list(actB), writes=[bb])
                for t in range(4):
                    epi(t, half, tb[t])

        def epi_half(t, half, bank):
            bt, bb = bank
            hs = slice(half * 512, (half + 1) * 512)
            P.op("dve", lambda e: e.scalar_tensor_tensor(out=xres[t][:, hs], in0=bt[:, :], scalar=0.5,
                                                         in1=xres[t][:, hs], op0=ALU.mult, op1=ALU.add),
                 reads=[bb, xresB[t]], writes=[xresB[t]])

        def epi_add(t, half, bank):
            bt, bb = bank
            hs = slice(half * 512, (half + 1) * 512)
            P.op("dve", lambda e: e.tensor_tensor(out=xres[t][:, hs], in0=bt[:, :], in1=xres[t][:, hs], op=ALU.add),
                 reads=[bb, xresB[t]], writes=[xresB[t]])

        mark(f'M{b}.wout')
        proj_res(wout_s, "wout", 8, slab[:, S_MIX:S_MIX + 8, :], slabB[S_MIX:S_MIX + 8], epi_half)

        mark(f'M{b}.norm2')
        norm_to_hT(1, None)
        mark(f'M{b}.ffnin')
        for ii in range(11):
            wg_, wgB_ = wload_std(wfi_s, "wfi", 0, 8, ii * 256, 256)
            wu_, wuB_ = wload_std(wfi_s, "wfi", 0, 8, D_FF + ii * 256, 256)
            for s in range(2):
                c = ii * 2 + s
                bg, bu = nb(), nb()
                fm_mm(bg, wg_, wgB_, s * 128, hT, [hTB], 8)
                fm_mm(bu, wu_, wuB_, s * 128, hT, [hTB], 8)
                f = c % 2
                P.op("act", lambda e, bt=bg[0], f=f: e.activation(out=ftt[f][:], in_=bt[:, :], func=AF.Tanh, scale=0.5),
                     reads=[bg[1]], writes=[fttB[f]])
                P.op("dve", lambda e, bt=bg[0], f=f: e.scalar_tensor_tensor(out=fa[f][:], in0=ftt[f][:], scalar=1.0,
                                                                           in1=bt[:, :], op0=ALU.add, op1=ALU.mult),
                     reads=[fttB[f], bg[1]], writes=[faB[f]])
                P.op("dve", lambda e, bt=bu[0], f=f, c=c: e.scalar_tensor_tensor(
                    out=slab[:, S_ACT + c, :], in0=fa[f][:], scalar=0.5, in1=bt[:, :], op0=ALU.mult, op1=ALU.mult),
                    reads=[faB[f], bu[1]], writes=[slabB[S_ACT + c]])
        mark(f'M{b}.ffnout')
        proj_res(wfo_s, "wfo", 22, slab[:, S_ACT:S_ACT + 22, :], slabB[S_ACT:S_ACT + 22], epi_add)

        mark(f'M{b}.norm3')
        norm_to_hT(2, None)
        mark(f'M{b}.ple')
        for t in range(4):
            i = t % 2
            P.dma("sp", lambda e, t=t, i=i, row0=row0: e.dma_start(out=p_f[i][:], in_=p_d[row0 + t * 128: row0 + (t + 1) * 128, :]),
                  psem[i], writes=[p_fB[i]])
            P.op("pool", lambda e, i=i: e.tensor_copy(out=p_bf[i][:], in_=p_f[i][:]), reads=[p_fB[i]],
                 writes=[p_bfB[i]])
            bt, bb = nb()
            btb = bt[:].bitcast(BF16)
            P.op("pe", [(lambda e, kc=kc, i=i, btb=btb: e.transpose(out=btb[:, kc * 128:(kc + 1) * 128],
                                                                    in_=p_bf[i][:, kc * 128:(kc + 1) * 128],
                                                                    identity=ident[:])) for kc in range(2)],
                 reads=[p_bfB[i], identB], writes=[bb])
            P.op("act", lambda e, t=t, btb=btb: e.activation(out=pT[:, :, t * 128:(t + 1) * 128],
                                                             in_=btb[:, 0:256].rearrange("p (k c) -> p k c", k=2),
                                                             func=AF.Copy), reads=[bb], writes=[pTB])

        def epi_gate(t, half, bank):
            bt, bb = bank
            P.op("act", lambda e: e.activation(out=esT[:, t, :], in_=bt[:, :], func=AF.Tanh, scale=0.5), reads=[bb],
                 writes=[tgtB[t]])

        def epi_ple(t, half, bank):
            bt, bb = bank
            hs = slice(half * 512, (half + 1) * 512)
            P.op("dve", lambda e: e.scalar_tensor_tensor(out=m1[:], in0=esT[:, t, :], scalar=1.0, in1=bt[:, :],
                                                         op0=ALU.add, op1=ALU.mult),
                 reads=[tgtB[t], bb], writes=[m1B])
            P.op("dve", lambda e: e.scalar_tensor_tensor(out=xres[t][:, hs], in0=m1[:], scalar=0.5, in1=xres[t][:, hs],
                                                         op0=ALU.mult, op1=ALU.add),
                 reads=[m1B, xresB[t]], writes=[xresB[t]])

        for half in range(2):
            tb = [nb() for _ in range(4)]
            for kg in range(2):
                w, wB = wload_std(wpgate_s, "wpgate", kg * 512, 4, half * 512, 512)
                for t in range(4):
                    bt, bb = tb[t]
                    P.op("pe", [(lambda e, kk=kk, t=t, bt=bt, w=w, kg=kg: e.matmul(
                        bt[:, :], lhsT=hT[:, kg * 4 + kk, t * 128:(t + 1) * 128], rhs=w[:, kk, :],
                        start=(kg * 4 + kk == 0), stop=(kg * 4 + kk == 7))) for kk in range(4)],
                        reads=[wB, hTB], writes=[bb])
            for t in range(4):
                epi_gate(t, half, tb[t])
            w, wB = wload_std(wple_s, "wple", 0, 2, half * 512, 512)
            for t in range(4):
                bt, bb = nb()
                P.op("pe", [(lambda e, kk=kk, t=t, bt=bt, w=w: e.matmul(bt[:, :], lhsT=pT[:, kk, t * 128:(t + 1) * 128],
                                                                        rhs=w[:, kk, :], start=(kk == 0),
                                                                        stop=(kk == 1))) for kk in range(2)],
                     reads=[wB, pTB], writes=[bb])
                epi_ple(t, half, (bt, bb))

        mark(f'M{b}.final')
        nxt_load = load_x(x_d, row0 + 512) if b + 1 < NT // 4 else None

        def out_dma(t):
            o = P.dma("sp", lambda e, t=t, row0=row0: e.dma_start(out=out_d[row0 + t * 128: row0 + (t + 1) * 128, :],
                                                              in_=ystage[t % 2][:]), osem[t % 2],
                      reads=[ystageB[t % 2]])
            if b == NT // 4 - 1:
                P.final.append(o)

        for t in range(4):
            P.op("act", lambda e, t=t: e.activation(out=junk_ap, in_=xres[t][:], func=AF.Square,
                                                   accum_out=stat[t][:, 0:1]),
                 reads=[xresB[t]], writes=[junkB, statB[t]])
            P.op("pool", lambda e, t=t: e.tensor_scalar(out=stat[t][:, 1:2], in0=stat[t][:, 0:1], scalar1=1.0 / D,
                                                       scalar2=EPS, op0=ALU.mult, op1=ALU.add),
                 reads=[statB[t]], writes=[statB[t]])
            P.op("pool", lambda e, t=t: e.tensor_tensor(out=stat[t][:, 2:3], in0=stat[t][:, 1:2], in1=mhalf[:, 0:1],
                                                       op=ALU.pow), reads=[statB[t], mhalfB], writes=[statB[t]])
            P.op("dve", lambda e, t=t: e.scalar_tensor_tensor(out=ystage[t % 2][:], in0=xres[t][:],
                                                             scalar=stat[t][:, 2:3], in1=gfin[:], op0=ALU.mult,
                                                             op1=ALU.mult),
                 reads=[xresB[t], statB[t], gfinB], writes=[ystageB[t % 2]])
            if nxt_load is not None:
                nxt_load(t)
            if t >= 1:
                out_dma(t - 1)
        out_dma(3)

    mark('end')
    nc._marks = marks
    P.emit()
    return nc


def _t5_bucket(dist):
    max_exact = 16
    d_f = np.maximum(dist, max_exact).astype(np.float32)
    large = max_exact + (np.log(d_f / max_exact) / math.log(128 / max_exact) * (32 - max_exact)).astype(np.int32)
    large = np.minimum(large, 31)
    return np.where(dist < max_exact, dist, large)


def _consts():
    s = np.arange(128)[:, None]
    t = np.arange(128)[None, :]
    same = (s // 64) == (t // 64)
    cst = np.zeros((128, 386), np.float32)
    cst[:, 0:128] = np.where(same & (s <= t), -1.0 / 16, 0.0)
    cst[:, 128:256] = np.where(same & (s > t), -1.0 / 16, 0.0)
    cst[:, 256] = np.where(np.arange(128) < 64, -1.0 / 16, 0.0)
    cst[:, 257] = np.where(np.arange(128) >= 64, -1.0 / 16, 0.0)
    cst[:, 258:386] = np.where(same & (s <= t), 1.0, 0.0)
    k = np.arange(128)[:, None]
    q = np.arange(128)[None, :]
    maskT = np.zeros((2, 128, 128), np.float32)
    maskT[0] = np.where(k > q, 0.0, NEG)
    maskT[1] = np.where(k <= q, 0.0, NEG)
    bucket = np.zeros((2, 128, 128), np.int64)
    for kb in range(2):
        dist = q + 128 - (kb * 128 + k)
        bucket[kb] = _t5_bucket(np.maximum(dist, 0))
    return cst, maskT, bucket


def _host_inputs(x, p, w_in, w_gk2, b_gk, sinks, rel_table, w_proj_attn, w_proj_gla, gla_norm, w_out, norm_mix,
                 norm_ffn, w_ffn_in, w_ffn_out, norm_ple, w_ple_gate, w_ple, norm_final, T):
    f = lambda a: np.ascontiguousarray(np.asarray(a, dtype=np.float32))
    x, p = f(x), f(p)
    B, S, _ = x.shape
    halves = S // T
    cst, maskT, bucket = _consts()
    rel = f(rel_table)
    biasT = np.ascontiguousarray(np.transpose(rel[bucket], (0, 1, 3, 2))).reshape(2, 128, 1024)
    sk = f(sinks)[0]
    sink_l = np.zeros((128, 4), np.float32)
    sink_l[0:64, :] = sk[0:4][None, :]
    sink_l[64:128, :] = sk[4:8][None, :]
    gcols = np.concatenate([f(norm_mix)[0].reshape(8, 128).T, f(norm_ffn)[0].reshape(8, 128).T,
                            f(norm_ple)[0].reshape(8, 128).T], axis=1)
    gn_col = np.ascontiguousarray(f(gla_norm)[0].reshape(2, 128).T)
    wgk_aug = np.concatenate([f(w_gk2)[0], f(b_gk)[0][None, :]], axis=0)
    shared = {
        "w_in": f(w_in)[0], "w_proj_attn": f(w_proj_attn)[0], "w_proj_gla": f(w_proj_gla)[0], "w_out": f(w_out)[0],
        "w_ffn_in": f(w_ffn_in)[0], "w_ffn_out": f(w_ffn_out)[0], "w_ple_gate": f(w_ple_gate)[0],
        "w_ple": f(w_ple)[0], "wgk_aug": np.ascontiguousarray(wgk_aug), "gcols": np.ascontiguousarray(gcols),
        "gn_col": gn_col, "norm_final": f(norm_final), "sink_l": sink_l, "biasT": biasT, "maskT": maskT, "cst": cst,
    }
    maps = []
    for b in range(B):
        for h in range(halves):
            m = dict(shared)
            m["x"] = np.ascontiguousarray(x[b, h * T:(h + 1) * T])
            m["p"] = np.ascontiguousarray(p[0, b, h * T:(h + 1) * T])
            if h == 0:
                m["xp"] = np.zeros((T, D), np.float32)
                m["mask0"] = np.full((128, 128), NEG, np.float32)
            else:
                m["xp"] = np.ascontiguousarray(x[b, (h - 1) * T:h * T])
                m["mask0"] = np.zeros((128, 128), np.float32)
            maps.append(m)
    return maps, B, halves


def run(inputs, T):
    maps, B, halves = _host_inputs(T=T, **inputs)
    nc = build_nc(T // 128, T // 128)
    n = len(maps)
    res = run_bass_kernel_spmd(nc, maps, core_ids=list(range(n)))
    out = np.zeros((B, halves * T, D), np.float32)
    for i, r in enumerate(res.results):
        b, h = i // halves, i % halves
        out[b, h * T:(h + 1) * T] = r["out"]
    return out


def kernel(**inputs):
    return run(inputs, 4096)
```

```python
import math
import numpy as np
import concourse.bass as bass
import concourse.mybir as mybir
from concourse.bass_utils import run_bass_kernel_spmd

F32 = mybir.dt.float32
BF16 = mybir.dt.bfloat16
AF = mybir.ActivationFunctionType
ALU = mybir.AluOpType

D = 1024
D_IN = 5904
D_FF = 2816
EPS = 1e-6
NEG = -30000.0
C_QA, C_KA, C_VA, C_QG, C_KG, C_VG, C_GK, C_OG, C_GA, C_GG = 0, 512, 640, 768, 1280, 1792, 2816, 2832, 3856, 4880


class Buf:
    __slots__ = ("name", "last_w", "readers")

    def __init__(self, name):
        self.name = name
        self.last_w = None
        self.readers = []


class Op:
    __slots__ = ("eng", "fns", "deps", "sem", "val", "signal", "is_dma")

    def __init__(self, eng, fns):
        self.eng = eng
        self.fns = fns
        self.deps = []
        self.sem = None
        self.val = 0
        self.signal = False
        self.is_dma = False


class Prog:
    ENGS = ("pe", "act", "dve", "pool", "sp")

    def __init__(self, nc):
        self.nc = nc
        self.ops = {e: [] for e in self.ENGS}
        self.esem = {e: nc.alloc_semaphore("es_" + e) for e in self.ENGS}
        self.dma_cnt = {}
        self.nsem = 0
        self.final = []

    def buf(self, name):
        return Buf(name)

    def new_sem(self):
        self.nsem += 1
        return self.nc.alloc_semaphore(f"ds_{self.nsem}")

    def _track(self, op, reads, writes):
        deps = []
        for b in reads:
            if b.last_w is not None:
                deps.append(b.last_w)
        for b in writes:
            if b.last_w is not None:
                deps.append(b.last_w)
            deps.extend(b.readers)
        for b in reads:
            b.readers.append(op)
        for b in writes:
            b.last_w = op
            b.readers = []
        seen = set()
        for d in deps:
            if d is op or id(d) in seen:
                continue
            seen.add(id(d))
            if d.eng == "pe" and op.eng == "pe" and not d.is_dma and not op.is_dma:
                continue
            op.deps.append(d)
            d.signal = True

    def op(self, eng, fns, reads=(), writes=()):
        if callable(fns):
            fns = [fns]
        o = Op(eng, list(fns))
        self._track(o, reads, writes)
        self.ops[eng].append(o)
        return o

    def dma(self, eng, fns, sem, reads=(), writes=()):
        if callable(fns):
            fns = [fns]
        o = Op(eng, list(fns))
        o.is_dma = True
        o.sem = sem
        c = self.dma_cnt.get(id(sem), 0) + 16 * len(o.fns)
        self.dma_cnt[id(sem)] = c
        o.val = c
        self._track(o, reads, writes)
        self.ops[eng].append(o)
        return o

    def emit(self):
        nc = self.nc
        for e in self.ENGS:
            c = 0
            for o in self.ops[e]:
                if o.is_dma:
                    continue
                if o.signal:
                    c += 1
                    o.sem = self.esem[e]
                    o.val = c
        handles = {"pe": "tensor", "act": "scalar", "dve": "vector", "pool": "gpsimd", "sp": "sync"}
        final = self.final
        with nc.Block() as block:
            for e in self.ENGS:
                ops = self.ops[e]

                def body(eng, ops=ops, e=e):
                    waited = {}
                    for o in ops:
                        for d in o.deps:
                            k = id(d.sem)
                            if waited.get(k, 0) >= d.val:
                                continue
                            eng.wait_ge(d.sem, d.val)
                            waited[k] = d.val
                        n = len(o.fns)
                        for i, f in enumerate(o.fns):
                            ins = f(eng)
                            if o.is_dma:
                                ins.then_inc(o.sem, 16)
                            elif o.signal and i == n - 1:
                                ins.then_inc(o.sem, 1)
                    if e == "sp":
                        for d in final:
                            eng.wait_ge(d.sem, d.val)

                getattr(block, handles[e])(body)


def build_nc(NT, NTP, dbg=None):
    assert NT % 4 == 0 and NTP % 4 == 0
    nc = bass.Bass("TRN2", target_bir_lowering=False)
    P = Prog(nc)
    T, TP = NT * 128, NTP * 128

    def din(name, shape, dt=F32):
        return nc.dram_tensor(name, list(shape), dt, kind="ExternalInput").ap()

    x_d = din("x", [T, D])
    xp_d = din("xp", [TP, D])
    p_d = din("p", [T, 256])
    w_in_d = din("w_in", [D, D_IN])
    wpa_d = din("w_proj_attn", [512, D])
    wpg_d = din("w_proj_gla", [D, D])
    wout_d = din("w_out", [D, D])
    wfi_d = din("w_ffn_in", [D, 2 * D_FF])
    wfo_d = din("w_ffn_out", [D_FF, D])
    wpgate_d = din("w_ple_gate", [D, D])
    wple_d = din("w_ple", [256, D])
    wgk_d = din("wgk_aug", [17, 512])
    gcols_d = din("gcols", [128, 24])
    gn_d = din("gn_col", [128, 2])
    gfin_d = din("norm_final", [D])
    sinkl_d = din("sink_l", [128, 4])
    biasT_d = din("biasT", [2, 128, 1024])
    maskT_d = din("maskT", [2, 128, 128])
    mask0_d = din("mask0", [128, 128])
    cst_d = din("cst", [128, 386])
    out_d = nc.dram_tensor("out", [T, D], F32, kind="ExternalOutput").ap()

    def dscr(name, shape):
        return nc.dram_tensor(name, list(shape), BF16, kind="Internal").ap()

    wi_s = dscr("wi_s", [D, D_IN])
    wpa_s = dscr("wpa_s", [512, D])
    wpg_s = dscr("wpg_s", [D, D])
    wout_s = dscr("wout_s", [D, D])
    wfi_s = dscr("wfi_s", [D, 2 * D_FF])
    wfo_s = dscr("wfo_s", [D_FF, D])
    wpgate_s = dscr("wpgate_s", [D, D])
    wple_s = dscr("wple_s", [256, D])

    def sb(name, shape, dt=F32):
        return nc.alloc_sbuf_tensor("s_" + name, list(shape), dt)

    xres = [sb(f"xres{t}", [128, D]) for t in range(4)]
    xresB = [P.buf(f"xres{t}") for t in range(4)]
    xn = [sb(f"xn{i}", [128, D], BF16) for i in range(2)]
    xnB = [P.buf(f"xn{i}") for i in range(2)]
    stat = [sb(f"stat{t}", [128, 4]) for t in range(4)]
    statB = [P.buf(f"stat{t}") for t in range(4)]
    hT = sb("hT", [128, 8, 512], BF16)
    hTB = P.buf("hT")
    slab = sb("slab", [128, 40, 512], BF16)
    slabB = [P.buf(f"slab{i}") for i in range(40)]
    S_ACT, S_GSIL, S_OGN, S_MIX, S_QA, S_QG, S_KG, S_OA = 0, 0, 8, 16, 24, 28, 32, 36
    ka_pad = [sb(f"ka_pad{g}", [128, 640], BF16) for g in range(2)]
    kaB = [P.buf(f"ka_s{s}") for s in range(5)]
    va_pad = sb("va_pad", [128, 5, 2, 128], BF16)
    vaB = [P.buf(f"va_s{s}") for s in range(5)]
    kgTM = [sb(f"kgTM{t}", [128, 512], BF16) for t in range(4)]
    kgTMB = [P.buf(f"kgTM{t}") for t in range(4)]
    vgTM = [sb(f"vgTM{t}", [128, 1024], BF16) for t in range(4)]
    vgTMB = [P.buf(f"vgTM{t}") for t in range(4)]
    gkT = sb("gkT", [32, 512])
    gkTB = P.buf("gkT")
    sp_t2 = [sb(f"sp_t{i}", [128, 512]) for i in range(2)]; spB2 = [P.buf(f"sp{i}") for i in range(2)]
    ED2 = [sb(f"ED{i}", [128, 512]) for i in range(2)]; EDB2 = [P.buf(f"ED{i}") for i in range(2)]
    Epos2 = [sb(f"Epos{i}", [128, 512]) for i in range(2)]; EposB2 = [P.buf(f"Epos{i}") for i in range(2)]
    Eneg2 = [sb(f"Eneg{i}", [128, 512]) for i in range(2)]; EnegB2 = [P.buf(f"Eneg{i}") for i in range(2)]
    el2 = [sb(f"el{i}", [128, 8]) for i in range(2)]; elB2 = [P.buf(f"el{i}") for i in range(2)]
    qt2 = [sb(f"qt{i}", [128, 4, 128], BF16) for i in range(2)]; qtB2 = [P.buf(f"qt{i}") for i in range(2)]
    kt2 = [sb(f"kt{i}", [128, 4, 128], BF16) for i in range(2)]; ktB2 = [P.buf(f"kt{i}") for i in range(2)]
    kd2 = [[sb(f"kd{i}_{c}", [128, 512], BF16) for c in range(2)] for i in range(2)]
    kdB2 = [[P.buf(f"kd{i}_{c}") for c in range(2)] for i in range(2)]
    ATs = sb("ATs", [128, 4, 128], BF16); ATsB = P.buf("ATs")
    sq = sb("sq", [128, 8, 128], BF16); sqB = P.buf("sq")
    junk_ap = sq[:].rearrange("p a b -> p (a b)")
    junkB = sqB
    rstd_g = sb("rstd_g", [128, 4, 128]); rstdgB = P.buf("rstd_g")
    otmp = sb("otmp", [128, 8, 128]); otmpB = P.buf("otmp")
    S_st = sb("S_st", [128, 1024]); SB_ = P.buf("S_st")
    Sbf = [sb(f"Sbf{i}", [128, 1024], BF16) for i in range(2)]
    SbfB = [P.buf(f"Sbf{i}") for i in range(2)]
    sT = [sb(f"sT{i}", [128, 512]) for i in range(2)]
    sTB = [P.buf(f"sT{i}") for i in range(2)]
    esT = sb("esT", [128, 4, 512], BF16)
    esTB = [P.buf(f"esT{i}") for i in range(4)]
    biasT8 = sb("biasT8", [128, 2, 1024]); biasB = P.buf("biasT8")
    mask0 = sb("mask0", [128, 128]); mask0B = P.buf("mask0")
    maskT = sb("maskT", [128, 2, 128]); maskTB = P.buf("maskT")
    esink = sb("esink", [128, 4]); esinkB = P.buf("esink")
    ta = sb("ta", [128, 512], BF16); taB = P.buf("ta")
    tg = sb("tg", [128, 512], BF16); tgB = P.buf("tg")
    m1 = sb("m1", [128, 512]); m1B = P.buf("m1")
    m2 = sb("m2", [128, 512]); m2B = P.buf("m2")
    lnv, lnvB = m1, m1B
    dtmp, dtmpB = m2, m2B
    ftt = [sb(f"ftt{i}", [128, 512]) for i in range(2)]
    fttB = [P.buf(f"ftt{i}") for i in range(2)]
    fa = [sb(f"fa{i}", [128, 512]) for i in range(2)]
    faB = [P.buf(f"fa{i}") for i in range(2)]
    p_f = [sb(f"p_f{i}", [128, 256]) for i in range(2)]
    p_fB = [P.buf(f"p_f{i}") for i in range(2)]
    p_bf = [sb(f"p_bf{i}", [128, 256], BF16) for i in range(2)]
    p_bfB = [P.buf(f"p_bf{i}") for i in range(2)]
    pT = sb("pT", [128, 2, 512], BF16); pTB = P.buf("pT")
    tgtB = esTB
    gfin = sb("gfin", [128, D]); gfinB = P.buf("gfin")
    gcols = sb("gcols", [128, 24]); gcolsB = P.buf("gcols")
    gnh = sb("gnh", [128, 2]); gnhB = P.buf("gnh")
    cst = sb("cst", [128, 386]); cstB = P.buf("cst")
    wgk = sb("wgk", [32, 512]); wgkB = P.buf("wgk")
    ident = sb("ident", [128, 128], BF16); identB = P.buf("ident")
    ones_bf = sb("ones_bf", [128, 128], BF16); onesB = P.buf("ones")
    ones_pad = sb("ones_pad", [128, 2, 128], BF16); onespB = P.buf("ones_pad")
    mhalf = sb("mhalf", [128, 1]); mhalfB = P.buf("mhalf")
    NSLOT = 6
    ring = [(sb(f"ws{i}", [128, 2048], BF16), P.buf(f"ws{i}"), P.new_sem()) for i in range(NSLOT)]
    ring_i = [0]
    NBANK = 8
    banks = [(nc.alloc_psum_tensor(f"pb{i}", [128, 512], F32), P.buf(f"pb{i}")) for i in range(NBANK)]
    bank_i = [0]
    marks = []

    def mark(name):
        marks.append((name, sum(len(o.fns) for o in P.ops['pe'])))

    pool_i = {}

    def nb(pool=None):
        if pool is None:
            r = banks[bank_i[0] % NBANK]
            bank_i[0] += 1
            return r
        i = pool_i.get(pool, 0)
        pool_i[pool] = i + 1
        return banks[pool[i % len(pool)]]

    PGLA2, PATT, PPROJ = ((0, 1, 2), (3, 4, 5)), (6, 7), (6, 7)

    U2m = cst[:, 0:128]
    SU2m = cst[:, 128:256]
    ind2m = cst[:, 256:258]
    maskA = cst[:, 258:386]

    sem_c = [P.new_sem() for _ in range(12)]
    P.dma("sp", lambda e: e.dma_start(out=cst[:], in_=cst_d), sem_c[0], writes=[cstB])
    P.dma("sp", lambda e: e.dma_start(out=gcols[:], in_=gcols_d), sem_c[1], writes=[gcolsB])
    P.dma("sp", lambda e: e.dma_start(out=gnh[:], in_=gn_d), sem_c[2], writes=[gnhB])
    P.dma("sp", lambda e: e.dma_start(out=gfin[:], in_=gfin_d.partition_broadcast(128)), sem_c[3], writes=[gfinB])
    P.dma("sp", lambda e: e.dma_start(out=esink[:], in_=sinkl_d), sem_c[4], writes=[esinkB])
    P.dma("sp", lambda e: e.dma_start(out=wgk[0:17, :], in_=wgk_d), sem_c[5], writes=[wgkB])
    P.dma("sp", lambda e: e.dma_start(out=mask0[:], in_=mask0_d), sem_c[6], writes=[mask0B])
    P.dma("sp", lambda e: e.dma_start(out=maskT[:], in_=maskT_d.rearrange("b k q -> k b q")), sem_c[7],
          writes=[maskTB])
    P.dma("sp", lambda e: e.dma_start(out=biasT8[:], in_=biasT_d.rearrange("b k c -> k b c")), sem_c[8],
          writes=[biasB])

    def flat(ap, n):
        return ap.rearrange("a b -> (a b)").rearrange("(n c) -> n c", c=2048)

    castB = {}

    def cast(name, src, dst, nrows_flat, pieces=1):
        b = P.buf("scr_" + name)
        castB[name] = b
        fs, fd = flat(src, nrows_flat), flat(dst, nrows_flat)
        per = nrows_flat // pieces
        fns = []
        for i in range(pieces):
            r0, r1 = i * per, (nrows_flat if i == pieces - 1 else (i + 1) * per)
            fns.append(lambda e, r0=r0, r1=r1: e.dma_start(out=fd[r0:r1, :], in_=fs[r0:r1, :]))
        P.dma("pool", fns, P.new_sem(), writes=[b])

    def cast_cols(name, c0, c1):
        b = P.buf("scr_" + name)
        castB[name] = [b]
        P.dma("pool", lambda e: e.dma_start(out=wi_s[:, c0:c1], in_=w_in_d[:, c0:c1]), P.new_sem(), writes=[b])

    pending_casts = []
    cast_cols("wiA", C_KG, C_OG)
    for nm_, a_, b_, np_ in (("wiB", 0, C_KG, 2), ("wiC", C_OG, C_GG, 4), ("wiD", C_GG, D_IN, 2)):
        castB[nm_] = [P.buf(f"scr_{nm_}{i}") for i in range(np_)]
        for i in range(np_):
            r0, r1 = i * (D // np_), (i + 1) * (D // np_)
            pending_casts.append(lambda nm_=nm_, a_=a_, b_=b_, i=i, r0=r0, r1=r1: P.dma(
                "pool", lambda e: e.dma_start(out=wi_s[r0:r1, a_:b_], in_=w_in_d[r0:r1, a_:b_]), P.new_sem(),
                writes=[castB[nm_][i]]))

    def wi_name(c0):
        return "wiB" if c0 < C_KG else ("wiA" if c0 < C_OG else ("wiC" if c0 < C_GG else "wiD"))
    P.op("pool", lambda e: e.memset(ident[:], 1.0), writes=[identB])
    P.op("pool", lambda e: e.affine_select(out=ident[:], in_=ident[:], pattern=[[-1, 128]], compare_op=ALU.is_equal,
                                           fill=0.0, base=0, channel_multiplier=1), reads=[identB], writes=[identB])
    P.op("pool", lambda e: e.memset(ones_bf[:], 1.0), writes=[onesB])
    P.op("pool", [lambda e: e.memset(ones_pad[:], 0.0)], writes=[onespB])
    P.op("pool", [lambda e: e.memset(ones_pad[:, 0, 0:64], 1.0), lambda e: e.memset(ones_pad[:, 1, 64:128], 1.0)],
         reads=[onespB], writes=[onespB])
    P.op("pool", lambda e: e.memset(mhalf[:], -0.5), writes=[mhalfB])
    P.op("pool", [lambda e: e.memset(ka_pad[0][:], 0.0), lambda e: e.memset(ka_pad[1][:], 0.0),
                  lambda e: e.memset(va_pad[:], 0.0)], writes=kaB + vaB)
    P.op("pool", [lambda e, i=i, c=c: e.memset(kd2[i][c][:], 0.0) for i in range(2) for c in range(2)],
         writes=kdB2[0] + kdB2[1])
    P.op("pool", lambda e: e.memset(gkT[:], 1.0), writes=[gkTB])
    P.op("pool", lambda e: e.memset(S_st[:], 0.0), writes=[SB_])
    P.op("pool", lambda e: e.memset(wgk[:], 0.0), writes=[wgkB]) if False else None
    P.op("act", lambda e: e.activation(out=esink[:], in_=esink[:], func=AF.Exp), reads=[esinkB], writes=[esinkB])
    for kb in range(2):
        P.op("dve", lambda e, kb=kb: e.tensor_tensor(
            out=biasT8[:, kb, :].rearrange("p (h q) -> p h q", h=8),
            in0=biasT8[:, kb, :].rearrange("p (h q) -> p h q", h=8),
            in1=maskT[:, kb, :].unsqueeze(1).broadcast_to([128, 8, 128]), op=ALU.add),
            reads=[biasB, maskTB], writes=[biasB])
    P.op("dve", lambda e: e.tensor_scalar(out=biasT8[:], in0=biasT8[:], scalar1=8.0, scalar2=None, op0=ALU.mult),
         reads=[biasB], writes=[biasB])
    P.op("dve", lambda e: e.tensor_scalar(out=mask0[:], in0=mask0[:], scalar1=8.0, scalar2=None, op0=ALU.mult),
         reads=[mask0B], writes=[mask0B])
    def defer_cast(name, src, dst, nrows_flat, pieces=1):
        castB[name] = [P.buf(f"scr_{name}{i}") for i in range(pieces)]
        fs, fd = flat(src, nrows_flat), flat(dst, nrows_flat)
        per = nrows_flat // pieces
        for i in range(pieces):
            r0, r1 = i * per, (nrows_flat if i == pieces - 1 else (i + 1) * per)
            pending_casts.append(lambda name=name, i=i, r0=r0, r1=r1, fs=fs, fd=fd: P.dma(
                "pool", lambda e: e.dma_start(out=fd[r0:r1, :], in_=fs[r0:r1, :]), P.new_sem(),
                writes=[castB[name][i]]))

    castS = {}
    defer_cast("wpa", wpa_d, wpa_s, 512 * D // 2048)
    defer_cast("wpg", wpg_d, wpg_s, D * D // 2048, pieces=2)
    defer_cast("wout", wout_d, wout_s, D * D // 2048, pieces=2)
    defer_cast("wfi", wfi_d, wfi_s, D * 2 * D_FF // 2048, pieces=10)
    defer_cast("wfo", wfo_d, wfo_s, D_FF * D // 2048, pieces=5)
    defer_cast("wpgate", wpgate_d, wpgate_s, D * D // 2048, pieces=2)
    defer_cast("wple", wple_d, wple_s, 256 * D // 2048)

    def wload(srcs, name, nk, ncols):
        t, b, s = ring[ring_i[0] % NSLOT]
        ring_i[0] += 1
        fns = [(lambda e, dv=dv, sa=sa, t=t: e.dma_start(out=dv(t), in_=sa)) for dv, sa in srcs]
        P.dma("sp", fns, s, reads=castB[name], writes=[b])
        return t, b

    def wload_std(scr, name, r0, nk, c0, ncols):
        src = scr[r0:r0 + nk * 128, c0:c0 + ncols].rearrange("(k p) c -> p k c", p=128)
        t, b = wload([(lambda t: t[:, 0:nk * ncols].rearrange("p (k c) -> p k c", k=nk), src)], name, nk, ncols)
        return t[:, 0:nk * ncols].rearrange("p (k c) -> p k c", k=nk), b

    evac_i = [0]

    def evac_copy(out_ap, in_ap, reads, writes, eng=None):
        if eng is None:
            eng = "act" if evac_i[0] % 2 == 0 else "dve"
            evac_i[0] += 1
        if eng == "act":
            P.op("act", lambda e: e.activation(out=out_ap, in_=in_ap, func=AF.Copy), reads=reads, writes=writes)
        else:
            P.op("dve", lambda e: e.tensor_copy(out=out_ap, in_=in_ap), reads=reads, writes=writes)

    def fm_mm(bank, w, wB, ccol, act_view, actB, nk):
        bt, bb = bank
        fns = [(lambda e, kc=kc: e.matmul(bt[:, :], lhsT=w[:, kc, ccol:ccol + 128], rhs=act_view[:, kc, :],
                                          start=(kc == 0), stop=(kc == nk - 1))) for kc in range(nk)]
        P.op("pe", fns, reads=[wB] + list(actB), writes=[bb])

    def norm_gen(gidx, load_fn, dst=None, pool=None):
        hTc, hTBc = dst if dst is not None else (hT, [hTB])
        for t in range(4):
            if load_fn is not None:
                load_fn(t)
            P.op("act", lambda e, t=t: e.activation(out=junk_ap, in_=xres[t][:], func=AF.Square,
                                                   accum_out=stat[t][:, 0:1]),
                 reads=[xresB[t]], writes=[junkB, statB[t]])
            P.op("pool", lambda e, t=t: e.tensor_scalar(out=stat[t][:, 1:2], in0=stat[t][:, 0:1], scalar1=1.0 / D,
                                                       scalar2=EPS, op0=ALU.mult, op1=ALU.add),
                 reads=[statB[t]], writes=[statB[t]])
            P.op("pool", lambda e, t=t: e.tensor_tensor(out=stat[t][:, 2:3], in0=stat[t][:, 1:2], in1=mhalf[:, 0:1],
                                                       op=ALU.pow),
                 reads=[statB[t], mhalfB], writes=[statB[t]])
            i = t % 2
            P.op("dve", lambda e, t=t, i=i: e.tensor_scalar(out=xn[i][:], in0=xres[t][:], scalar1=stat[t][:, 2:3],
                                                           scalar2=None, op0=ALU.mult),
                 reads=[xresB[t], statB[t]], writes=[xnB[i]])
            bt, bb = nb(pool)
            btb = bt[:].bitcast(BF16)
            P.op("pe", [(lambda e, kc=kc, i=i, btb=btb: e.transpose(out=btb[:, kc * 128:(kc + 1) * 128],
                                                                    in_=xn[i][:, kc * 128:(kc + 1) * 128],
                                                                    identity=ident[:])) for kc in range(8)],
                 reads=[xnB[i], identB], writes=[bb])
            P.op("dve", lambda e, t=t, btb=btb: e.tensor_tensor(
                out=hTc[:, :, t * 128:(t + 1) * 128], in0=btb.rearrange("p (k c) -> p k c", k=8),
                in1=gcols[:, gidx * 8:(gidx + 1) * 8].unsqueeze(2).broadcast_to([128, 8, 128]), op=ALU.mult),
                reads=[bb, gcolsB], writes=hTBc)
            yield

    def norm_to_hT(gidx, load_fn, dst=None):
        for _ in norm_gen(gidx, load_fn, dst):
            pass


    ystage = [slab[:, 4 * i:4 * i + 4, :].rearrange("p a b -> p (a b)").bitcast(F32) for i in range(4)]
    ystageB = [slabB[4 * i:4 * i + 4] for i in range(4)]
    xsem = [P.new_sem() for _ in range(4)]
    osem = [P.new_sem() for _ in range(4)]

    def load_x(src, row0):
        def f(t):
            P.dma("sp", lambda e: e.dma_start(out=xres[t][:], in_=src[row0 + t * 128: row0 + (t + 1) * 128, :]),
                  xsem[t], writes=[xresB[t]])
        return f

    def proj_tm(c0, ncols, dst_fn, dstB_fn):
        w, wB = wload_std(wi_s, wi_name(c0), 0, 8, c0, ncols)
        for t in range(4):
            bt, bb = nb()
            P.op("pe", [(lambda e, kc=kc, t=t, bt=bt: e.matmul(bt[:, 0:ncols], lhsT=hT[:, kc, t * 128:(t + 1) * 128],
                                                               rhs=w[:, kc, :], start=(kc == 0), stop=(kc == 7)))
                        for kc in range(8)], reads=[wB, hTB], writes=[bb])
            evac_copy(dst_fn(t), bt[:, 0:ncols], [bb], [dstB_fn(t)])

    def proj_kv_a(src=None):
        hTc, hTBc = src if src is not None else (hT, [hTB])
        w, wB = wload_std(wi_s, "wiB", 0, 8, C_KA, 256)
        bank = nb()
        fm_mm(bank, w, wB, 0, hTc, hTBc, 8)
        bt, bb = bank
        P.op("act", [lambda e: e.activation(out=ka_pad[0][0:64, 128:640], in_=bt[0:64, :], func=AF.Copy),
                     lambda e: e.activation(out=ka_pad[1][64:128, 128:640], in_=bt[64:128, :], func=AF.Copy)],
             reads=[bb], writes=kaB[1:5])
        for t in range(4):
            bt2, bb2 = nb()
            P.op("pe", [(lambda e, kc=kc, t=t, bt2=bt2: e.matmul(bt2[:, 0:128], lhsT=hTc[:, kc, t * 128:(t + 1) * 128],
                                                                 rhs=w[:, kc, 128:256], start=(kc == 0),
                                                                 stop=(kc == 7))) for kc in range(8)],
                 reads=[wB] + hTBc, writes=[bb2])
            P.op("act", [lambda e, t=t, bt2=bt2: e.activation(out=va_pad[:, t + 1, 0, 0:64], in_=bt2[:, 0:64],
                                                              func=AF.Copy),
                         lambda e, t=t, bt2=bt2: e.activation(out=va_pad[:, t + 1, 1, 64:128], in_=bt2[:, 64:128],
                                                              func=AF.Copy)],
                 reads=[bb2], writes=[vaB[t + 1]])

    def carry_kv():
        P.op("pool", [lambda e: e.tensor_copy(out=ka_pad[0][:, 0:128], in_=ka_pad[0][:, 512:640]),
                      lambda e: e.tensor_copy(out=ka_pad[1][:, 0:128], in_=ka_pad[1][:, 512:640]),
                      lambda e: e.tensor_copy(out=va_pad[:, 0], in_=va_pad[:, 4])],
             reads=[kaB[4], vaB[4]], writes=[kaB[0], vaB[0]])

    def proj_gla_tm():
        for i in range(2):
            proj_tm(C_KG + i * 256, 256, lambda t, i=i: kgTM[t][:, i * 256:(i + 1) * 256], lambda t: kgTMB[t])
        for i in range(4):
            proj_tm(C_VG + i * 256, 256, lambda t, i=i: vgTM[t][:, i * 256:(i + 1) * 256], lambda t: vgTMB[t])

    def proj_gk(src=None):
        hTc, hTBc = src if src is not None else (hT, [hTB])
        w, wB = wload_std(wi_s, "wiA", 0, 8, C_GK, 16)
        bt, bb = nb()
        P.op("pe", [(lambda e, kc=kc: e.matmul(bt[0:16, :], lhsT=w[:, kc, 0:16], rhs=hTc[:, kc, :], start=(kc == 0),
                                               stop=(kc == 7))) for kc in range(8)], reads=[wB] + hTBc, writes=[bb])
        P.op("act", lambda e: e.activation(out=gkT[0:16, :], in_=bt[0:16, :], func=AF.Copy), reads=[bb],
             writes=[gkTB])

    def prefix_proj_gen(src):
        hTc, hTBc = src
        proj_gk(src)
        yield
        groups = []
        for i in range(2):
            groups.append((wload_std(wi_s, "wiA", 0, 8, C_KG + i * 256, 256), 0, i))
        for i in range(4):
            groups.append((wload_std(wi_s, "wiA", 0, 8, C_VG + i * 256, 256), 1, i))
        for t in range(4):
            for (w, wB), kind, i in groups:
                bt, bb = nb(PPROJ)
                P.op("pe", [(lambda e, kc=kc, t=t, bt=bt, w=w: e.matmul(bt[:, 0:256],
                                                                        lhsT=hTc[:, kc, t * 128:(t + 1) * 128],
                                                                        rhs=w[:, kc, :], start=(kc == 0),
                                                                        stop=(kc == 7))) for kc in range(8)],
                     reads=[wB] + hTBc, writes=[bb])
                if kind == 0:
                    evac_copy(kgTM[t][:, i * 256:(i + 1) * 256], bt[:, 0:256], [bb], [kgTMB[t]])
                else:
                    evac_copy(vgTM[t][:, i * 256:(i + 1) * 256], bt[:, 0:256], [bb], [vgTMB[t]])
            yield

    def og_gen():
        for i in range(4):
            w, wB = wload_std(wi_s, "wiC", 0, 8, C_OG + i * 256, 256)
            for cc in range(2):
                idx = i * 2 + cc
                bank = nb()
                fm_mm(bank, w, wB, cc * 128, hT, [hTB], 8)
                bt, bb = bank
                f = idx % 2
                P.op("act", lambda e, bt=bt, f=f: e.activation(out=ftt[f][:], in_=bt[:, :], func=AF.Exp, scale=-1.0),
                     reads=[bb], writes=[fttB[f]])
                P.op("act", lambda e, f=f: e.activation(out=ftt[f][:], in_=ftt[f][:], func=AF.Ln, bias=1.0),
                     reads=[fttB[f]], writes=[fttB[f]])
                P.op("act", lambda e, f=f: e.activation(out=ftt[f][:], in_=ftt[f][:], func=AF.Exp, scale=-1.0),
                     reads=[fttB[f]], writes=[fttB[f]])
                P.op("dve", lambda e, bt=bt, f=f, idx=idx: e.scalar_tensor_tensor(
                    out=slab[:, S_GSIL + idx, :], in0=bt[:, :], scalar=gnh[:, idx % 2:idx % 2 + 1], in1=ftt[f][:],
                    op0=ALU.mult, op1=ALU.mult), reads=[bb, gnhB, fttB[f]], writes=[slabB[S_GSIL + idx]])
                yield

    state_par = [0]

    def gla_tile(t, full):
        tok = slice(t * 128, (t + 1) * 128)
        par = t % 2
        PGLA = PGLA2[par]
        sp_t, spB, ED, EDB = sp_t2[par], spB2[par], ED2[par], EDB2[par]
        Epos, EposB, Eneg, EnegB = Epos2[par], EposB2[par], Eneg2[par], EnegB2[par]
        qt, qtB, kt, ktB, kd, kdB = qt2[par], qtB2[par], kt2[par], ktB2[par], kd2[par], kdB2[par]
        el, elB = el2[par], elB2[par]
        zt, zb = nb(PGLA)
        P.op("pe", lambda e: e.matmul(zt[:, :], lhsT=gkT[0:17, tok], rhs=wgk[0:17, :], start=True, stop=True),
             reads=[gkTB, wgkB], writes=[zb])
        P.op("act", lambda e: e.activation(out=sp_t[:], in_=zt[:, :], func=AF.Exp, scale=-1.0), reads=[zb],
             writes=[spB])
        P.op("act", lambda e: e.activation(out=sp_t[:], in_=sp_t[:], func=AF.Ln, bias=1.0), reads=[spB],
             writes=[spB])
        yield
        dt_, db = nb(PGLA)
        P.op("pe", lambda e: e.matmul(dt_[:, :], lhsT=SU2m, rhs=sp_t[:], start=True, stop=True),
             reads=[cstB, spB], writes=[db])
        P.op("act", lambda e: e.activation(out=ED[:], in_=dt_[:, :], func=AF.Exp), reads=[db], writes=[EDB])
        if not full:
            et, eb = nb(PGLA)
            P.op("pe", [(lambda e, h=h: e.matmul(et[:, h * 2:(h + 1) * 2], lhsT=sp_t[:, h * 128:(h + 1) * 128],
                                                 rhs=ind2m, start=True, stop=True)) for h in range(4)],
                 reads=[cstB, spB], writes=[eb])
            P.op("act", lambda e: e.activation(out=el[:], in_=et[:, 0:8], func=AF.Exp), reads=[eb], writes=[elB])
        if full:
            gt, gb = nb(PGLA)
            P.op("pe", [(lambda e, h=h: e.matmul(gt[:, h * 128:(h + 1) * 128], lhsT=sp_t[:, h * 128:(h + 1) * 128],
                                                 rhs=U2m, start=True, stop=True)) for h in range(4)],
                 reads=[cstB, spB], writes=[gb])
            P.op("act", lambda e: e.activation(out=Epos[:], in_=gt[:, :], func=AF.Exp), reads=[gb], writes=[EposB])
            P.op("act", lambda e: e.activation(out=Eneg[:], in_=gt[:, :], func=AF.Exp, scale=-1.0), reads=[gb],
                 writes=[EnegB])
            yield
            P.op("dve", lambda e: e.scalar_tensor_tensor(
                out=qt[:], in0=slab[:, S_QG:S_QG + 4, tok], scalar=128.0 ** -0.5,
                in1=Epos[:].rearrange("p (h c) -> p h c", h=4), op0=ALU.mult, op1=ALU.mult),
                reads=slabB[S_QG:S_QG + 4] + [EposB], writes=[qtB])
            P.op("dve", lambda e: e.tensor_tensor(
                out=kt[:], in0=slab[:, S_KG:S_KG + 4, tok], in1=Eneg[:].rearrange("p (h c) -> p h c", h=4),
                op=ALU.mult), reads=slabB[S_KG:S_KG + 4] + [EnegB], writes=[ktB])
        P.op("pool", lambda e: e.tensor_tensor(out=kd[0][0:64, :], in0=kgTM[t][0:64, :], in1=ED[0:64, :], op=ALU.mult),
             reads=[kgTMB[t], EDB], writes=[kdB[0]])
        P.op("pool", lambda e: e.tensor_tensor(out=kd[1][64:128, :], in0=kgTM[t][64:128, :], in1=ED[64:128, :],
                                               op=ALU.mult), reads=[kgTMB[t], EDB], writes=[kdB[1]])
        if not full and pending_casts:
            pending_casts.pop(0)()
        yield
        if full:
            at, ab = nb(PGLA)
            P.op("pe", [(lambda e, h=h: e.matmul(at[:, h * 128:(h + 1) * 128], lhsT=kt[:, h, :], rhs=qt[:, h, :],
                                                 start=True, stop=True)) for h in range(4)],
                 reads=[ktB, qtB], writes=[ab])
            P.op("dve", lambda e: e.tensor_tensor(out=ATs[:], in0=at[:, :].rearrange("p (h c) -> p h c", h=4),
                                                  in1=maskA.unsqueeze(1).broadcast_to([128, 4, 128]), op=ALU.mult),
                 reads=[ab, cstB], writes=[ATsB])
        sps = [None, None]

        def sps_mm(c):
            pair = []
            for hp in range(2):
                st, sbb = nb(PGLA)
                P.op("pe", [(lambda e, hh=hh, hp=hp, c=c, st=st: e.matmul(
                    st[:, hh * 256:(hh + 1) * 256], lhsT=kd[c][:, (hp * 2 + hh) * 128:(hp * 2 + hh + 1) * 128],
                    rhs=vgTM[t][:, (hp * 2 + hh) * 256:(hp * 2 + hh + 1) * 256], start=True, stop=True))
                    for hh in range(2)], reads=[kdB[c], vgTMB[t]], writes=[sbb])
                pair.append((st, sbb))
            sps[c] = pair

        def upd(c):
            for h in range(4):
                st, sbb = sps[c][h // 2]
                if full:
                    ci = h * 128 + c * 64 + 63
                    sc, scB = Epos[:, ci:ci + 1], EposB
                else:
                    sc, scB = el[:, h * 2 + c:h * 2 + c + 1], elB
                P.op("dve", lambda e, h=h, st=st, c=c, sc=sc: e.scalar_tensor_tensor(
                    out=S_st[:, h * 256:(h + 1) * 256], in0=S_st[:, h * 256:(h + 1) * 256],
                    scalar=sc, in1=st[:, (h % 2) * 256:(h % 2 + 1) * 256],
                    op0=ALU.mult, op1=ALU.add), reads=[SB_, scB, sbb], writes=[SB_])

        pa = state_par[0]
        if not full:
            sps_mm(0)
            upd(0)
            sps_mm(1)
            upd(1)
            yield
            return
        sps_mm(0)
        upd(0)
        P.op("act", lambda e: e.activation(out=Sbf[1 - pa][:], in_=S_st[:], func=AF.Copy), reads=[SB_],
             writes=[SbfB[1 - pa]])
        yield
        sps_mm(1)
        upd(1)
        obanks = [nb(PGLA), nb(PGLA)]
        for idx in range(8):
            h, dvc = idx // 2, idx % 2
            ot, ob = obanks[idx // 4]
            oc = (idx % 4) * 128
            col = h * 256 + dvc * 128
            P.op("pe", [
                lambda e, ot=ot, oc=oc, col=col, h=h: e.matmul(ot[:, oc:oc + 128], lhsT=vgTM[t][:, col:col + 128],
                                                               rhs=ATs[:, h, :], start=True, stop=False),
                lambda e, ot=ot, oc=oc, col=col, h=h: e.matmul(ot[:, oc:oc + 64], lhsT=Sbf[pa][:, col:col + 128],
                                                               rhs=qt[:, h, 0:64], start=False, stop=False),
                lambda e, ot=ot, oc=oc, col=col, h=h: e.matmul(ot[:, oc + 64:oc + 128],
                                                               lhsT=Sbf[1 - pa][:, col:col + 128],
                                                               rhs=qt[:, h, 64:128], start=False, stop=True)],
                reads=[vgTMB[t], ATsB, SbfB[0], SbfB[1], qtB], writes=[ob])
        P.op("act", lambda e: e.activation(out=Sbf[pa][:], in_=S_st[:], func=AF.Copy), reads=[SB_],
             writes=[SbfB[pa]])
        for k in range(2):
            ot, ob = obanks[k]
            P.op("act", lambda e, ot=ot, k=k: e.activation(out=sq[:, k * 4:(k + 1) * 4, :],
                                                           in_=ot[:, :].rearrange("p (i c) -> p i c", i=4),
                                                           func=AF.Square), reads=[ob], writes=[sqB])
        yield
        st_, sb_ = nb(PGLA)
        sqv = sq[:].rearrange("p (h v) c -> p h v c", v=2)
        P.op("pe", [(lambda e, v=v: e.matmul(st_[:, :], lhsT=ones_bf[:], rhs=sqv[:, :, v, :], start=(v == 0),
                                             stop=(v == 1))) for v in range(2)], reads=[sqB, onesB], writes=[sb_])
        P.op("act", lambda e: e.activation(out=lnv[:], in_=st_[:, :], func=AF.Ln, scale=1.0 / 256, bias=EPS),
             reads=[sb_], writes=[lnvB])
        P.op("act", lambda e: e.activation(out=rstd_g[:].rearrange("p h c -> p (h c)"), in_=lnv[:], func=AF.Exp,
                                           scale=-0.5), reads=[lnvB], writes=[rstdgB])
        for k in range(2):
            ot, ob = obanks[k]
            P.op("dve", lambda e, ot=ot, k=k: e.tensor_tensor(
                out=otmp[:, k * 4:(k + 1) * 4, :].rearrange("p (h v) c -> p h v c", v=2),
                in0=ot[:, :].rearrange("p (h v c) -> p h v c", h=2, v=2),
                in1=rstd_g[:, k * 2:(k + 1) * 2, :].unsqueeze(2).broadcast_to([128, 2, 2, 128]), op=ALU.mult),
                reads=[ob, rstdgB], writes=[otmpB])
        P.op("pool", lambda e: e.tensor_tensor(out=slab[:, S_OGN:S_OGN + 8, tok], in0=otmp[:],
                                               in1=slab[:, S_GSIL:S_GSIL + 8, tok], op=ALU.mult),
             reads=[otmpB] + slabB[S_GSIL:S_GSIL + 8], writes=slabB[S_OGN:S_OGN + 8])
        yield

    def attn_tile(t, first):
        tok = slice(t * 128, (t + 1) * 128)
        k = 0
        for g in range(2):
            for kb in range(2):
                bt, bb = nb(PATT)
                s_ = t + kb
                P.op("pe", lambda e, bt=bt, g=g, s_=s_: e.matmul(bt[:, :], lhsT=ka_pad[g][:, s_ * 128:(s_ + 1) * 128],
                                                                 rhs=slab[:, S_QA:S_QA + 4, tok], start=True,
                                                                 stop=True),
                     reads=[kaB[s_]] + slabB[S_QA:S_QA + 4], writes=[bb])
                i = k % 2
                P.op("dve", lambda e, bt=bt, g=g, kb=kb, i=i: e.tensor_tensor(
                    out=sT[i][:], in0=bt[:, :], in1=biasT8[:, kb, g * 512:(g + 1) * 512], op=ALU.add),
                    reads=[bb, biasB], writes=[sTB[i]])
                if first and kb == 0:
                    P.op("dve", lambda e, i=i: e.tensor_tensor(
                        out=sT[i][:].rearrange("p (j q) -> p j q", j=4), in0=sT[i][:].rearrange("p (j q) -> p j q", j=4),
                        in1=mask0[:].unsqueeze(1).broadcast_to([128, 4, 128]), op=ALU.add),
                        reads=[sTB[i], mask0B], writes=[sTB[i]])
                P.op("act", lambda e, i=i, k=k: e.activation(out=esT[:, k, :], in_=sT[i][:], func=AF.Exp, scale=0.125),
                     reads=[sTB[i]], writes=[esTB[k]])
                k += 1
                if k == 2:
                    yield
        yield
        ot, ob = nb(PATT)
        dt_, db = nb(PATT)
        fo, fd = [], []
        for k, (g, kb) in enumerate([(0, 0), (0, 1), (1, 0), (1, 1)]):
            s_ = t + kb
            fo.append(lambda e, g=g, s_=s_, k=k: e.matmul(ot[:, :], lhsT=va_pad[:, s_, g, :], rhs=esT[:, k, :],
                                                          start=(k == 0), stop=(k == 3)))
            fd.append(lambda e, g=g, k=k: e.matmul(dt_[:, :], lhsT=ones_pad[:, g, :], rhs=esT[:, k, :],
                                                   start=(k == 0), stop=(k == 3)))
        P.op("pe", fd, reads=[onespB] + esTB, writes=[db])
        P.op("pe", fo, reads=[vaB[t], vaB[t + 1]] + esTB, writes=[ob])
        P.op("dve", lambda e: e.tensor_tensor(out=dtmp[:].rearrange("p (j q) -> p j q", j=4),
                                              in0=dt_[:, :].rearrange("p (j q) -> p j q", j=4),
                                              in1=esink[:].unsqueeze(2).broadcast_to([128, 4, 128]), op=ALU.add),
             reads=[db, esinkB], writes=[dtmpB])
        P.op("act", lambda e: e.activation(out=dtmp[:], in_=dtmp[:], func=AF.Ln), reads=[dtmpB], writes=[dtmpB])
        P.op("act", lambda e: e.activation(out=dtmp[:], in_=dtmp[:], func=AF.Exp, scale=-1.0), reads=[dtmpB],
             writes=[dtmpB])
        yield
        P.op("dve", lambda e: e.tensor_tensor(out=slab[:, S_OA:S_OA + 4, tok],
                                              in0=ot[:, :].rearrange("p (j q) -> p j q", j=4),
                                              in1=dtmp[:].rearrange("p (j q) -> p j q", j=4), op=ALU.mult),
             reads=[ob, dtmpB], writes=slabB[S_OA:S_OA + 4])
        yield

    def run_merged(gens):
        gens = list(gens)
        while gens:
            for g in list(gens):
                try:
                    next(g)
                except StopIteration:
                    gens.remove(g)

    def run_weighted(pairs):
        pairs = [[g, w] for g, w in pairs]
        while pairs:
            for pr in list(pairs):
                for _ in range(pr[1]):
                    try:
                        next(pr[0])
                    except StopIteration:
                        pairs.remove(pr)
                        break

    def run_gla_region(full, extra, delay):
        ge = chain(gla_tile(t, full) for t in (0, 2))
        go = chain(gla_tile(t, full) for t in (1, 3))
        gens = [[ge, 0], [go, delay]] + [[g, 0] for g in extra]
        step = 0
        while gens:
            for pr in list(gens):
                if step < pr[1]:
                    continue
                try:
                    next(pr[0])
                except StopIteration:
                    gens.remove(pr)
            step += 1

    def chain(gs):
        for g in gs:
            yield from g

    NPB = NTP // 4
    hT_alt = (slab[:, 0:8, :], slabB[0:8])

    def hsel(bp):
        return hT_alt if (NPB - 1 - bp) % 2 == 0 else (hT, [hTB])

    mark('P0.norm')
    norm_to_hT(0, load_x(xp_d, 0), hsel(0))
    for bp in range(NPB):
        last = bp == NPB - 1
        src = hsel(bp)
        mark(f'P{bp}.proj')
        ld = load_x(x_d, 0) if last else load_x(xp_d, (bp + 1) * 512)
        for t in range(4):
            ld(t)
        if last:
            proj_kv_a(src)
            carry_kv()
            nxt = norm_gen(0, None, None, PPROJ)
        else:
            nxt = norm_gen(0, None, hsel(bp + 1), PPROJ)
        pg = prefix_proj_gen(src)
        next(pg)
        next(pg)
        next(pg)
        mark(f'P{bp}.gla')
        run_gla_region(False, [pg, nxt], 2)
    while pending_casts:
        pending_casts.pop(0)()
    P.op("act", lambda e: e.activation(out=Sbf[0][:], in_=S_st[:], func=AF.Copy), reads=[SB_], writes=[SbfB[0]])
    state_par[0] = 0

    psem = [P.new_sem() for _ in range(2)]
    for b in range(NT // 4):
        row0 = b * 512
        mark(f'M{b}.norm1')
        if b > 0:
            norm_to_hT(0, None)
        mark(f'M{b}.inproj')
        for j0 in (0, 2):
            srcs = []
            for g in range(2):
                for jj in range(2):
                    c0 = C_QA + g * 256 + (j0 + jj) * 64
                    src = wi_s[:, c0:c0 + 64].rearrange("(k p) c -> p k c", p=128)
                    srcs.append((lambda t, g=g, jj=jj: t[:, 0:2048].rearrange(
                        "p (k j g d) -> p k j g d", k=8, j=2, g=2)[:, :, jj, g, :], src))
            wt, wB = wload(srcs, "wiB", 8, 256)
            wv = wt[:, 0:2048].rearrange("p (k j c) -> p k j c", k=8, j=2)
            for jj in range(2):
                bt, bb = nb()
                P.op("pe", [(lambda e, kc=kc, jj=jj, bt=bt, wv=wv: e.matmul(bt[:, :], lhsT=wv[:, kc, jj, :],
                                                                            rhs=hT[:, kc, :], start=(kc == 0),
                                                                            stop=(kc == 7))) for kc in range(8)],
                     reads=[wB, hTB], writes=[bb])
                evac_copy(slab[:, S_QA + j0 + jj, :], bt[:, :], [bb], [slabB[S_QA + j0 + jj]])
        proj_kv_a()
        for (c_base, s_base) in ((C_QG, S_QG), (C_KG, S_KG)):
            for i in range(2):
                w, wB = wload_std(wi_s, wi_name(c_base + i * 256), 0, 8, c_base + i * 256, 256)
                for cc in range(2):
                    bank = nb()
                    fm_mm(bank, w, wB, cc * 128, hT, [hTB], 8)
                    evac_copy(slab[:, s_base + i * 2 + cc, :], bank[0][:, :], [bank[1]], [slabB[s_base + i * 2 + cc]])
        proj_gk()
        proj_gla_tm()
        mark(f'M{b}.attn')
        for _ in og_gen():
            pass
        run_gla_region(True, [chain(attn_tile(t, first=(b == 0 and t == 0)) for t in range(4))], 3)
        carry_kv()
        mark(f'M{b}.mixed')
        for qq in range(4):
            wga, wgaB = wload_std(wi_s, "wiC", 0, 8, C_GA + qq * 256, 256)
            wgg, wggB = wload_std(wi_s, "wiD", 0, 8, C_GG + qq * 256, 256)
            srcs = []
            for g in range(2):
                src = wpa_s[g * 256:(g + 1) * 256, qq * 256:(qq + 1) * 256].rearrange("(j p) c -> p j c", p=64)
                srcs.append((lambda t, g=g: t[g * 64:(g + 1) * 64, 0:1024].rearrange("p (j c) -> p j c", j=4), src))
            wpat, wpaB = wload(srcs, "wpa", 4, 256)
            wpa = wpat[:, 0:1024].rearrange("p (j c) -> p j c", j=4)
            wpg, wpgB = wload_std(wpg_s, "wpg", 0, 8, qq * 256, 256)
            for mm_ in range(2):
                m = qq * 2 + mm_
                bga, bgg, bya, byg = nb(), nb(), nb(), nb()
                fm_mm(bga, wga, wgaB, mm_ * 128, hT, [hTB], 8)
                P.op("act", lambda e, bt=bga[0]: e.activation(out=ta[:], in_=bt[:, :], func=AF.Tanh, scale=0.5),
                     reads=[bga[1]], writes=[taB])
                fm_mm(bgg, wgg, wggB, mm_ * 128, hT, [hTB], 8)
                P.op("act", lambda e, bt=bgg[0]: e.activation(out=tg[:], in_=bt[:, :], func=AF.Tanh, scale=0.5),
                     reads=[bgg[1]], writes=[tgB])
                fm_mm(bya, wpa, wpaB, mm_ * 128, slab[:, S_OA:S_OA + 4, :], slabB[S_OA:S_OA + 4], 4)
                fm_mm(byg, wpg, wpgB, mm_ * 128, slab[:, S_OGN:S_OGN + 8, :], slabB[S_OGN:S_OGN + 8], 8)
                P.op("dve", lambda e, bt=bya[0]: e.scalar_tensor_tensor(out=m1[:], in0=ta[:], scalar=1.0, in1=bt[:, :],
                                                                        op0=ALU.add, op1=ALU.mult),
                     reads=[taB, bya[1]], writes=[m1B])
                P.op("dve", lambda e, bt=byg[0]: e.scalar_tensor_tensor(out=m2[:], in0=tg[:], scalar=1.0, in1=bt[:, :],
                                                                        op0=ALU.add, op1=ALU.mult),
                     reads=[tgB, byg[1]], writes=[m2B])
                P.op("pool", lambda e, m=m: e.tensor_tensor(out=slab[:, S_MIX + m, :], in0=m1[:], in1=m2[:],
                                                            op=ALU.add),
                     reads=[m1B, m2B], writes=[slabB[S_MIX + m]])

        def proj_res(scr, name, nkc, act_view, actB, epi):
            for half in range(2):
                tb = [nb() for _ in range(4)]
                ngrp = (nkc + 3) // 4
                for kg in range(ngrp):
                    k0 = kg * 4
                    nk = min(4, nkc - k0)
                    w, wB = wload_std(scr, name, k0 * 128, nk, half * 512, 512)
                    for t in range(4):
                        bt, bb = tb[t]
                        P.op("pe", [(lambda e, kk=kk, t=t, bt=bt, w=w, k0=k0: e.matmul(
                            bt[:, :], lhsT=act_view[:, k0 + kk, t * 128:(t + 1) * 128], rhs=w[:, kk, :],
                            start=(k0 + kk == 0), stop=(k0 + kk == nkc - 1))) for kk in range(nk)],
                            reads=[wB] + list(actB), writes=[bb])
                for t in range(4):
                    epi(t, half, tb[t])

        def epi_half(t, half, bank):
            bt, bb = bank
            hs = slice(half * 512, (half + 1) * 512)
            P.op("dve", lambda e: e.scalar_tensor_tensor(out=xres[t][:, hs], in0=bt[:, :], scalar=0.5,
                                                         in1=xres[t][:, hs], op0=ALU.mult, op1=ALU.add),
                 reads=[bb, xresB[t]], writes=[xresB[t]])

        def epi_add(t, half, bank):
            bt, bb = bank
            hs = slice(half * 512, (half + 1) * 512)
            P.op("dve", lambda e: e.tensor_tensor(out=xres[t][:, hs], in0=bt[:, :], in1=xres[t][:, hs], op=ALU.add),
                 reads=[bb, xresB[t]], writes=[xresB[t]])

        mark(f'M{b}.wout')
        proj_res(wout_s, "wout", 8, slab[:, S_MIX:S_MIX + 8, :], slabB[S_MIX:S_MIX + 8], epi_half)

        mark(f'M{b}.norm2')
        norm_to_hT(1, None)
        mark(f'M{b}.ffnin')
        for ii in range(11):
            wg_, wgB_ = wload_std(wfi_s, "wfi", 0, 8, ii * 256, 256)
            wu_, wuB_ = wload_std(wfi_s, "wfi", 0, 8, D_FF + ii * 256, 256)
            for s in range(2):
                c = ii * 2 + s
                bg, bu = nb(), nb()
                fm_mm(bg, wg_, wgB_, s * 128, hT, [hTB], 8)
                fm_mm(bu, wu_, wuB_, s * 128, hT, [hTB], 8)
                f = c % 2
                P.op("act", lambda e, bt=bg[0], f=f: e.activation(out=ftt[f][:], in_=bt[:, :], func=AF.Tanh, scale=0.5),
                     reads=[bg[1]], writes=[fttB[f]])
                P.op("dve", lambda e, bt=bg[0], f=f: e.scalar_tensor_tensor(out=fa[f][:], in0=ftt[f][:], scalar=1.0,
                                                                           in1=bt[:, :], op0=ALU.add, op1=ALU.mult),
                     reads=[fttB[f], bg[1]], writes=[faB[f]])
                P.op("dve", lambda e, bt=bu[0], f=f, c=c: e.scalar_tensor_tensor(
                    out=slab[:, S_ACT + c, :], in0=fa[f][:], scalar=0.5, in1=bt[:, :], op0=ALU.mult, op1=ALU.mult),
                    reads=[faB[f], bu[1]], writes=[slabB[S_ACT + c]])
        mark(f'M{b}.ffnout')
        proj_res(wfo_s, "wfo", 22, slab[:, S_ACT:S_ACT + 22, :], slabB[S_ACT:S_ACT + 22], epi_add)

        mark(f'M{b}.norm3')
        norm_to_hT(2, None)
        mark(f'M{b}.ple')
        for t in range(4):
            i = t % 2
            P.dma("sp", lambda e, t=t, i=i, row0=row0: e.dma_start(out=p_f[i][:], in_=p_d[row0 + t * 128: row0 + (t + 1) * 128, :]),
                  psem[i], writes=[p_fB[i]])
            P.op("pool", lambda e, i=i: e.tensor_copy(out=p_bf[i][:], in_=p_f[i][:]), reads=[p_fB[i]],
                 writes=[p_bfB[i]])
            bt, bb = nb()
            btb = bt[:].bitcast(BF16)
            P.op("pe", [(lambda e, kc=kc, i=i, btb=btb: e.transpose(out=btb[:, kc * 128:(kc + 1) * 128],
                                                                    in_=p_bf[i][:, kc * 128:(kc + 1) * 128],
                                                                    identity=ident[:])) for kc in range(2)],
                 reads=[p_bfB[i], identB], writes=[bb])
            P.op("act", lambda e, t=t, btb=btb: e.activation(out=pT[:, :, t * 128:(t + 1) * 128],
                                                             in_=btb[:, 0:256].rearrange("p (k c) -> p k c", k=2),
                                                             func=AF.Copy), reads=[bb], writes=[pTB])

        def epi_gate(t, half, bank):
            bt, bb = bank
            P.op("act", lambda e: e.activation(out=esT[:, t, :], in_=bt[:, :], func=AF.Tanh, scale=0.5), reads=[bb],
                 writes=[tgtB[t]])

        def epi_ple(t, half, bank):
            bt, bb = bank
            hs = slice(half * 512, (half + 1) * 512)
            P.op("dve", lambda e: e.scalar_tensor_tensor(out=m1[:], in0=esT[:, t, :], scalar=1.0, in1=bt[:, :],
                                                         op0=ALU.add, op1=ALU.mult),
                 reads=[tgtB[t], bb], writes=[m1B])
            P.op("dve", lambda e: e.scalar_tensor_tensor(out=xres[t][:, hs], in0=m1[:], scalar=0.5, in1=xres[t][:, hs],
                                                         op0=ALU.mult, op1=ALU.add),
                 reads=[m1B, xresB[t]], writes=[xresB[t]])

        for half in range(2):
            tb = [nb() for _ in range(4)]
            for kg in range(2):
                w, wB = wload_std(wpgate_s, "wpgate", kg * 512, 4, half * 512, 512)
                for t in range(4):
                    bt, bb = tb[t]
                    P.op("pe", [(lambda e, kk=kk, t=t, bt=bt, w=w, kg=kg: e.matmul(
                        bt[:, :], lhsT=hT[:, kg * 4 + kk, t * 128:(t + 1) * 128], rhs=w[:, kk, :],
                        start=(kg * 4 + kk == 0), stop=(kg * 4 + kk == 7))) for kk in range(4)],
                        reads=[wB, hTB], writes=[bb])
            for t in range(4):
                epi_gate(t, half, tb[t])
            w, wB = wload_std(wple_s, "wple", 0, 2, half * 512, 512)
            for t in range(4):
                bt, bb = nb()
                P.op("pe", [(lambda e, kk=kk, t=t, bt=bt, w=w: e.matmul(bt[:, :], lhsT=pT[:, kk, t * 128:(t + 1) * 128],
                                                                        rhs=w[:, kk, :], start=(kk == 0),
                                                                        stop=(kk == 1))) for kk in range(2)],
                     reads=[wB, pTB], writes=[bb])
                epi_ple(t, half, (bt, bb))

        mark(f'M{b}.final')
        nxt_load = load_x(x_d, row0 + 512) if b + 1 < NT // 4 else None

        def out_dma(t):
            o = P.dma("sp", lambda e, t=t, row0=row0: e.dma_start(out=out_d[row0 + t * 128: row0 + (t + 1) * 128, :],
                                                              in_=ystage[t]), osem[t], reads=ystageB[t])
            if b == NT // 4 - 1:
                P.final.append(o)

        for t in range(4):
            P.op("act", lambda e, t=t: e.activation(out=junk_ap, in_=xres[t][:], func=AF.Square,
                                                   accum_out=stat[t][:, 0:1]),
                 reads=[xresB[t]], writes=[junkB, statB[t]])
            P.op("pool", lambda e, t=t: e.tensor_scalar(out=stat[t][:, 1:2], in0=stat[t][:, 0:1], scalar1=1.0 / D,
                                                       scalar2=EPS, op0=ALU.mult, op1=ALU.add),
                 reads=[statB[t]], writes=[statB[t]])
            P.op("pool", lambda e, t=t: e.tensor_tensor(out=stat[t][:, 2:3], in0=stat[t][:, 1:2], in1=mhalf[:, 0:1],
                                                       op=ALU.pow), reads=[statB[t], mhalfB], writes=[statB[t]])
            P.op("dve", lambda e, t=t: e.scalar_tensor_tensor(out=ystage[t], in0=xres[t][:],
                                                             scalar=stat[t][:, 2:3], in1=gfin[:], op0=ALU.mult,
                                                             op1=ALU.mult),
                 reads=[xresB[t], statB[t], gfinB], writes=ystageB[t])
            if nxt_load is not None:
                nxt_load(t)
            out_dma(t)

    mark('end')
    nc._marks = marks
    P.emit()
    return nc


def _t5_bucket(dist):
    max_exact = 16
    d_f = np.maximum(dist, max_exact).astype(np.float32)
    large = max_exact + (np.log(d_f / max_exact) / math.log(128 / max_exact) * (32 - max_exact)).astype(np.int32)
    large = np.minimum(large, 31)
    return np.where(dist < max_exact, dist, large)


def _consts():
    s = np.arange(128)[:, None]
    t = np.arange(128)[None, :]
    same = (s // 64) == (t // 64)
    cst = np.zeros((128, 386), np.float32)
    cst[:, 0:128] = np.where(same & (s <= t), -1.0 / 16, 0.0)
    cst[:, 128:256] = np.where(same & (s > t), -1.0 / 16, 0.0)
    cst[:, 256] = np.where(np.arange(128) < 64, -1.0 / 16, 0.0)
    cst[:, 257] = np.where(np.arange(128) >= 64, -1.0 / 16, 0.0)
    cst[:, 258:386] = np.where(same & (s <= t), 1.0, 0.0)
    k = np.arange(128)[:, None]
    q = np.arange(128)[None, :]
    maskT = np.zeros((2, 128, 128), np.float32)
    maskT[0] = np.where(k > q, 0.0, NEG)
    maskT[1] = np.where(k <= q, 0.0, NEG)
    bucket = np.zeros((2, 128, 128), np.int64)
    for kb in range(2):
        dist = q + 128 - (kb * 128 + k)
        bucket[kb] = _t5_bucket(np.maximum(dist, 0))
    return cst, maskT, bucket


def _host_inputs(x, p, w_in, w_gk2, b_gk, sinks, rel_table, w_proj_attn, w_proj_gla, gla_norm, w_out, norm_mix,
                 norm_ffn, w_ffn_in, w_ffn_out, norm_ple, w_ple_gate, w_ple, norm_final, T):
    f = lambda a: np.ascontiguousarray(np.asarray(a, dtype=np.float32))
    x, p = f(x), f(p)
    B, S, _ = x.shape
    halves = S // T
    cst, maskT, bucket = _consts()
    rel = f(rel_table)
    biasT = np.ascontiguousarray(np.transpose(rel[bucket], (0, 1, 3, 2))).reshape(2, 128, 1024)
    sk = f(sinks)[0]
    sink_l = np.zeros((128, 4), np.float32)
    sink_l[0:64, :] = sk[0:4][None, :]
    sink_l[64:128, :] = sk[4:8][None, :]
    gcols = np.concatenate([f(norm_mix)[0].reshape(8, 128).T, f(norm_ffn)[0].reshape(8, 128).T,
                            f(norm_ple)[0].reshape(8, 128).T], axis=1)
    gn_col = np.ascontiguousarray(f(gla_norm)[0].reshape(2, 128).T)
    wgk_aug = np.concatenate([f(w_gk2)[0], f(b_gk)[0][None, :]], axis=0)
    shared = {
        "w_in": f(w_in)[0], "w_proj_attn": f(w_proj_attn)[0], "w_proj_gla": f(w_proj_gla)[0], "w_out": f(w_out)[0],
        "w_ffn_in": f(w_ffn_in)[0], "w_ffn_out": f(w_ffn_out)[0], "w_ple_gate": f(w_ple_gate)[0],
        "w_ple": f(w_ple)[0], "wgk_aug": np.ascontiguousarray(wgk_aug), "gcols": np.ascontiguousarray(gcols),
        "gn_col": gn_col, "norm_final": f(norm_final), "sink_l": sink_l, "biasT": biasT, "maskT": maskT, "cst": cst,
    }
    maps = []
    for b in range(B):
        for h in range(halves):
            m = dict(shared)
            m["x"] = np.ascontiguousarray(x[b, h * T:(h + 1) * T])
            m["p"] = np.ascontiguousarray(p[0, b, h * T:(h + 1) * T])
            if h == 0:
                m["xp"] = np.zeros((T, D), np.float32)
                m["mask0"] = np.full((128, 128), NEG, np.float32)
            else:
                m["xp"] = np.ascontiguousarray(x[b, (h - 1) * T:h * T])
                m["mask0"] = np.zeros((128, 128), np.float32)
            maps.append(m)
    return maps, B, halves


def run(inputs, T):
    maps, B, halves = _host_inputs(T=T, **inputs)
    nc = build_nc(T // 128, T // 128)
    n = len(maps)
    res = run_bass_kernel_spmd(nc, maps, core_ids=list(range(n)))
    out = np.zeros((B, halves * T, D), np.float32)
    for i, r in enumerate(res.results):
        b, h = i // halves, i % halves
        out[b, h * T:(h + 1) * T] = r["out"]
    return out


def kernel(**inputs):
    return run(inputs, 4096)
```

```python
import math
import numpy as np
import concourse.bass as bass
import concourse.mybir as mybir
from concourse.bass_utils import run_bass_kernel_spmd

F32 = mybir.dt.float32
BF16 = mybir.dt.bfloat16
AF = mybir.ActivationFunctionType
ALU = mybir.AluOpType

D = 1024
D_IN = 5904
D_FF = 2816
EPS = 1e-6
NEG = -30000.0
C_QA, C_KA, C_VA, C_QG, C_KG, C_VG, C_GK, C_OG, C_GA, C_GG = 0, 512, 640, 768, 1280, 1792, 2816, 2832, 3856, 4880


class Buf:
    __slots__ = ("name", "last_w", "readers")

    def __init__(self, name):
        self.name = name
        self.last_w = None
        self.readers = []


class Op:
    __slots__ = ("eng", "fns", "deps", "sem", "val", "signal", "is_dma")

    def __init__(self, eng, fns):
        self.eng = eng
        self.fns = fns
        self.deps = []
        self.sem = None
        self.val = 0
        self.signal = False
        self.is_dma = False


class Prog:
    ENGS = ("pe", "act", "dve", "pool", "sp")

    def __init__(self, nc):
        self.nc = nc
        self.ops = {e: [] for e in self.ENGS}
        self.esem = {e: nc.alloc_semaphore("es_" + e) for e in self.ENGS}
        self.dma_cnt = {}
        self.nsem = 0
        self.final = []

    def buf(self, name):
        return Buf(name)

    def new_sem(self):
        self.nsem += 1
        return self.nc.alloc_semaphore(f"ds_{self.nsem}")

    def _track(self, op, reads, writes):
        deps = []
        for b in reads:
            if b.last_w is not None:
                deps.append(b.last_w)
        for b in writes:
            if b.last_w is not None:
                deps.append(b.last_w)
            deps.extend(b.readers)
        for b in reads:
            b.readers.append(op)
        for b in writes:
            b.last_w = op
            b.readers = []
        seen = set()
        for d in deps:
            if d is op or id(d) in seen:
                continue
            seen.add(id(d))
            if d.eng == "pe" and op.eng == "pe" and not d.is_dma and not op.is_dma:
                continue
            op.deps.append(d)
            d.signal = True

    def op(self, eng, fns, reads=(), writes=()):
        if callable(fns):
            fns = [fns]
        o = Op(eng, list(fns))
        self._track(o, reads, writes)
        self.ops[eng].append(o)
        return o

    def dma(self, eng, fns, sem, reads=(), writes=()):
        if callable(fns):
            fns = [fns]
        o = Op(eng, list(fns))
        o.is_dma = True
        o.sem = sem
        c = self.dma_cnt.get(id(sem), 0) + 16 * len(o.fns)
        self.dma_cnt[id(sem)] = c
        o.val = c
        self._track(o, reads, writes)
        self.ops[eng].append(o)
        return o

    def emit(self):
        nc = self.nc
        for e in self.ENGS:
            c = 0
            for o in self.ops[e]:
                if o.is_dma:
                    continue
                if o.signal:
                    c += 1
                    o.sem = self.esem[e]
                    o.val = c
        handles = {"pe": "tensor", "act": "scalar", "dve": "vector", "pool": "gpsimd", "sp": "sync"}
        final = self.final
        with nc.Block() as block:
            for e in self.ENGS:
                ops = self.ops[e]

                def body(eng, ops=ops, e=e):
                    waited = {}
                    for o in ops:
                        for d in o.deps:
                            k = id(d.sem)
                            if waited.get(k, 0) >= d.val:
                                continue
                            eng.wait_ge(d.sem, d.val)
                            waited[k] = d.val
                        n = len(o.fns)
                        for i, f in enumerate(o.fns):
                            ins = f(eng)
                            if o.is_dma:
                                ins.then_inc(o.sem, 16)
                            elif o.signal and i == n - 1:
                                ins.then_inc(o.sem, 1)
                    if e == "sp":
                        for d in final:
                            eng.wait_ge(d.sem, d.val)

                getattr(block, handles[e])(body)


def build_nc(NT, NTP, dbg=None):
    assert NT % 4 == 0 and NTP % 4 == 0
    nc = bass.Bass("TRN2", target_bir_lowering=False)
    P = Prog(nc)
    T, TP = NT * 128, NTP * 128

    def din(name, shape, dt=F32):
        return nc.dram_tensor(name, list(shape), dt, kind="ExternalInput").ap()

    x_d = din("x", [T, D])
    xp_d = din("xp", [TP, D])
    p_d = din("p", [T, 256])
    w_in_d = din("w_in", [D, D_IN])
    wpa_d = din("w_proj_attn", [512, D])
    wpg_d = din("w_proj_gla", [D, D])
    wout_d = din("w_out", [D, D])
    wfi_d = din("w_ffn_in", [D, 2 * D_FF])
    wfo_d = din("w_ffn_out", [D_FF, D])
    wpgate_d = din("w_ple_gate", [D, D])
    wple_d = din("w_ple", [256, D])
    wgk_d = din("wgk_aug", [17, 512])
    gcols_d = din("gcols", [128, 24])
    gn_d = din("gn_col", [128, 2])
    gfin_d = din("norm_final", [D])
    sinkl_d = din("sink_l", [128, 4])
    biasT_d = din("biasT", [2, 128, 1024])
    maskT_d = din("maskT", [2, 128, 128])
    mask0_d = din("mask0", [128, 128])
    cst_d = din("cst", [128, 386])
    out_d = nc.dram_tensor("out", [T, D], F32, kind="ExternalOutput").ap()

    def dscr(name, shape):
        return nc.dram_tensor(name, list(shape), BF16, kind="Internal").ap()

    wi_s = dscr("wi_s", [D, D_IN])
    wpa_s = dscr("wpa_s", [512, D])
    wpg_s = dscr("wpg_s", [D, D])
    wout_s = dscr("wout_s", [D, D])
    wfi_s = dscr("wfi_s", [D, 2 * D_FF])
    wfo_s = dscr("wfo_s", [D_FF, D])
    wpgate_s = dscr("wpgate_s", [D, D])
    wple_s = dscr("wple_s", [256, D])

    def sb(name, shape, dt=F32):
        return nc.alloc_sbuf_tensor("s_" + name, list(shape), dt)

    xres = [sb(f"xres{t}", [128, D]) for t in range(4)]
    xresB = [P.buf(f"xres{t}") for t in range(4)]
    xn = [sb(f"xn{i}", [128, D], BF16) for i in range(2)]
    xnB = [P.buf(f"xn{i}") for i in range(2)]
    stat = [sb(f"stat{t}", [128, 4]) for t in range(4)]
    statB = [P.buf(f"stat{t}") for t in range(4)]
    hT = sb("hT", [128, 8, 512], BF16)
    hTB = P.buf("hT")
    slab = sb("slab", [128, 40, 512], BF16)
    slabB = [P.buf(f"slab{i}") for i in range(40)]
    S_ACT, S_GSIL, S_OGN, S_MIX, S_QA, S_QG, S_KG, S_OA = 0, 0, 8, 16, 24, 28, 32, 36
    ka_pad = [sb(f"ka_pad{g}", [128, 640], BF16) for g in range(2)]
    kaB = [P.buf(f"ka_s{s}") for s in range(5)]
    va_pad = sb("va_pad", [128, 5, 2, 128], BF16)
    vaB = [P.buf(f"va_s{s}") for s in range(5)]
    kgTM = [sb(f"kgTM{t}", [128, 512], BF16) for t in range(4)]
    kgTMB = [P.buf(f"kgTM{t}") for t in range(4)]
    vgTM = [sb(f"vgTM{t}", [128, 1024], BF16) for t in range(4)]
    vgTMB = [P.buf(f"vgTM{t}") for t in range(4)]
    gkT = sb("gkT", [32, 512])
    gkTB = P.buf("gkT")
    sp_t2 = [sb(f"sp_t{i}", [128, 512]) for i in range(2)]; spB2 = [P.buf(f"sp{i}") for i in range(2)]
    ED2 = [sb(f"ED{i}", [128, 512]) for i in range(2)]; EDB2 = [P.buf(f"ED{i}") for i in range(2)]
    Epos2 = [sb(f"Epos{i}", [128, 512]) for i in range(2)]; EposB2 = [P.buf(f"Epos{i}") for i in range(2)]
    Eneg2 = [sb(f"Eneg{i}", [128, 512]) for i in range(2)]; EnegB2 = [P.buf(f"Eneg{i}") for i in range(2)]
    el2 = [sb(f"el{i}", [128, 8]) for i in range(2)]; elB2 = [P.buf(f"el{i}") for i in range(2)]
    qt2 = [sb(f"qt{i}", [128, 4, 128], BF16) for i in range(2)]; qtB2 = [P.buf(f"qt{i}") for i in range(2)]
    kt2 = [sb(f"kt{i}", [128, 4, 128], BF16) for i in range(2)]; ktB2 = [P.buf(f"kt{i}") for i in range(2)]
    kd2 = [[sb(f"kd{i}_{c}", [128, 512], BF16) for c in range(2)] for i in range(2)]
    kdB2 = [[P.buf(f"kd{i}_{c}") for c in range(2)] for i in range(2)]
    ATs = sb("ATs", [128, 4, 128], BF16); ATsB = P.buf("ATs")
    sq = sb("sq", [128, 8, 128], BF16); sqB = P.buf("sq")
    junk_ap = sq[:].rearrange("p a b -> p (a b)")
    junkB = sqB
    rstd_g = sb("rstd_g", [128, 4, 128]); rstdgB = P.buf("rstd_g")
    otmp = sb("otmp", [128, 8, 128]); otmpB = P.buf("otmp")
    S_st = sb("S_st", [128, 1024]); SB_ = P.buf("S_st")
    Sbf = [sb(f"Sbf{i}", [128, 1024], BF16) for i in range(2)]
    SbfB = [P.buf(f"Sbf{i}") for i in range(2)]
    sT = [sb(f"sT{i}", [128, 512]) for i in range(2)]
    sTB = [P.buf(f"sT{i}") for i in range(2)]
    esT = sb("esT", [128, 4, 512], BF16)
    esTB = [P.buf(f"esT{i}") for i in range(4)]
    biasT8 = sb("biasT8", [128, 2, 1024]); biasB = P.buf("biasT8")
    mask0 = sb("mask0", [128, 128]); mask0B = P.buf("mask0")
    bias_hi = sb("bias_hi", [128, 2, 1024], BF16); bias_lo = sb("bias_lo", [128, 2, 1024], BF16)
    biasHLB = P.buf("bias_hl")
    mask0_bf = sb("mask0_bf", [128, 128], BF16); mask0bB = P.buf("mask0_bf")
    maskT = sb("maskT", [128, 2, 128]); maskTB = P.buf("maskT")
    esink = sb("esink", [128, 4]); esinkB = P.buf("esink")
    ta = sb("ta", [128, 512], BF16); taB = P.buf("ta")
    tg = sb("tg", [128, 512], BF16); tgB = P.buf("tg")
    m1 = sb("m1", [128, 512]); m1B = P.buf("m1")
    m2 = sb("m2", [128, 512]); m2B = P.buf("m2")
    lnv, lnvB = m1, m1B
    dtmp, dtmpB = m2, m2B
    ftt = [sb(f"ftt{i}", [128, 512]) for i in range(2)]
    fttB = [P.buf(f"ftt{i}") for i in range(2)]
    fa = [sb(f"fa{i}", [128, 512]) for i in range(2)]
    faB = [P.buf(f"fa{i}") for i in range(2)]
    p_f = [sb(f"p_f{i}", [128, 256]) for i in range(2)]
    p_fB = [P.buf(f"p_f{i}") for i in range(2)]
    p_bf = [sb(f"p_bf{i}", [128, 256], BF16) for i in range(2)]
    p_bfB = [P.buf(f"p_bf{i}") for i in range(2)]
    pT = sb("pT", [128, 2, 512], BF16); pTB = P.buf("pT")
    tgtB = esTB
    gfin = sb("gfin", [128, D]); gfinB = P.buf("gfin")
    gcols = sb("gcols", [128, 24]); gcolsB = P.buf("gcols")
    gnh = sb("gnh", [128, 2]); gnhB = P.buf("gnh")
    cst = sb("cst", [128, 386]); cstB = P.buf("cst")
    wgk = sb("wgk", [32, 512]); wgkB = P.buf("wgk")
    ident = sb("ident", [128, 128], BF16); identB = P.buf("ident")
    ones_bf = sb("ones_bf", [128, 128], BF16); onesB = P.buf("ones")
    ones_pad = sb("ones_pad", [128, 2, 128], BF16); onespB = P.buf("ones_pad")
    mhalf = sb("mhalf", [128, 1]); mhalfB = P.buf("mhalf")
    NSLOT = 6
    ring = [(sb(f"ws{i}", [128, 2048], BF16), P.buf(f"ws{i}"), P.new_sem()) for i in range(NSLOT)]
    ring_i = [0]
    NBANK = 8
    banks = [(nc.alloc_psum_tensor(f"pb{i}", [128, 512], F32), P.buf(f"pb{i}")) for i in range(NBANK)]
    bank_i = [0]
    marks = []

    def mark(name):
        marks.append((name, sum(len(o.fns) for o in P.ops['pe'])))

    pool_i = {}

    def nb(pool=None):
        if pool is None:
            r = banks[bank_i[0] % NBANK]
            bank_i[0] += 1
            return r
        i = pool_i.get(pool, 0)
        pool_i[pool] = i + 1
        return banks[pool[i % len(pool)]]

    PGLA2, PATT, PPROJ = ((0, 1, 2), (3, 4, 5)), (6, 7), (6, 7)

    U2m = cst[:, 0:128]
    SU2m = cst[:, 128:256]
    ind2m = cst[:, 256:258]
    maskA = cst[:, 258:386]

    sem_c = [P.new_sem() for _ in range(12)]
    P.dma("sp", lambda e: e.dma_start(out=cst[:], in_=cst_d), sem_c[0], writes=[cstB])
    P.dma("sp", lambda e: e.dma_start(out=gcols[:], in_=gcols_d), sem_c[1], writes=[gcolsB])
    P.dma("sp", lambda e: e.dma_start(out=gnh[:], in_=gn_d), sem_c[2], writes=[gnhB])
    P.dma("sp", lambda e: e.dma_start(out=gfin[:], in_=gfin_d.partition_broadcast(128)), sem_c[3], writes=[gfinB])
    P.dma("sp", lambda e: e.dma_start(out=esink[:], in_=sinkl_d), sem_c[4], writes=[esinkB])
    P.dma("sp", lambda e: e.dma_start(out=wgk[0:17, :], in_=wgk_d), sem_c[5], writes=[wgkB])
    P.dma("sp", lambda e: e.dma_start(out=mask0[:], in_=mask0_d), sem_c[6], writes=[mask0B])
    P.dma("sp", lambda e: e.dma_start(out=maskT[:], in_=maskT_d.rearrange("b k q -> k b q")), sem_c[7],
          writes=[maskTB])
    P.dma("sp", lambda e: e.dma_start(out=biasT8[:], in_=biasT_d.rearrange("b k c -> k b c")), sem_c[8],
          writes=[biasB])

    def flat(ap, n):
        return ap.rearrange("a b -> (a b)").rearrange("(n c) -> n c", c=2048)

    castB = {}

    def cast(name, src, dst, nrows_flat, pieces=1):
        b = P.buf("scr_" + name)
        castB[name] = b
        fs, fd = flat(src, nrows_flat), flat(dst, nrows_flat)
        per = nrows_flat // pieces
        fns = []
        for i in range(pieces):
            r0, r1 = i * per, (nrows_flat if i == pieces - 1 else (i + 1) * per)
            fns.append(lambda e, r0=r0, r1=r1: e.dma_start(out=fd[r0:r1, :], in_=fs[r0:r1, :]))
        P.dma("pool", fns, P.new_sem(), writes=[b])

    def cast_cols(name, c0, c1):
        b = P.buf("scr_" + name)
        castB[name] = [b]
        P.dma("pool", lambda e: e.dma_start(out=wi_s[:, c0:c1], in_=w_in_d[:, c0:c1]), P.new_sem(), writes=[b])

    pending_casts = []
    cast_cols("wiA", C_KG, C_OG)
    for nm_, a_, b_, np_ in (("wiB", 0, C_KG, 2), ("wiC", C_OG, C_GG, 4), ("wiD", C_GG, D_IN, 2)):
        castB[nm_] = [P.buf(f"scr_{nm_}{i}") for i in range(np_)]
        for i in range(np_):
            r0, r1 = i * (D // np_), (i + 1) * (D // np_)
            pending_casts.append(lambda nm_=nm_, a_=a_, b_=b_, i=i, r0=r0, r1=r1: P.dma(
                "pool", lambda e: e.dma_start(out=wi_s[r0:r1, a_:b_], in_=w_in_d[r0:r1, a_:b_]), P.new_sem(),
                writes=[castB[nm_][i]]))

    def wi_name(c0):
        return "wiB" if c0 < C_KG else ("wiA" if c0 < C_OG else ("wiC" if c0 < C_GG else "wiD"))
    P.op("pool", lambda e: e.memset(ident[:], 1.0), writes=[identB])
    P.op("pool", lambda e: e.affine_select(out=ident[:], in_=ident[:], pattern=[[-1, 128]], compare_op=ALU.is_equal,
                                           fill=0.0, base=0, channel_multiplier=1), reads=[identB], writes=[identB])
    P.op("pool", lambda e: e.memset(ones_bf[:], 1.0), writes=[onesB])
    P.op("pool", [lambda e: e.memset(ones_pad[:], 0.0)], writes=[onespB])
    P.op("pool", [lambda e: e.memset(ones_pad[:, 0, 0:64], 1.0), lambda e: e.memset(ones_pad[:, 1, 64:128], 1.0)],
         reads=[onespB], writes=[onespB])
    P.op("pool", lambda e: e.memset(mhalf[:], -0.5), writes=[mhalfB])
    P.op("pool", [lambda e: e.memset(ka_pad[0][:], 0.0), lambda e: e.memset(ka_pad[1][:], 0.0),
                  lambda e: e.memset(va_pad[:], 0.0)], writes=kaB + vaB)
    P.op("pool", [lambda e, i=i, c=c: e.memset(kd2[i][c][:], 0.0) for i in range(2) for c in range(2)],
         writes=kdB2[0] + kdB2[1])
    P.op("pool", lambda e: e.memset(gkT[:], 1.0), writes=[gkTB])
    P.op("pool", lambda e: e.memset(S_st[:], 0.0), writes=[SB_])
    P.op("pool", lambda e: e.memset(wgk[:], 0.0), writes=[wgkB]) if False else None
    P.op("act", lambda e: e.activation(out=esink[:], in_=esink[:], func=AF.Exp), reads=[esinkB], writes=[esinkB])
    for kb in range(2):
        P.op("dve", lambda e, kb=kb: e.tensor_tensor(
            out=biasT8[:, kb, :].rearrange("p (h q) -> p h q", h=8),
            in0=biasT8[:, kb, :].rearrange("p (h q) -> p h q", h=8),
            in1=maskT[:, kb, :].unsqueeze(1).broadcast_to([128, 8, 128]), op=ALU.add),
            reads=[biasB, maskTB], writes=[biasB])
    P.op("dve", lambda e: e.tensor_scalar(out=biasT8[:], in0=biasT8[:], scalar1=8.0, scalar2=None, op0=ALU.mult),
         reads=[biasB], writes=[biasB])
    P.op("dve", lambda e: e.tensor_scalar(out=mask0[:], in0=mask0[:], scalar1=8.0, scalar2=None, op0=ALU.mult),
         reads=[mask0B], writes=[mask0B])
    P.op("dve", lambda e: e.tensor_copy(out=mask0_bf[:], in_=mask0[:]), reads=[mask0B], writes=[mask0bB])
    P.op("dve", lambda e: e.tensor_copy(out=bias_hi[:], in_=biasT8[:]), reads=[biasB], writes=[biasHLB])
    P.op("dve", lambda e: e.tensor_tensor(out=biasT8[:], in0=biasT8[:], in1=bias_hi[:], op=ALU.subtract),
         reads=[biasB, biasHLB], writes=[biasB])
    P.op("dve", lambda e: e.tensor_copy(out=bias_lo[:], in_=biasT8[:]), reads=[biasB, biasHLB], writes=[biasHLB])
    def defer_cast(name, src, dst, nrows_flat, pieces=1):
        castB[name] = [P.buf(f"scr_{name}{i}") for i in range(pieces)]
        fs, fd = flat(src, nrows_flat), flat(dst, nrows_flat)
        per = nrows_flat // pieces
        for i in range(pieces):
            r0, r1 = i * per, (nrows_flat if i == pieces - 1 else (i + 1) * per)
            pending_casts.append(lambda name=name, i=i, r0=r0, r1=r1, fs=fs, fd=fd: P.dma(
                "pool", lambda e: e.dma_start(out=fd[r0:r1, :], in_=fs[r0:r1, :]), P.new_sem(),
                writes=[castB[name][i]]))

    castS = {}
    defer_cast("wpa", wpa_d, wpa_s, 512 * D // 2048)
    defer_cast("wpg", wpg_d, wpg_s, D * D // 2048, pieces=2)
    defer_cast("wout", wout_d, wout_s, D * D // 2048, pieces=2)
    defer_cast("wfi", wfi_d, wfi_s, D * 2 * D_FF // 2048, pieces=10)
    defer_cast("wfo", wfo_d, wfo_s, D_FF * D // 2048, pieces=5)
    defer_cast("wpgate", wpgate_d, wpgate_s, D * D // 2048, pieces=2)
    defer_cast("wple", wple_d, wple_s, 256 * D // 2048)

    def wload(srcs, name, nk, ncols):
        t, b, s = ring[ring_i[0] % NSLOT]
        ring_i[0] += 1
        fns = [(lambda e, dv=dv, sa=sa, t=t: e.dma_start(out=dv(t), in_=sa)) for dv, sa in srcs]
        P.dma("sp", fns, s, reads=castB[name], writes=[b])
        return t, b

    def wload_std(scr, name, r0, nk, c0, ncols):
        src = scr[r0:r0 + nk * 128, c0:c0 + ncols].rearrange("(k p) c -> p k c", p=128)
        t, b = wload([(lambda t: t[:, 0:nk * ncols].rearrange("p (k c) -> p k c", k=nk), src)], name, nk, ncols)
        return t[:, 0:nk * ncols].rearrange("p (k c) -> p k c", k=nk), b

    evac_i = [0]

    def evac_copy(out_ap, in_ap, reads, writes, eng=None):
        if eng is None:
            eng = "act" if evac_i[0] % 2 == 0 else "dve"
            evac_i[0] += 1
        if eng == "act":
            P.op("act", lambda e: e.activation(out=out_ap, in_=in_ap, func=AF.Copy), reads=reads, writes=writes)
        else:
            P.op("dve", lambda e: e.tensor_copy(out=out_ap, in_=in_ap), reads=reads, writes=writes)

    def fm_mm(bank, w, wB, ccol, act_view, actB, nk):
        bt, bb = bank
        fns = [(lambda e, kc=kc: e.matmul(bt[:, :], lhsT=w[:, kc, ccol:ccol + 128], rhs=act_view[:, kc, :],
                                          start=(kc == 0), stop=(kc == nk - 1))) for kc in range(nk)]
        P.op("pe", fns, reads=[wB] + list(actB), writes=[bb])

    def norm_gen(gidx, load_fn, dst=None, pool=None):
        hTc, hTBc = dst if dst is not None else (hT, [hTB])
        for t in range(4):
            if load_fn is not None:
                load_fn(t)
            P.op("act", lambda e, t=t: e.activation(out=junk_ap, in_=xres[t][:], func=AF.Square,
                                                   accum_out=stat[t][:, 0:1]),
                 reads=[xresB[t]], writes=[junkB, statB[t]])
            P.op("pool", lambda e, t=t: e.tensor_scalar(out=stat[t][:, 1:2], in0=stat[t][:, 0:1], scalar1=1.0 / D,
                                                       scalar2=EPS, op0=ALU.mult, op1=ALU.add),
                 reads=[statB[t]], writes=[statB[t]])
            P.op("pool", lambda e, t=t: e.tensor_tensor(out=stat[t][:, 2:3], in0=stat[t][:, 1:2], in1=mhalf[:, 0:1],
                                                       op=ALU.pow),
                 reads=[statB[t], mhalfB], writes=[statB[t]])
            i = t % 2
            P.op("act", lambda e, t=t, i=i: e.activation(out=xn[i][:], in_=xres[t][:], func=AF.Identity,
                                                        scale=stat[t][:, 2:3]),
                 reads=[xresB[t], statB[t]], writes=[xnB[i]])
            bt, bb = nb(pool)
            btb = bt[:].bitcast(BF16)
            P.op("pe", [(lambda e, kc=kc, i=i, btb=btb: e.transpose(out=btb[:, kc * 128:(kc + 1) * 128],
                                                                    in_=xn[i][:, kc * 128:(kc + 1) * 128],
                                                                    identity=ident[:])) for kc in range(8)],
                 reads=[xnB[i], identB], writes=[bb])
            P.op("dve", lambda e, t=t, btb=btb: e.tensor_tensor(
                out=hTc[:, :, t * 128:(t + 1) * 128], in0=btb.rearrange("p (k c) -> p k c", k=8),
                in1=gcols[:, gidx * 8:(gidx + 1) * 8].unsqueeze(2).broadcast_to([128, 8, 128]), op=ALU.mult),
                reads=[bb, gcolsB], writes=hTBc)
            yield

    def norm_to_hT(gidx, load_fn, dst=None):
        for _ in norm_gen(gidx, load_fn, dst):
            pass


    ystage = [slab[:, 4 * i:4 * i + 4, :].rearrange("p a b -> p (a b)").bitcast(F32) for i in range(4)]
    ystageB = [slabB[4 * i:4 * i + 4] for i in range(4)]
    xsem = [P.new_sem() for _ in range(4)]
    osem = [P.new_sem() for _ in range(4)]

    def load_x(src, row0):
        def f(t):
            P.dma("sp", lambda e: e.dma_start(out=xres[t][:], in_=src[row0 + t * 128: row0 + (t + 1) * 128, :]),
                  xsem[t], writes=[xresB[t]])
        return f

    def proj_tm(c0, ncols, dst_fn, dstB_fn):
        w, wB = wload_std(wi_s, wi_name(c0), 0, 8, c0, ncols)
        for t in range(4):
            bt, bb = nb()
            P.op("pe", [(lambda e, kc=kc, t=t, bt=bt: e.matmul(bt[:, 0:ncols], lhsT=hT[:, kc, t * 128:(t + 1) * 128],
                                                               rhs=w[:, kc, :], start=(kc == 0), stop=(kc == 7)))
                        for kc in range(8)], reads=[wB, hTB], writes=[bb])
            evac_copy(dst_fn(t), bt[:, 0:ncols], [bb], [dstB_fn(t)])

    def proj_kv_a(src=None):
        hTc, hTBc = src if src is not None else (hT, [hTB])
        w, wB = wload_std(wi_s, "wiB", 0, 8, C_KA, 256)
        bank = nb()
        fm_mm(bank, w, wB, 0, hTc, hTBc, 8)
        bt, bb = bank
        P.op("act", [lambda e: e.activation(out=ka_pad[0][0:64, 128:640], in_=bt[0:64, :], func=AF.Copy),
                     lambda e: e.activation(out=ka_pad[1][64:128, 128:640], in_=bt[64:128, :], func=AF.Copy)],
             reads=[bb], writes=kaB[1:5])
        for t in range(4):
            bt2, bb2 = nb()
            P.op("pe", [(lambda e, kc=kc, t=t, bt2=bt2: e.matmul(bt2[:, 0:128], lhsT=hTc[:, kc, t * 128:(t + 1) * 128],
                                                                 rhs=w[:, kc, 128:256], start=(kc == 0),
                                                                 stop=(kc == 7))) for kc in range(8)],
                 reads=[wB] + hTBc, writes=[bb2])
            P.op("act", [lambda e, t=t, bt2=bt2: e.activation(out=va_pad[:, t + 1, 0, 0:64], in_=bt2[:, 0:64],
                                                              func=AF.Copy),
                         lambda e, t=t, bt2=bt2: e.activation(out=va_pad[:, t + 1, 1, 64:128], in_=bt2[:, 64:128],
                                                              func=AF.Copy)],
                 reads=[bb2], writes=[vaB[t + 1]])

    def carry_kv():
        P.op("pool", [lambda e: e.tensor_copy(out=ka_pad[0][:, 0:128], in_=ka_pad[0][:, 512:640]),
                      lambda e: e.tensor_copy(out=ka_pad[1][:, 0:128], in_=ka_pad[1][:, 512:640]),
                      lambda e: e.tensor_copy(out=va_pad[:, 0], in_=va_pad[:, 4])],
             reads=[kaB[4], vaB[4]], writes=[kaB[0], vaB[0]])

    def proj_gla_tm():
        for i in range(2):
            proj_tm(C_KG + i * 256, 256, lambda t, i=i: kgTM[t][:, i * 256:(i + 1) * 256], lambda t: kgTMB[t])
        for i in range(4):
            proj_tm(C_VG + i * 256, 256, lambda t, i=i: vgTM[t][:, i * 256:(i + 1) * 256], lambda t: vgTMB[t])

    def proj_gk(src=None):
        hTc, hTBc = src if src is not None else (hT, [hTB])
        w, wB = wload_std(wi_s, "wiA", 0, 8, C_GK, 16)
        bt, bb = nb()
        P.op("pe", [(lambda e, kc=kc: e.matmul(bt[0:16, :], lhsT=w[:, kc, 0:16], rhs=hTc[:, kc, :], start=(kc == 0),
                                               stop=(kc == 7))) for kc in range(8)], reads=[wB] + hTBc, writes=[bb])
        P.op("act", lambda e: e.activation(out=gkT[0:16, :], in_=bt[0:16, :], func=AF.Copy), reads=[bb],
             writes=[gkTB])

    def prefix_proj_gen(src):
        hTc, hTBc = src
        proj_gk(src)
        yield
        groups = []
        for i in range(2):
            groups.append((wload_std(wi_s, "wiA", 0, 8, C_KG + i * 256, 256), 0, i))
        for i in range(4):
            groups.append((wload_std(wi_s, "wiA", 0, 8, C_VG + i * 256, 256), 1, i))
        for t in range(4):
            for (w, wB), kind, i in groups:
                bt, bb = nb(PPROJ)
                P.op("pe", [(lambda e, kc=kc, t=t, bt=bt, w=w: e.matmul(bt[:, 0:256],
                                                                        lhsT=hTc[:, kc, t * 128:(t + 1) * 128],
                                                                        rhs=w[:, kc, :], start=(kc == 0),
                                                                        stop=(kc == 7))) for kc in range(8)],
                     reads=[wB] + hTBc, writes=[bb])
                if kind == 0:
                    evac_copy(kgTM[t][:, i * 256:(i + 1) * 256], bt[:, 0:256], [bb], [kgTMB[t]])
                else:
                    evac_copy(vgTM[t][:, i * 256:(i + 1) * 256], bt[:, 0:256], [bb], [vgTMB[t]])
            yield

    def og_gen():
        for i in range(4):
            w, wB = wload_std(wi_s, "wiC", 0, 8, C_OG + i * 256, 256)
            for cc in range(2):
                idx = i * 2 + cc
                bank = nb()
                fm_mm(bank, w, wB, cc * 128, hT, [hTB], 8)
                bt, bb = bank
                f = idx % 2
                P.op("act", lambda e, bt=bt, f=f: e.activation(out=ftt[f][:], in_=bt[:, :], func=AF.Exp, scale=-1.0),
                     reads=[bb], writes=[fttB[f]])
                P.op("act", lambda e, f=f: e.activation(out=ftt[f][:], in_=ftt[f][:], func=AF.Ln, bias=1.0),
                     reads=[fttB[f]], writes=[fttB[f]])
                P.op("act", lambda e, f=f: e.activation(out=ftt[f][:], in_=ftt[f][:], func=AF.Exp, scale=-1.0),
                     reads=[fttB[f]], writes=[fttB[f]])
                P.op("dve", lambda e, bt=bt, f=f, idx=idx: e.scalar_tensor_tensor(
                    out=slab[:, S_GSIL + idx, :], in0=bt[:, :], scalar=gnh[:, idx % 2:idx % 2 + 1], in1=ftt[f][:],
                    op0=ALU.mult, op1=ALU.mult), reads=[bb, gnhB, fttB[f]], writes=[slabB[S_GSIL + idx]])
                yield

    state_par = [0]

    def gla_tile(t, full):
        tok = slice(t * 128, (t + 1) * 128)
        par = t % 2
        PGLA = PGLA2[par]
        sp_t, spB, ED, EDB = sp_t2[par], spB2[par], ED2[par], EDB2[par]
        Epos, EposB, Eneg, EnegB = Epos2[par], EposB2[par], Eneg2[par], EnegB2[par]
        qt, qtB, kt, ktB, kd, kdB = qt2[par], qtB2[par], kt2[par], ktB2[par], kd2[par], kdB2[par]
        el, elB = el2[par], elB2[par]
        zt, zb = nb(PGLA)
        P.op("pe", lambda e: e.matmul(zt[:, :], lhsT=gkT[0:17, tok], rhs=wgk[0:17, :], start=True, stop=True),
             reads=[gkTB, wgkB], writes=[zb])
        P.op("act", lambda e: e.activation(out=sp_t[:], in_=zt[:, :], func=AF.Exp, scale=-1.0), reads=[zb],
             writes=[spB])
        P.op("act", lambda e: e.activation(out=sp_t[:], in_=sp_t[:], func=AF.Ln, bias=1.0), reads=[spB],
             writes=[spB])
        yield
        dt_, db = nb(PGLA)
        P.op("pe", lambda e: e.matmul(dt_[:, :], lhsT=SU2m, rhs=sp_t[:], start=True, stop=True),
             reads=[cstB, spB], writes=[db])
        P.op("act", lambda e: e.activation(out=ED[:], in_=dt_[:, :], func=AF.Exp), reads=[db], writes=[EDB])
        if not full:
            et, eb = nb(PGLA)
            P.op("pe", [(lambda e, h=h: e.matmul(et[:, h * 2:(h + 1) * 2], lhsT=sp_t[:, h * 128:(h + 1) * 128],
                                                 rhs=ind2m, start=True, stop=True)) for h in range(4)],
                 reads=[cstB, spB], writes=[eb])
            P.op("act", lambda e: e.activation(out=el[:], in_=et[:, 0:8], func=AF.Exp), reads=[eb], writes=[elB])
        if full:
            gt, gb = nb(PGLA)
            P.op("pe", [(lambda e, h=h: e.matmul(gt[:, h * 128:(h + 1) * 128], lhsT=sp_t[:, h * 128:(h + 1) * 128],
                                                 rhs=U2m, start=True, stop=True)) for h in range(4)],
                 reads=[cstB, spB], writes=[gb])
            P.op("act", lambda e: e.activation(out=Epos[:], in_=gt[:, :], func=AF.Exp), reads=[gb], writes=[EposB])
            P.op("act", lambda e: e.activation(out=Eneg[:], in_=gt[:, :], func=AF.Exp, scale=-1.0), reads=[gb],
                 writes=[EnegB])
            yield
            P.op("dve", lambda e: e.scalar_tensor_tensor(
                out=qt[:], in0=slab[:, S_QG:S_QG + 4, tok], scalar=128.0 ** -0.5,
                in1=Epos[:].rearrange("p (h c) -> p h c", h=4), op0=ALU.mult, op1=ALU.mult),
                reads=slabB[S_QG:S_QG + 4] + [EposB], writes=[qtB])
            P.op("dve", lambda e: e.tensor_tensor(
                out=kt[:], in0=slab[:, S_KG:S_KG + 4, tok], in1=Eneg[:].rearrange("p (h c) -> p h c", h=4),
                op=ALU.mult), reads=slabB[S_KG:S_KG + 4] + [EnegB], writes=[ktB])
        P.op("pool", lambda e: e.tensor_tensor(out=kd[0][0:64, :], in0=kgTM[t][0:64, :], in1=ED[0:64, :], op=ALU.mult),
             reads=[kgTMB[t], EDB], writes=[kdB[0]])
        P.op("pool", lambda e: e.tensor_tensor(out=kd[1][64:128, :], in0=kgTM[t][64:128, :], in1=ED[64:128, :],
                                               op=ALU.mult), reads=[kgTMB[t], EDB], writes=[kdB[1]])
        if not full and pending_casts:
            pending_casts.pop(0)()
        yield
        if full:
            at, ab = nb(PGLA)
            P.op("pe", [(lambda e, h=h: e.matmul(at[:, h * 128:(h + 1) * 128], lhsT=kt[:, h, :], rhs=qt[:, h, :],
                                                 start=True, stop=True)) for h in range(4)],
                 reads=[ktB, qtB], writes=[ab])
            P.op("dve", lambda e: e.tensor_tensor(out=ATs[:], in0=at[:, :].rearrange("p (h c) -> p h c", h=4),
                                                  in1=maskA.unsqueeze(1).broadcast_to([128, 4, 128]), op=ALU.mult),
                 reads=[ab, cstB], writes=[ATsB])
        sps = [None, None]

        def sps_mm(c):
            pair = []
            for hp in range(2):
                st, sbb = nb(PGLA)
                P.op("pe", [(lambda e, hh=hh, hp=hp, c=c, st=st: e.matmul(
                    st[:, hh * 256:(hh + 1) * 256], lhsT=kd[c][:, (hp * 2 + hh) * 128:(hp * 2 + hh + 1) * 128],
                    rhs=vgTM[t][:, (hp * 2 + hh) * 256:(hp * 2 + hh + 1) * 256], start=True, stop=True))
                    for hh in range(2)], reads=[kdB[c], vgTMB[t]], writes=[sbb])
                pair.append((st, sbb))
            sps[c] = pair

        def upd(c):
            for h in range(4):
                st, sbb = sps[c][h // 2]
                if full:
                    ci = h * 128 + c * 64 + 63
                    sc, scB = Epos[:, ci:ci + 1], EposB
                else:
                    sc, scB = el[:, h * 2 + c:h * 2 + c + 1], elB
                P.op("dve", lambda e, h=h, st=st, c=c, sc=sc: e.scalar_tensor_tensor(
                    out=S_st[:, h * 256:(h + 1) * 256], in0=S_st[:, h * 256:(h + 1) * 256],
                    scalar=sc, in1=st[:, (h % 2) * 256:(h % 2 + 1) * 256],
                    op0=ALU.mult, op1=ALU.add), reads=[SB_, scB, sbb], writes=[SB_])

        pa = state_par[0]
        if not full:
            sps_mm(0)
            upd(0)
            sps_mm(1)
            upd(1)
            yield
            return
        sps_mm(0)
        upd(0)
        P.op("act", lambda e: e.activation(out=Sbf[1 - pa][:], in_=S_st[:], func=AF.Copy), reads=[SB_],
             writes=[SbfB[1 - pa]])
        yield
        sps_mm(1)
        upd(1)
        obanks = [nb(PGLA), nb(PGLA)]
        for idx in range(8):
            h, dvc = idx // 2, idx % 2
            ot, ob = obanks[idx // 4]
            oc = (idx % 4) * 128
            col = h * 256 + dvc * 128
            P.op("pe", [
                lambda e, ot=ot, oc=oc, col=col, h=h: e.matmul(ot[:, oc:oc + 128], lhsT=vgTM[t][:, col:col + 128],
                                                               rhs=ATs[:, h, :], start=True, stop=False),
                lambda e, ot=ot, oc=oc, col=col, h=h: e.matmul(ot[:, oc:oc + 64], lhsT=Sbf[pa][:, col:col + 128],
                                                               rhs=qt[:, h, 0:64], start=False, stop=False),
                lambda e, ot=ot, oc=oc, col=col, h=h: e.matmul(ot[:, oc + 64:oc + 128],
                                                               lhsT=Sbf[1 - pa][:, col:col + 128],
                                                               rhs=qt[:, h, 64:128], start=False, stop=True)],
                reads=[vgTMB[t], ATsB, SbfB[0], SbfB[1], qtB], writes=[ob])
        P.op("act", lambda e: e.activation(out=Sbf[pa][:], in_=S_st[:], func=AF.Copy), reads=[SB_],
             writes=[SbfB[pa]])
        for k in range(2):
            ot, ob = obanks[k]
            P.op("act", lambda e, ot=ot, k=k: e.activation(out=sq[:, k * 4:(k + 1) * 4, :],
                                                           in_=ot[:, :].rearrange("p (i c) -> p i c", i=4),
                                                           func=AF.Square), reads=[ob], writes=[sqB])
        yield
        st_, sb_ = nb(PGLA)
        sqv = sq[:].rearrange("p (h v) c -> p h v c", v=2)
        P.op("pe", [(lambda e, v=v: e.matmul(st_[:, :], lhsT=ones_bf[:], rhs=sqv[:, :, v, :], start=(v == 0),
                                             stop=(v == 1))) for v in range(2)], reads=[sqB, onesB], writes=[sb_])
        P.op("act", lambda e: e.activation(out=lnv[:], in_=st_[:, :], func=AF.Ln, scale=1.0 / 256, bias=EPS),
             reads=[sb_], writes=[lnvB])
        P.op("act", lambda e: e.activation(out=rstd_g[:].rearrange("p h c -> p (h c)"), in_=lnv[:], func=AF.Exp,
                                           scale=-0.5), reads=[lnvB], writes=[rstdgB])
        for k in range(2):
            ot, ob = obanks[k]
            P.op("dve", lambda e, ot=ot, k=k: e.tensor_tensor(
                out=otmp[:, k * 4:(k + 1) * 4, :].rearrange("p (h v) c -> p h v c", v=2),
                in0=ot[:, :].rearrange("p (h v c) -> p h v c", h=2, v=2),
                in1=rstd_g[:, k * 2:(k + 1) * 2, :].unsqueeze(2).broadcast_to([128, 2, 2, 128]), op=ALU.mult),
                reads=[ob, rstdgB], writes=[otmpB])
        P.op("pool", lambda e: e.tensor_tensor(out=slab[:, S_OGN:S_OGN + 8, tok], in0=otmp[:],
                                               in1=slab[:, S_GSIL:S_GSIL + 8, tok], op=ALU.mult),
             reads=[otmpB] + slabB[S_GSIL:S_GSIL + 8], writes=slabB[S_OGN:S_OGN + 8])
        yield

    def attn_tile(t, first):
        tok = slice(t * 128, (t + 1) * 128)
        k = 0
        for g in range(2):
            for kb in range(2):
                bt, bb = nb(PATT)
                s_ = t + kb
                add_m0 = first and kb == 0
                fns = [lambda e, bt=bt, g=g, s_=s_: e.matmul(bt[:, :], lhsT=ka_pad[g][:, s_ * 128:(s_ + 1) * 128],
                                                             rhs=slab[:, S_QA:S_QA + 4, tok], start=True, stop=False),
                       lambda e, bt=bt, g=g, kb=kb: e.matmul(bt[:, :], lhsT=ident[:],
                                                             rhs=bias_hi[:, kb, g * 512:(g + 1) * 512], start=False,
                                                             stop=False),
                       lambda e, bt=bt, g=g, kb=kb, add_m0=add_m0: e.matmul(
                           bt[:, :], lhsT=ident[:], rhs=bias_lo[:, kb, g * 512:(g + 1) * 512], start=False,
                           stop=(not add_m0))]
                if add_m0:
                    fns.append(lambda e, bt=bt: e.matmul(bt[:, :], lhsT=ident[:],
                                                         rhs=mask0_bf[:].unsqueeze(1).broadcast_to([128, 4, 128]),
                                                         start=False, stop=True))
                P.op("pe", fns, reads=[kaB[s_], identB, biasHLB, mask0bB] + slabB[S_QA:S_QA + 4], writes=[bb])
                P.op("act", lambda e, bt=bt, k=k: e.activation(out=esT[:, k, :], in_=bt[:, :], func=AF.Exp, scale=0.125),
                     reads=[bb], writes=[esTB[k]])
                k += 1
                if k == 2:
                    yield
        yield
        ot, ob = nb(PATT)
        dt_, db = nb(PATT)
        fo, fd = [], []
        for k, (g, kb) in enumerate([(0, 0), (0, 1), (1, 0), (1, 1)]):
            s_ = t + kb
            fo.append(lambda e, g=g, s_=s_, k=k: e.matmul(ot[:, :], lhsT=va_pad[:, s_, g, :], rhs=esT[:, k, :],
                                                          start=(k == 0), stop=(k == 3)))
            fd.append(lambda e, g=g, k=k: e.matmul(dt_[:, :], lhsT=ones_pad[:, g, :], rhs=esT[:, k, :],
                                                   start=(k == 0), stop=(k == 3)))
        P.op("pe", fd, reads=[onespB] + esTB, writes=[db])
        P.op("pe", fo, reads=[vaB[t], vaB[t + 1]] + esTB, writes=[ob])
        P.op("dve", lambda e: e.tensor_tensor(out=dtmp[:].rearrange("p (j q) -> p j q", j=4),
                                              in0=dt_[:, :].rearrange("p (j q) -> p j q", j=4),
                                              in1=esink[:].unsqueeze(2).broadcast_to([128, 4, 128]), op=ALU.add),
             reads=[db, esinkB], writes=[dtmpB])
        P.op("act", lambda e: e.activation(out=dtmp[:], in_=dtmp[:], func=AF.Ln), reads=[dtmpB], writes=[dtmpB])
        P.op("act", lambda e: e.activation(out=dtmp[:], in_=dtmp[:], func=AF.Exp, scale=-1.0), reads=[dtmpB],
             writes=[dtmpB])
        yield
        P.op("dve", lambda e: e.tensor_tensor(out=slab[:, S_OA:S_OA + 4, tok],
                                              in0=ot[:, :].rearrange("p (j q) -> p j q", j=4),
                                              in1=dtmp[:].rearrange("p (j q) -> p j q", j=4), op=ALU.mult),
             reads=[ob, dtmpB], writes=slabB[S_OA:S_OA + 4])
        yield

    def run_merged(gens):
        gens = list(gens)
        while gens:
            for g in list(gens):
                try:
                    next(g)
                except StopIteration:
                    gens.remove(g)

    def run_weighted(pairs):
        pairs = [[g, w] for g, w in pairs]
        while pairs:
            for pr in list(pairs):
                for _ in range(pr[1]):
                    try:
                        next(pr[0])
                    except StopIteration:
                        pairs.remove(pr)
                        break

    def run_gla_region(full, extra, delay):
        ge = chain(gla_tile(t, full) for t in (0, 2))
        go = chain(gla_tile(t, full) for t in (1, 3))
        gens = [[ge, 0], [go, delay]] + [[g, 0] for g in extra]
        step = 0
        while gens:
            for pr in list(gens):
                if step < pr[1]:
                    continue
                try:
                    next(pr[0])
                except StopIteration:
                    gens.remove(pr)
            step += 1

    def chain(gs):
        for g in gs:
            yield from g

    NPB = NTP // 4
    hT_alt = (slab[:, 0:8, :], slabB[0:8])

    def hsel(bp):
        return hT_alt if (NPB - 1 - bp) % 2 == 0 else (hT, [hTB])

    mark('P0.norm')
    norm_to_hT(0, load_x(xp_d, 0), hsel(0))
    for bp in range(NPB):
        last = bp == NPB - 1
        src = hsel(bp)
        mark(f'P{bp}.proj')
        ld = load_x(x_d, 0) if last else load_x(xp_d, (bp + 1) * 512)
        for t in range(4):
            ld(t)
        if last:
            proj_kv_a(src)
            carry_kv()
            nxt = norm_gen(0, None, None, PPROJ)
        else:
            nxt = norm_gen(0, None, hsel(bp + 1), PPROJ)
        pg = prefix_proj_gen(src)
        next(pg)
        next(pg)
        next(pg)
        mark(f'P{bp}.gla')
        run_gla_region(False, [pg, nxt], 2)
    while pending_casts:
        pending_casts.pop(0)()
    P.op("act", lambda e: e.activation(out=Sbf[0][:], in_=S_st[:], func=AF.Copy), reads=[SB_], writes=[SbfB[0]])
    state_par[0] = 0

    psem = [P.new_sem() for _ in range(2)]
    for b in range(NT // 4):
        row0 = b * 512
        mark(f'M{b}.norm1')
        if b > 0:
            norm_to_hT(0, None)
        mark(f'M{b}.inproj')
        for j0 in (0, 2):
            srcs = []
            for g in range(2):
                for jj in range(2):
                    c0 = C_QA + g * 256 + (j0 + jj) * 64
                    src = wi_s[:, c0:c0 + 64].rearrange("(k p) c -> p k c", p=128)
                    srcs.append((lambda t, g=g, jj=jj: t[:, 0:2048].rearrange(
                        "p (k j g d) -> p k j g d", k=8, j=2, g=2)[:, :, jj, g, :], src))
            wt, wB = wload(srcs, "wiB", 8, 256)
            wv = wt[:, 0:2048].rearrange("p (k j c) -> p k j c", k=8, j=2)
            for jj in range(2):
                bt, bb = nb()
                P.op("pe", [(lambda e, kc=kc, jj=jj, bt=bt, wv=wv: e.matmul(bt[:, :], lhsT=wv[:, kc, jj, :],
                                                                            rhs=hT[:, kc, :], start=(kc == 0),
                                                                            stop=(kc == 7))) for kc in range(8)],
                     reads=[wB, hTB], writes=[bb])
                evac_copy(slab[:, S_QA + j0 + jj, :], bt[:, :], [bb], [slabB[S_QA + j0 + jj]])
        proj_kv_a()
        for (c_base, s_base) in ((C_QG, S_QG), (C_KG, S_KG)):
            for i in range(2):
                w, wB = wload_std(wi_s, wi_name(c_base + i * 256), 0, 8, c_base + i * 256, 256)
                for cc in range(2):
                    bank = nb()
                    fm_mm(bank, w, wB, cc * 128, hT, [hTB], 8)
                    evac_copy(slab[:, s_base + i * 2 + cc, :], bank[0][:, :], [bank[1]], [slabB[s_base + i * 2 + cc]])
        proj_gk()
        proj_gla_tm()
        mark(f'M{b}.attn')
        for _ in og_gen():
            pass
        run_gla_region(True, [chain(attn_tile(t, first=(b == 0 and t == 0)) for t in range(4))], 3)
        carry_kv()
        mark(f'M{b}.mixed')
        for qq in range(4):
            wga, wgaB = wload_std(wi_s, "wiC", 0, 8, C_GA + qq * 256, 256)
            wgg, wggB = wload_std(wi_s, "wiD", 0, 8, C_GG + qq * 256, 256)
            srcs = []
            for g in range(2):
                src = wpa_s[g * 256:(g + 1) * 256, qq * 256:(qq + 1) * 256].rearrange("(j p) c -> p j c", p=64)
                srcs.append((lambda t, g=g: t[g * 64:(g + 1) * 64, 0:1024].rearrange("p (j c) -> p j c", j=4), src))
            wpat, wpaB = wload(srcs, "wpa", 4, 256)
            wpa = wpat[:, 0:1024].rearrange("p (j c) -> p j c", j=4)
            wpg, wpgB = wload_std(wpg_s, "wpg", 0, 8, qq * 256, 256)
            for mm_ in range(2):
                m = qq * 2 + mm_
                bga, bgg, bya, byg = nb(), nb(), nb(), nb()
                fm_mm(bga, wga, wgaB, mm_ * 128, hT, [hTB], 8)
                P.op("act", lambda e, bt=bga[0]: e.activation(out=ta[:], in_=bt[:, :], func=AF.Tanh, scale=0.5),
                     reads=[bga[1]], writes=[taB])
                fm_mm(bgg, wgg, wggB, mm_ * 128, hT, [hTB], 8)
                P.op("act", lambda e, bt=bgg[0]: e.activation(out=tg[:], in_=bt[:, :], func=AF.Tanh, scale=0.5),
                     reads=[bgg[1]], writes=[tgB])
                fm_mm(bya, wpa, wpaB, mm_ * 128, slab[:, S_OA:S_OA + 4, :], slabB[S_OA:S_OA + 4], 4)
                fm_mm(byg, wpg, wpgB, mm_ * 128, slab[:, S_OGN:S_OGN + 8, :], slabB[S_OGN:S_OGN + 8], 8)
                P.op("dve", lambda e, bt=bya[0]: e.scalar_tensor_tensor(out=m1[:], in0=ta[:], scalar=1.0, in1=bt[:, :],
                                                                        op0=ALU.add, op1=ALU.mult),
                     reads=[taB, bya[1]], writes=[m1B])
                P.op("dve", lambda e, bt=byg[0]: e.scalar_tensor_tensor(out=m2[:], in0=tg[:], scalar=1.0, in1=bt[:, :],
                                                                        op0=ALU.add, op1=ALU.mult),
                     reads=[tgB, byg[1]], writes=[m2B])
                P.op("pool", lambda e, m=m: e.tensor_tensor(out=slab[:, S_MIX + m, :], in0=m1[:], in1=m2[:],
                                                            op=ALU.add),
                     reads=[m1B, m2B], writes=[slabB[S_MIX + m]])

        def proj_res(scr, name, nkc, act_view, actB, epi):
            for half in range(2):
                tb = [nb() for _ in range(4)]
                ngrp = (nkc + 3) // 4
                for kg in range(ngrp):
                    k0 = kg * 4
                    nk = min(4, nkc - k0)
                    w, wB = wload_std(scr, name, k0 * 128, nk, half * 512, 512)
                    for t in range(4):
                        bt, bb = tb[t]
                        P.op("pe", [(lambda e, kk=kk, t=t, bt=bt, w=w, k0=k0: e.matmul(
                            bt[:, :], lhsT=act_view[:, k0 + kk, t * 128:(t + 1) * 128], rhs=w[:, kk, :],
                            start=(k0 + kk == 0), stop=(k0 + kk == nkc - 1))) for kk in range(nk)],
                            reads=[wB] + list(actB), writes=[bb])
                for t in range(4):
                    epi(t, half, tb[t])

        def epi_half(t, half, bank):
            bt, bb = bank
            hs = slice(half * 512, (half + 1) * 512)
            P.op("dve", lambda e: e.scalar_tensor_tensor(out=xres[t][:, hs], in0=bt[:, :], scalar=0.5,
                                                         in1=xres[t][:, hs], op0=ALU.mult, op1=ALU.add),
                 reads=[bb, xresB[t]], writes=[xresB[t]])

        def epi_add(t, half, bank):
            bt, bb = bank
            hs = slice(half * 512, (half + 1) * 512)
            P.op("dve", lambda e: e.tensor_tensor(out=xres[t][:, hs], in0=bt[:, :], in1=xres[t][:, hs], op=ALU.add),
                 reads=[bb, xresB[t]], writes=[xresB[t]])

        mark(f'M{b}.wout')
        proj_res(wout_s, "wout", 8, slab[:, S_MIX:S_MIX + 8, :], slabB[S_MIX:S_MIX + 8], epi_half)

        mark(f'M{b}.norm2')
        norm_to_hT(1, None)
        mark(f'M{b}.ffnin')
        for ii in range(11):
            wg_, wgB_ = wload_std(wfi_s, "wfi", 0, 8, ii * 256, 256)
            wu_, wuB_ = wload_std(wfi_s, "wfi", 0, 8, D_FF + ii * 256, 256)
            for s in range(2):
                c = ii * 2 + s
                bg, bu = nb(), nb()
                fm_mm(bg, wg_, wgB_, s * 128, hT, [hTB], 8)
                fm_mm(bu, wu_, wuB_, s * 128, hT, [hTB], 8)
                f = c % 2
                P.op("act", lambda e, bt=bg[0], f=f: e.activation(out=ftt[f][:], in_=bt[:, :], func=AF.Tanh, scale=0.5),
                     reads=[bg[1]], writes=[fttB[f]])
                P.op("dve", lambda e, bt=bg[0], f=f: e.scalar_tensor_tensor(out=fa[f][:], in0=ftt[f][:], scalar=1.0,
                                                                           in1=bt[:, :], op0=ALU.add, op1=ALU.mult),
                     reads=[fttB[f], bg[1]], writes=[faB[f]])
                P.op("dve", lambda e, bt=bu[0], f=f, c=c: e.scalar_tensor_tensor(
                    out=slab[:, S_ACT + c, :], in0=fa[f][:], scalar=0.5, in1=bt[:, :], op0=ALU.mult, op1=ALU.mult),
                    reads=[faB[f], bu[1]], writes=[slabB[S_ACT + c]])
        mark(f'M{b}.ffnout')
        proj_res(wfo_s, "wfo", 22, slab[:, S_ACT:S_ACT + 22, :], slabB[S_ACT:S_ACT + 22], epi_add)

        mark(f'M{b}.norm3')
        norm_to_hT(2, None)
        mark(f'M{b}.ple')
        for t in range(4):
            i = t % 2
            P.dma("sp", lambda e, t=t, i=i, row0=row0: e.dma_start(out=p_f[i][:], in_=p_d[row0 + t * 128: row0 + (t + 1) * 128, :]),
                  psem[i], writes=[p_fB[i]])
            P.op("pool", lambda e, i=i: e.tensor_copy(out=p_bf[i][:], in_=p_f[i][:]), reads=[p_fB[i]],
                 writes=[p_bfB[i]])
            bt, bb = nb()
            btb = bt[:].bitcast(BF16)
            P.op("pe", [(lambda e, kc=kc, i=i, btb=btb: e.transpose(out=btb[:, kc * 128:(kc + 1) * 128],
                                                                    in_=p_bf[i][:, kc * 128:(kc + 1) * 128],
                                                                    identity=ident[:])) for kc in range(2)],
                 reads=[p_bfB[i], identB], writes=[bb])
            P.op("act", lambda e, t=t, btb=btb: e.activation(out=pT[:, :, t * 128:(t + 1) * 128],
                                                             in_=btb[:, 0:256].rearrange("p (k c) -> p k c", k=2),
                                                             func=AF.Copy), reads=[bb], writes=[pTB])

        def epi_gate(t, half, bank):
            bt, bb = bank
            P.op("act", lambda e: e.activation(out=esT[:, t, :], in_=bt[:, :], func=AF.Tanh, scale=0.5), reads=[bb],
                 writes=[tgtB[t]])

        def epi_ple(t, half, bank):
            bt, bb = bank
            hs = slice(half * 512, (half + 1) * 512)
            P.op("dve", lambda e: e.scalar_tensor_tensor(out=m1[:], in0=esT[:, t, :], scalar=1.0, in1=bt[:, :],
                                                         op0=ALU.add, op1=ALU.mult),
                 reads=[tgtB[t], bb], writes=[m1B])
            P.op("dve", lambda e: e.scalar_tensor_tensor(out=xres[t][:, hs], in0=m1[:], scalar=0.5, in1=xres[t][:, hs],
                                                         op0=ALU.mult, op1=ALU.add),
                 reads=[m1B, xresB[t]], writes=[xresB[t]])

        for half in range(2):
            tb = [nb() for _ in range(4)]
            for kg in range(2):
                w, wB = wload_std(wpgate_s, "wpgate", kg * 512, 4, half * 512, 512)
                for t in range(4):
                    bt, bb = tb[t]
                    P.op("pe", [(lambda e, kk=kk, t=t, bt=bt, w=w, kg=kg: e.matmul(
                        bt[:, :], lhsT=hT[:, kg * 4 + kk, t * 128:(t + 1) * 128], rhs=w[:, kk, :],
                        start=(kg * 4 + kk == 0), stop=(kg * 4 + kk == 7))) for kk in range(4)],
                        reads=[wB, hTB], writes=[bb])
            for t in range(4):
                epi_gate(t, half, tb[t])
            w, wB = wload_std(wple_s, "wple", 0, 2, half * 512, 512)
            for t in range(4):
                bt, bb = nb()
                P.op("pe", [(lambda e, kk=kk, t=t, bt=bt, w=w: e.matmul(bt[:, :], lhsT=pT[:, kk, t * 128:(t + 1) * 128],
                                                                        rhs=w[:, kk, :], start=(kk == 0),
                                                                        stop=(kk == 1))) for kk in range(2)],
                     reads=[wB, pTB], writes=[bb])
                epi_ple(t, half, (bt, bb))

        mark(f'M{b}.final')
        nxt_load = load_x(x_d, row0 + 512) if b + 1 < NT // 4 else None

        def out_dma(t):
            o = P.dma("sp", lambda e, t=t, row0=row0: e.dma_start(out=out_d[row0 + t * 128: row0 + (t + 1) * 128, :],
                                                              in_=ystage[t]), osem[t], reads=ystageB[t])
            if b == NT // 4 - 1:
                P.final.append(o)

        for t in range(4):
            P.op("act", lambda e, t=t: e.activation(out=junk_ap, in_=xres[t][:], func=AF.Square,
                                                   accum_out=stat[t][:, 0:1]),
                 reads=[xresB[t]], writes=[junkB, statB[t]])
            P.op("pool", lambda e, t=t: e.tensor_scalar(out=stat[t][:, 1:2], in0=stat[t][:, 0:1], scalar1=1.0 / D,
                                                       scalar2=EPS, op0=ALU.mult, op1=ALU.add),
                 reads=[statB[t]], writes=[statB[t]])
            P.op("pool", lambda e, t=t: e.tensor_tensor(out=stat[t][:, 2:3], in0=stat[t][:, 1:2], in1=mhalf[:, 0:1],
                                                       op=ALU.pow), reads=[statB[t], mhalfB], writes=[statB[t]])
            P.op("dve", lambda e, t=t: e.scalar_tensor_tensor(out=ystage[t], in0=xres[t][:],
                                                             scalar=stat[t][:, 2:3], in1=gfin[:], op0=ALU.mult,
                                                             op1=ALU.mult),
                 reads=[xresB[t], statB[t], gfinB], writes=ystageB[t])
            if nxt_load is not None:
                nxt_load(t)
            out_dma(t)

    mark('end')
    nc._marks = marks
    P.emit()
    return nc


def _t5_bucket(dist):
    max_exact = 16
    d_f = np.maximum(dist, max_exact).astype(np.float32)
    large = max_exact + (np.log(d_f / max_exact) / math.log(128 / max_exact) * (32 - max_exact)).astype(np.int32)
    large = np.minimum(large, 31)
    return np.where(dist < max_exact, dist, large)


def _consts():
    s = np.arange(128)[:, None]
    t = np.arange(128)[None, :]
    same = (s // 64) == (t // 64)
    cst = np.zeros((128, 386), np.float32)
    cst[:, 0:128] = np.where(same & (s <= t), -1.0 / 16, 0.0)
    cst[:, 128:256] = np.where(same & (s > t), -1.0 / 16, 0.0)
    cst[:, 256] = np.where(np.arange(128) < 64, -1.0 / 16, 0.0)
    cst[:, 257] = np.where(np.arange(128) >= 64, -1.0 / 16, 0.0)
    cst[:, 258:386] = np.where(same & (s <= t), 1.0, 0.0)
    k = np.arange(128)[:, None]
    q = np.arange(128)[None, :]
    maskT = np.zeros((2, 128, 128), np.float32)
    maskT[0] = np.where(k > q, 0.0, NEG)
    maskT[1] = np.where(k <= q, 0.0, NEG)
    bucket = np.zeros((2, 128, 128), np.int64)
    for kb in range(2):
        dist = q + 128 - (kb * 128 + k)
        bucket[kb] = _t5_bucket(np.maximum(dist, 0))
    return cst, maskT, bucket


def _host_inputs(x, p, w_in, w_gk2, b_gk, sinks, rel_table, w_proj_attn, w_proj_gla, gla_norm, w_out, norm_mix,
                 norm_ffn, w_ffn_in, w_ffn_out, norm_ple, w_ple_gate, w_ple, norm_final, T):
    f = lambda a: np.ascontiguousarray(np.asarray(a, dtype=np.float32))
    x, p = f(x), f(p)
    B, S, _ = x.shape
    halves = S // T
    cst, maskT, bucket = _consts()
    rel = f(rel_table)
    biasT = np.ascontiguousarray(np.transpose(rel[bucket], (0, 1, 3, 2))).reshape(2, 128, 1024)
    sk = f(sinks)[0]
    sink_l = np.zeros((128, 4), np.float32)
    sink_l[0:64, :] = sk[0:4][None, :]
    sink_l[64:128, :] = sk[4:8][None, :]
    gcols = np.concatenate([f(norm_mix)[0].reshape(8, 128).T, f(norm_ffn)[0].reshape(8, 128).T,
                            f(norm_ple)[0].reshape(8, 128).T], axis=1)
    gn_col = np.ascontiguousarray(f(gla_norm)[0].reshape(2, 128).T)
    wgk_aug = np.concatenate([f(w_gk2)[0], f(b_gk)[0][None, :]], axis=0)
    shared = {
        "w_in": f(w_in)[0], "w_proj_attn": f(w_proj_attn)[0], "w_proj_gla": f(w_proj_gla)[0], "w_out": f(w_out)[0],
        "w_ffn_in": f(w_ffn_in)[0], "w_ffn_out": f(w_ffn_out)[0], "w_ple_gate": f(w_ple_gate)[0],
        "w_ple": f(w_ple)[0], "wgk_aug": np.ascontiguousarray(wgk_aug), "gcols": np.ascontiguousarray(gcols),
        "gn_col": gn_col, "norm_final": f(norm_final), "sink_l": sink_l, "biasT": biasT, "maskT": maskT, "cst": cst,
    }
    maps = []
    for b in range(B):
        for h in range(halves):
            m = dict(shared)
            m["x"] = np.ascontiguousarray(x[b, h * T:(h + 1) * T])
            m["p"] = np.ascontiguousarray(p[0, b, h * T:(h + 1) * T])
            if h == 0:
                m["xp"] = np.zeros((T, D), np.float32)
                m["mask0"] = np.full((128, 128), NEG, np.float32)
            else:
                m["xp"] = np.ascontiguousarray(x[b, (h - 1) * T:h * T])
                m["mask0"] = np.zeros((128, 128), np.float32)
            maps.append(m)
    return maps, B, halves


def run(inputs, T):
    maps, B, halves = _host_inputs(T=T, **inputs)
    nc = build_nc(T // 128, T // 128)
    n = len(maps)
    res = run_bass_kernel_spmd(nc, maps, core_ids=list(range(n)))
    out = np.zeros((B, halves * T, D), np.float32)
    for i, r in enumerate(res.results):
        b, h = i // halves, i % halves
        out[b, h * T:(h + 1) * T] = r["out"]
    return out


def kernel(**inputs):
    return run(inputs, 4096)
```

```python
import math
import numpy as np
import concourse.bass as bass
import concourse.mybir as mybir
from concourse.bass_utils import run_bass_kernel_spmd

F32 = mybir.dt.float32
BF16 = mybir.dt.bfloat16
AF = mybir.ActivationFunctionType
ALU = mybir.AluOpType

D = 1024
D_IN = 5904
D_FF = 2816
EPS = 1e-6
NEG = -30000.0
C_QA, C_KA, C_VA, C_QG, C_KG, C_VG, C_GK, C_OG, C_GA, C_GG = 0, 512, 640, 768, 1280, 1792, 2816, 2832, 3856, 4880


class Buf:
    __slots__ = ("name", "last_w", "readers")

    def __init__(self, name):
        self.name = name
        self.last_w = None
        self.readers = []


class Op:
    __slots__ = ("eng", "fns", "deps", "sem", "val", "signal", "is_dma")

    def __init__(self, eng, fns):
        self.eng = eng
        self.fns = fns
        self.deps = []
        self.sem = None
        self.val = 0
        self.signal = False
        self.is_dma = False


class Prog:
    ENGS = ("pe", "act", "dve", "pool", "sp")

    def __init__(self, nc):
        self.nc = nc
        self.ops = {e: [] for e in self.ENGS}
        self.esem = {e: nc.alloc_semaphore("es_" + e) for e in self.ENGS}
        self.dma_cnt = {}
        self.nsem = 0
        self.final = []

    def buf(self, name):
        return Buf(name)

    def new_sem(self):
        self.nsem += 1
        return self.nc.alloc_semaphore(f"ds_{self.nsem}")

    def _track(self, op, reads, writes):
        deps = []
        for b in reads:
            if b.last_w is not None:
                deps.append(b.last_w)
        for b in writes:
            if b.last_w is not None:
                deps.append(b.last_w)
            deps.extend(b.readers)
        for b in reads:
            b.readers.append(op)
        for b in writes:
            b.last_w = op
            b.readers = []
        seen = set()
        for d in deps:
            if d is op or id(d) in seen:
                continue
            seen.add(id(d))
            if d.eng == "pe" and op.eng == "pe" and not d.is_dma and not op.is_dma:
                continue
            op.deps.append(d)
            d.signal = True

    def op(self, eng, fns, reads=(), writes=()):
        if callable(fns):
            fns = [fns]
        o = Op(eng, list(fns))
        self._track(o, reads, writes)
        self.ops[eng].append(o)
        return o

    def dma(self, eng, fns, sem, reads=(), writes=()):
        if callable(fns):
            fns = [fns]
        o = Op(eng, list(fns))
        o.is_dma = True
        o.sem = sem
        c = self.dma_cnt.get(id(sem), 0) + 16 * len(o.fns)
        self.dma_cnt[id(sem)] = c
        o.val = c
        self._track(o, reads, writes)
        self.ops[eng].append(o)
        return o

    def emit(self):
        nc = self.nc
        for e in self.ENGS:
            c = 0
            for o in self.ops[e]:
                if o.is_dma:
                    continue
                if o.signal:
                    c += 1
                    o.sem = self.esem[e]
                    o.val = c
        handles = {"pe": "tensor", "act": "scalar", "dve": "vector", "pool": "gpsimd", "sp": "sync"}
        final = self.final
        with nc.Block() as block:
            for e in self.ENGS:
                ops = self.ops[e]

                def body(eng, ops=ops, e=e):
                    waited = {}
                    for o in ops:
                        for d in o.deps:
                            k = id(d.sem)
                            if waited.get(k, 0) >= d.val:
                                continue
                            eng.wait_ge(d.sem, d.val)
                            waited[k] = d.val
                        n = len(o.fns)
                        for i, f in enumerate(o.fns):
                            ins = f(eng)
                            if o.is_dma:
                                ins.then_inc(o.sem, 16)
                            elif o.signal and i == n - 1:
                                ins.then_inc(o.sem, 1)
                    if e == "sp":
                        for d in final:
                            eng.wait_ge(d.sem, d.val)

                getattr(block, handles[e])(body)


def build_nc(NT, NTP, dbg=None):
    assert NT % 4 == 0 and NTP % 4 == 0
    nc = bass.Bass("TRN2", target_bir_lowering=False)
    P = Prog(nc)
    T, TP = NT * 128, NTP * 128

    def din(name, shape, dt=F32):
        return nc.dram_tensor(name, list(shape), dt, kind="ExternalInput").ap()

    x_d = din("x", [T, D])
    xp_d = din("xp", [TP, D])
    p_d = din("p", [T, 256])
    w_in_d = din("w_in", [D, D_IN])
    wpa_d = din("w_proj_attn", [512, D])
    wpg_d = din("w_proj_gla", [D, D])
    wout_d = din("w_out", [D, D])
    wfi_d = din("w_ffn_in", [D, 2 * D_FF])
    wfo_d = din("w_ffn_out", [D_FF, D])
    wpgate_d = din("w_ple_gate", [D, D])
    wple_d = din("w_ple", [256, D])
    wgk_d = din("wgk_aug", [17, 512])
    gcols_d = din("gcols", [128, 24])
    gn_d = din("gn_col", [128, 2])
    gfin_d = din("norm_final", [D])
    sinkl_d = din("sink_l", [128, 4])
    biasT_d = din("biasT", [2, 128, 1024])
    maskT_d = din("maskT", [2, 128, 128])
    mask0_d = din("mask0", [128, 128])
    cst_d = din("cst", [128, 386])
    out_d = nc.dram_tensor("out", [T, D], F32, kind="ExternalOutput").ap()

    def dscr(name, shape):
        return nc.dram_tensor(name, list(shape), BF16, kind="Internal").ap()

    wi_s = dscr("wi_s", [D, D_IN])
    wpa_s = dscr("wpa_s", [512, D])
    wpg_s = dscr("wpg_s", [D, D])
    wout_s = dscr("wout_s", [D, D])
    wfi_s = dscr("wfi_s", [D, 2 * D_FF])
    wfo_s = dscr("wfo_s", [D_FF, D])
    wpgate_s = dscr("wpgate_s", [D, D])
    wple_s = dscr("wple_s", [256, D])

    def sb(name, shape, dt=F32):
        return nc.alloc_sbuf_tensor("s_" + name, list(shape), dt)

    xres = [sb(f"xres{t}", [128, D]) for t in range(4)]
    xresB = [P.buf(f"xres{t}") for t in range(4)]
    xn = [sb(f"xn{i}", [128, D], BF16) for i in range(2)]
    xnB = [P.buf(f"xn{i}") for i in range(2)]
    stat = [sb(f"stat{t}", [128, 4]) for t in range(4)]
    statB = [P.buf(f"stat{t}") for t in range(4)]
    hT = sb("hT", [128, 8, 512], BF16)
    hTB = P.buf("hT")
    slab = sb("slab", [128, 40, 512], BF16)
    slabB = [P.buf(f"slab{i}") for i in range(40)]
    S_ACT, S_GSIL, S_OGN, S_MIX, S_QA, S_QG, S_KG, S_OA = 0, 0, 8, 16, 24, 28, 32, 36
    ka_pad = [sb(f"ka_pad{g}", [128, 640], BF16) for g in range(2)]
    kaB = [P.buf(f"ka_s{s}") for s in range(5)]
    va_pad = sb("va_pad", [128, 5, 2, 128], BF16)
    vaB = [P.buf(f"va_s{s}") for s in range(5)]
    kgTM = [sb(f"kgTM{t}", [128, 512], BF16) for t in range(4)]
    kgTMB = [P.buf(f"kgTM{t}") for t in range(4)]
    vgTM = [sb(f"vgTM{t}", [128, 1024], BF16) for t in range(4)]
    vgTMB = [P.buf(f"vgTM{t}") for t in range(4)]
    gkT = sb("gkT", [32, 512])
    gkTB = P.buf("gkT")
    sp_t2 = [sb(f"sp_t{i}", [128, 512]) for i in range(2)]; spB2 = [P.buf(f"sp{i}") for i in range(2)]
    ED2 = [sb(f"ED{i}", [128, 512]) for i in range(2)]; EDB2 = [P.buf(f"ED{i}") for i in range(2)]
    Epos2 = [sb(f"Epos{i}", [128, 512]) for i in range(2)]; EposB2 = [P.buf(f"Epos{i}") for i in range(2)]
    Eneg2 = [sb(f"Eneg{i}", [128, 512]) for i in range(2)]; EnegB2 = [P.buf(f"Eneg{i}") for i in range(2)]
    el2 = [sb(f"el{i}", [128, 8]) for i in range(2)]; elB2 = [P.buf(f"el{i}") for i in range(2)]
    qt2 = [sb(f"qt{i}", [128, 4, 128], BF16) for i in range(2)]; qtB2 = [P.buf(f"qt{i}") for i in range(2)]
    kt2 = [sb(f"kt{i}", [128, 4, 128], BF16) for i in range(2)]; ktB2 = [P.buf(f"kt{i}") for i in range(2)]
    kd2 = [[sb(f"kd{i}_{c}", [128, 512], BF16) for c in range(2)] for i in range(2)]
    kdB2 = [[P.buf(f"kd{i}_{c}") for c in range(2)] for i in range(2)]
    ATs = sb("ATs", [128, 4, 128], BF16); ATsB = P.buf("ATs")
    sq = sb("sq", [128, 8, 128], BF16); sqB = P.buf("sq")
    junk_ap = sq[:].rearrange("p a b -> p (a b)")
    junkB = sqB
    rstd_g = sb("rstd_g", [128, 4, 128]); rstdgB = P.buf("rstd_g")
    otmp = sb("otmp", [128, 8, 128]); otmpB = P.buf("otmp")
    S_st = sb("S_st", [128, 1024]); SB_ = P.buf("S_st")
    Sbf = [sb(f"Sbf{i}", [128, 1024], BF16) for i in range(2)]
    SbfB = [P.buf(f"Sbf{i}") for i in range(2)]
    sT = [sb(f"sT{i}", [128, 512]) for i in range(2)]
    sTB = [P.buf(f"sT{i}") for i in range(2)]
    esT = sb("esT", [128, 4, 512], BF16)
    esTB = [P.buf(f"esT{i}") for i in range(4)]
    biasT8 = sb("biasT8", [128, 2, 1024]); biasB = P.buf("biasT8")
    mask0 = sb("mask0", [128, 128]); mask0B = P.buf("mask0")
    bias_hi = sb("bias_hi", [128, 2, 1024], BF16); bias_lo = sb("bias_lo", [128, 2, 1024], BF16)
    biasHLB = P.buf("bias_hl")
    mask0_bf = sb("mask0_bf", [128, 128], BF16); mask0bB = P.buf("mask0_bf")
    maskT = sb("maskT", [128, 2, 128]); maskTB = P.buf("maskT")
    esink = sb("esink", [128, 4]); esinkB = P.buf("esink")
    ta = sb("ta", [128, 512], BF16); taB = P.buf("ta")
    tg = sb("tg", [128, 512], BF16); tgB = P.buf("tg")
    m1 = sb("m1", [128, 512]); m1B = P.buf("m1")
    m2 = sb("m2", [128, 512]); m2B = P.buf("m2")
    lnv, lnvB = m1, m1B
    dtmp, dtmpB = m2, m2B
    ftt = [sb(f"ftt{i}", [128, 512]) for i in range(2)]
    fttB = [P.buf(f"ftt{i}") for i in range(2)]
    fa = [sb(f"fa{i}", [128, 512]) for i in range(2)]
    faB = [P.buf(f"fa{i}") for i in range(2)]
    p_f = [sb(f"p_f{i}", [128, 256]) for i in range(2)]
    p_fB = [P.buf(f"p_f{i}") for i in range(2)]
    p_bf = [sb(f"p_bf{i}", [128, 256], BF16) for i in range(2)]
    p_bfB = [P.buf(f"p_bf{i}") for i in range(2)]
    pT = sb("pT", [128, 2, 512], BF16); pTB = P.buf("pT")
    tgtB = esTB
    gfin = sb("gfin", [128, D]); gfinB = P.buf("gfin")
    gcols = sb("gcols", [128, 24]); gcolsB = P.buf("gcols")
    gnh = sb("gnh", [128, 2]); gnhB = P.buf("gnh")
    cst = sb("cst", [128, 386]); cstB = P.buf("cst")
    wgk = sb("wgk", [32, 512]); wgkB = P.buf("wgk")
    ident = sb("ident", [128, 128], BF16); identB = P.buf("ident")
    ones_bf = sb("ones_bf", [128, 128], BF16); onesB = P.buf("ones")
    ones_pad = sb("ones_pad", [128, 2, 128], BF16); onespB = P.buf("ones_pad")
    mhalf = sb("mhalf", [128, 1]); mhalfB = P.buf("mhalf")
    NSLOT = 6
    ring = [(sb(f"ws{i}", [128, 2048], BF16), P.buf(f"ws{i}"), P.new_sem()) for i in range(NSLOT)]
    ring_i = [0]
    NBANK = 8
    banks = [(nc.alloc_psum_tensor(f"pb{i}", [128, 512], F32), P.buf(f"pb{i}")) for i in range(NBANK)]
    bank_i = [0]
    marks = []

    def mark(name):
        marks.append((name, sum(len(o.fns) for o in P.ops['pe'])))

    pool_i = {}

    def nb(pool=None):
        if pool is None:
            r = banks[bank_i[0] % NBANK]
            bank_i[0] += 1
            return r
        i = pool_i.get(pool, 0)
        pool_i[pool] = i + 1
        return banks[pool[i % len(pool)]]

    PGLA2, PATT, PPROJ = ((0, 1, 2), (3, 4, 5)), (6, 7), (6, 7)

    U2m = cst[:, 0:128]
    SU2m = cst[:, 128:256]
    ind2m = cst[:, 256:258]
    maskA = cst[:, 258:386]

    sem_c = [P.new_sem() for _ in range(12)]
    P.dma("sp", lambda e: e.dma_start(out=cst[:], in_=cst_d), sem_c[0], writes=[cstB])
    P.dma("sp", lambda e: e.dma_start(out=gcols[:], in_=gcols_d), sem_c[1], writes=[gcolsB])
    P.dma("sp", lambda e: e.dma_start(out=gnh[:], in_=gn_d), sem_c[2], writes=[gnhB])
    P.dma("sp", lambda e: e.dma_start(out=gfin[:], in_=gfin_d.partition_broadcast(128)), sem_c[3], writes=[gfinB])
    P.dma("sp", lambda e: e.dma_start(out=esink[:], in_=sinkl_d), sem_c[4], writes=[esinkB])
    P.dma("sp", lambda e: e.dma_start(out=wgk[0:17, :], in_=wgk_d), sem_c[5], writes=[wgkB])
    P.dma("sp", lambda e: e.dma_start(out=mask0[:], in_=mask0_d), sem_c[6], writes=[mask0B])
    P.dma("sp", lambda e: e.dma_start(out=maskT[:], in_=maskT_d.rearrange("b k q -> k b q")), sem_c[7],
          writes=[maskTB])
    P.dma("sp", lambda e: e.dma_start(out=biasT8[:], in_=biasT_d.rearrange("b k c -> k b c")), sem_c[8],
          writes=[biasB])

    def flat(ap, n):
        return ap.rearrange("a b -> (a b)").rearrange("(n c) -> n c", c=2048)

    castB = {}

    def cast(name, src, dst, nrows_flat, pieces=1):
        b = P.buf("scr_" + name)
        castB[name] = b
        fs, fd = flat(src, nrows_flat), flat(dst, nrows_flat)
        per = nrows_flat // pieces
        fns = []
        for i in range(pieces):
            r0, r1 = i * per, (nrows_flat if i == pieces - 1 else (i + 1) * per)
            fns.append(lambda e, r0=r0, r1=r1: e.dma_start(out=fd[r0:r1, :], in_=fs[r0:r1, :]))
        P.dma("pool", fns, P.new_sem(), writes=[b])

    def cast_cols(name, c0, c1):
        b = P.buf("scr_" + name)
        castB[name] = [b]
        P.dma("pool", lambda e: e.dma_start(out=wi_s[:, c0:c1], in_=w_in_d[:, c0:c1]), P.new_sem(), writes=[b])

    pending_casts = []
    cast_cols("wiA", C_KG, C_OG)
    for nm_, a_, b_, np_ in (("wiB", 0, C_KG, 2), ("wiC", C_OG, C_GG, 4), ("wiD", C_GG, D_IN, 2)):
        castB[nm_] = [P.buf(f"scr_{nm_}{i}") for i in range(np_)]
        for i in range(np_):
            r0, r1 = i * (D // np_), (i + 1) * (D // np_)
            pending_casts.append(lambda nm_=nm_, a_=a_, b_=b_, i=i, r0=r0, r1=r1: P.dma(
                "pool", lambda e: e.dma_start(out=wi_s[r0:r1, a_:b_], in_=w_in_d[r0:r1, a_:b_]), P.new_sem(),
                writes=[castB[nm_][i]]))

    def wi_name(c0):
        return "wiB" if c0 < C_KG else ("wiA" if c0 < C_OG else ("wiC" if c0 < C_GG else "wiD"))
    P.op("pool", lambda e: e.memset(ident[:], 1.0), writes=[identB])
    P.op("pool", lambda e: e.affine_select(out=ident[:], in_=ident[:], pattern=[[-1, 128]], compare_op=ALU.is_equal,
                                           fill=0.0, base=0, channel_multiplier=1), reads=[identB], writes=[identB])
    P.op("pool", lambda e: e.memset(ones_bf[:], 1.0), writes=[onesB])
    P.op("pool", [lambda e: e.memset(ones_pad[:], 0.0)], writes=[onespB])
    P.op("pool", [lambda e: e.memset(ones_pad[:, 0, 0:64], 1.0), lambda e: e.memset(ones_pad[:, 1, 64:128], 1.0)],
         reads=[onespB], writes=[onespB])
    P.op("pool", lambda e: e.memset(mhalf[:], -0.5), writes=[mhalfB])
    P.op("pool", [lambda e: e.memset(ka_pad[0][:], 0.0), lambda e: e.memset(ka_pad[1][:], 0.0),
                  lambda e: e.memset(va_pad[:], 0.0)], writes=kaB + vaB)
    P.op("pool", [lambda e, i=i, c=c: e.memset(kd2[i][c][:], 0.0) for i in range(2) for c in range(2)],
         writes=kdB2[0] + kdB2[1])
    P.op("pool", lambda e: e.memset(gkT[:], 1.0), writes=[gkTB])
    P.op("pool", lambda e: e.memset(S_st[:], 0.0), writes=[SB_])
    P.op("pool", lambda e: e.memset(wgk[:], 0.0), writes=[wgkB]) if False else None
    P.op("act", lambda e: e.activation(out=esink[:], in_=esink[:], func=AF.Exp), reads=[esinkB], writes=[esinkB])
    for kb in range(2):
        P.op("dve", lambda e, kb=kb: e.tensor_tensor(
            out=biasT8[:, kb, :].rearrange("p (h q) -> p h q", h=8),
            in0=biasT8[:, kb, :].rearrange("p (h q) -> p h q", h=8),
            in1=maskT[:, kb, :].unsqueeze(1).broadcast_to([128, 8, 128]), op=ALU.add),
            reads=[biasB, maskTB], writes=[biasB])
    P.op("dve", lambda e: e.tensor_scalar(out=biasT8[:], in0=biasT8[:], scalar1=8.0, scalar2=None, op0=ALU.mult),
         reads=[biasB], writes=[biasB])
    P.op("dve", lambda e: e.tensor_scalar(out=mask0[:], in0=mask0[:], scalar1=8.0, scalar2=None, op0=ALU.mult),
         reads=[mask0B], writes=[mask0B])
    P.op("dve", lambda e: e.tensor_copy(out=mask0_bf[:], in_=mask0[:]), reads=[mask0B], writes=[mask0bB])
    P.op("dve", lambda e: e.tensor_copy(out=bias_hi[:], in_=biasT8[:]), reads=[biasB], writes=[biasHLB])
    P.op("dve", lambda e: e.tensor_tensor(out=biasT8[:], in0=biasT8[:], in1=bias_hi[:], op=ALU.subtract),
         reads=[biasB, biasHLB], writes=[biasB])
    P.op("dve", lambda e: e.tensor_copy(out=bias_lo[:], in_=biasT8[:]), reads=[biasB, biasHLB], writes=[biasHLB])
    def defer_cast(name, src, dst, nrows_flat, pieces=1):
        castB[name] = [P.buf(f"scr_{name}{i}") for i in range(pieces)]
        fs, fd = flat(src, nrows_flat), flat(dst, nrows_flat)
        per = nrows_flat // pieces
        for i in range(pieces):
            r0, r1 = i * per, (nrows_flat if i == pieces - 1 else (i + 1) * per)
            pending_casts.append(lambda name=name, i=i, r0=r0, r1=r1, fs=fs, fd=fd: P.dma(
                "pool", lambda e: e.dma_start(out=fd[r0:r1, :], in_=fs[r0:r1, :]), P.new_sem(),
                writes=[castB[name][i]]))

    castS = {}
    defer_cast("wpa", wpa_d, wpa_s, 512 * D // 2048)
    defer_cast("wpg", wpg_d, wpg_s, D * D // 2048, pieces=2)
    defer_cast("wout", wout_d, wout_s, D * D // 2048, pieces=2)
    defer_cast("wfi", wfi_d, wfi_s, D * 2 * D_FF // 2048, pieces=10)
    defer_cast("wfo", wfo_d, wfo_s, D_FF * D // 2048, pieces=5)
    defer_cast("wpgate", wpgate_d, wpgate_s, D * D // 2048, pieces=2)
    defer_cast("wple", wple_d, wple_s, 256 * D // 2048)

    def wload(srcs, name, nk, ncols):
        t, b, s = ring[ring_i[0] % NSLOT]
        ring_i[0] += 1
        fns = [(lambda e, dv=dv, sa=sa, t=t: e.dma_start(out=dv(t), in_=sa)) for dv, sa in srcs]
        P.dma("sp", fns, s, reads=castB[name], writes=[b])
        return t, b

    def wload_std(scr, name, r0, nk, c0, ncols):
        src = scr[r0:r0 + nk * 128, c0:c0 + ncols].rearrange("(k p) c -> p k c", p=128)
        t, b = wload([(lambda t: t[:, 0:nk * ncols].rearrange("p (k c) -> p k c", k=nk), src)], name, nk, ncols)
        return t[:, 0:nk * ncols].rearrange("p (k c) -> p k c", k=nk), b

    evac_i = [0]

    def evac_copy(out_ap, in_ap, reads, writes, eng=None):
        if eng is None:
            eng = "act" if evac_i[0] % 2 == 0 else "dve"
            evac_i[0] += 1
        if eng == "act":
            P.op("act", lambda e: e.activation(out=out_ap, in_=in_ap, func=AF.Copy), reads=reads, writes=writes)
        else:
            P.op("dve", lambda e: e.tensor_copy(out=out_ap, in_=in_ap), reads=reads, writes=writes)

    def fm_mm(bank, w, wB, ccol, act_view, actB, nk):
        bt, bb = bank
        fns = [(lambda e, kc=kc: e.matmul(bt[:, :], lhsT=w[:, kc, ccol:ccol + 128], rhs=act_view[:, kc, :],
                                          start=(kc == 0), stop=(kc == nk - 1))) for kc in range(nk)]
        P.op("pe", fns, reads=[wB] + list(actB), writes=[bb])

    def norm_gen(gidx, load_fn, dst=None, pool=None):
        hTc, hTBc = dst if dst is not None else (hT, [hTB])

        def stage_a(t):
            if load_fn is not None:
                load_fn(t)
            P.op("act", lambda e, t=t: e.activation(out=junk_ap, in_=xres[t][:], func=AF.Square,
                                                   accum_out=stat[t][:, 0:1]),
                 reads=[xresB[t]], writes=[junkB, statB[t]])
            P.op("pool", lambda e, t=t: e.tensor_scalar(out=stat[t][:, 1:2], in0=stat[t][:, 0:1], scalar1=1.0 / D,
                                                       scalar2=EPS, op0=ALU.mult, op1=ALU.add),
                 reads=[statB[t]], writes=[statB[t]])
            P.op("pool", lambda e, t=t: e.tensor_tensor(out=stat[t][:, 2:3], in0=stat[t][:, 1:2], in1=mhalf[:, 0:1],
                                                       op=ALU.pow),
                 reads=[statB[t], mhalfB], writes=[statB[t]])
            i = t % 2
            P.op("dve", lambda e, t=t, i=i: e.tensor_scalar(out=xn[i][:], in0=xres[t][:], scalar1=stat[t][:, 2:3],
                                                           scalar2=None, op0=ALU.mult),
                 reads=[xresB[t], statB[t]], writes=[xnB[i]])

        def stage_b(t):
            i = t % 2
            bt, bb = nb(pool)
            btb = bt[:].bitcast(BF16)
            P.op("pe", [(lambda e, kc=kc, i=i, btb=btb: e.transpose(out=btb[:, kc * 128:(kc + 1) * 128],
                                                                    in_=xn[i][:, kc * 128:(kc + 1) * 128],
                                                                    identity=ident[:])) for kc in range(8)],
                 reads=[xnB[i], identB], writes=[bb])
            P.op("dve", lambda e, t=t, btb=btb: e.tensor_tensor(
                out=hTc[:, :, t * 128:(t + 1) * 128], in0=btb.rearrange("p (k c) -> p k c", k=8),
                in1=gcols[:, gidx * 8:(gidx + 1) * 8].unsqueeze(2).broadcast_to([128, 8, 128]), op=ALU.mult),
                reads=[bb, gcolsB], writes=hTBc)

        stage_a(0)
        yield
        for t in range(1, 4):
            stage_a(t)
            stage_b(t - 1)
            yield
        stage_b(3)
        yield

    def norm_to_hT(gidx, load_fn, dst=None):
        for _ in norm_gen(gidx, load_fn, dst):
            pass


    ystage = [slab[:, 4 * i:4 * i + 4, :].rearrange("p a b -> p (a b)").bitcast(F32) for i in range(4)]
    ystageB = [slabB[4 * i:4 * i + 4] for i in range(4)]
    xsem = [P.new_sem() for _ in range(4)]
    osem = [P.new_sem() for _ in range(4)]

    def load_x(src, row0):
        def f(t):
            P.dma("sp", lambda e: e.dma_start(out=xres[t][:], in_=src[row0 + t * 128: row0 + (t + 1) * 128, :]),
                  xsem[t], writes=[xresB[t]])
        return f

    def proj_tm(c0, ncols, dst_fn, dstB_fn):
        w, wB = wload_std(wi_s, wi_name(c0), 0, 8, c0, ncols)
        for t in range(4):
            bt, bb = nb()
            P.op("pe", [(lambda e, kc=kc, t=t, bt=bt: e.matmul(bt[:, 0:ncols], lhsT=hT[:, kc, t * 128:(t + 1) * 128],
                                                               rhs=w[:, kc, :], start=(kc == 0), stop=(kc == 7)))
                        for kc in range(8)], reads=[wB, hTB], writes=[bb])
            evac_copy(dst_fn(t), bt[:, 0:ncols], [bb], [dstB_fn(t)])

    def proj_kv_a(src=None):
        hTc, hTBc = src if src is not None else (hT, [hTB])
        w, wB = wload_std(wi_s, "wiB", 0, 8, C_KA, 256)
        bank = nb()
        fm_mm(bank, w, wB, 0, hTc, hTBc, 8)
        bt, bb = bank
        P.op("act", [lambda e: e.activation(out=ka_pad[0][0:64, 128:640], in_=bt[0:64, :], func=AF.Copy),
                     lambda e: e.activation(out=ka_pad[1][64:128, 128:640], in_=bt[64:128, :], func=AF.Copy)],
             reads=[bb], writes=kaB[1:5])
        for t in range(4):
            bt2, bb2 = nb()
            P.op("pe", [(lambda e, kc=kc, t=t, bt2=bt2: e.matmul(bt2[:, 0:128], lhsT=hTc[:, kc, t * 128:(t + 1) * 128],
                                                                 rhs=w[:, kc, 128:256], start=(kc == 0),
                                                                 stop=(kc == 7))) for kc in range(8)],
                 reads=[wB] + hTBc, writes=[bb2])
            P.op("act", [lambda e, t=t, bt2=bt2: e.activation(out=va_pad[:, t + 1, 0, 0:64], in_=bt2[:, 0:64],
                                                              func=AF.Copy),
                         lambda e, t=t, bt2=bt2: e.activation(out=va_pad[:, t + 1, 1, 64:128], in_=bt2[:, 64:128],
                                                              func=AF.Copy)],
                 reads=[bb2], writes=[vaB[t + 1]])

    def carry_kv():
        P.op("pool", [lambda e: e.tensor_copy(out=ka_pad[0][:, 0:128], in_=ka_pad[0][:, 512:640]),
                      lambda e: e.tensor_copy(out=ka_pad[1][:, 0:128], in_=ka_pad[1][:, 512:640]),
                      lambda e: e.tensor_copy(out=va_pad[:, 0], in_=va_pad[:, 4])],
             reads=[kaB[4], vaB[4]], writes=[kaB[0], vaB[0]])

    def proj_gla_tm():
        for i in range(2):
            proj_tm(C_KG + i * 256, 256, lambda t, i=i: kgTM[t][:, i * 256:(i + 1) * 256], lambda t: kgTMB[t])
        for i in range(4):
            proj_tm(C_VG + i * 256, 256, lambda t, i=i: vgTM[t][:, i * 256:(i + 1) * 256], lambda t: vgTMB[t])

    def proj_gk(src=None):
        hTc, hTBc = src if src is not None else (hT, [hTB])
        w, wB = wload_std(wi_s, "wiA", 0, 8, C_GK, 16)
        bt, bb = nb()
        P.op("pe", [(lambda e, kc=kc: e.matmul(bt[0:16, :], lhsT=w[:, kc, 0:16], rhs=hTc[:, kc, :], start=(kc == 0),
                                               stop=(kc == 7))) for kc in range(8)], reads=[wB] + hTBc, writes=[bb])
        P.op("act", lambda e: e.activation(out=gkT[0:16, :], in_=bt[0:16, :], func=AF.Copy), reads=[bb],
             writes=[gkTB])

    def prefix_proj_gen(src):
        hTc, hTBc = src
        proj_gk(src)
        yield
        groups = []
        for i in range(2):
            groups.append((wload_std(wi_s, "wiA", 0, 8, C_KG + i * 256, 256), 0, i))
        for i in range(4):
            groups.append((wload_std(wi_s, "wiA", 0, 8, C_VG + i * 256, 256), 1, i))
        for t in range(4):
            for (w, wB), kind, i in groups:
                bt, bb = nb(PPROJ)
                P.op("pe", [(lambda e, kc=kc, t=t, bt=bt, w=w: e.matmul(bt[:, 0:256],
                                                                        lhsT=hTc[:, kc, t * 128:(t + 1) * 128],
                                                                        rhs=w[:, kc, :], start=(kc == 0),
                                                                        stop=(kc == 7))) for kc in range(8)],
                     reads=[wB] + hTBc, writes=[bb])
                if kind == 0:
                    evac_copy(kgTM[t][:, i * 256:(i + 1) * 256], bt[:, 0:256], [bb], [kgTMB[t]])
                else:
                    evac_copy(vgTM[t][:, i * 256:(i + 1) * 256], bt[:, 0:256], [bb], [vgTMB[t]])
            yield

    def og_gen():
        for i in range(4):
            w, wB = wload_std(wi_s, "wiC", 0, 8, C_OG + i * 256, 256)
            for cc in range(2):
                idx = i * 2 + cc
                bank = nb()
                fm_mm(bank, w, wB, cc * 128, hT, [hTB], 8)
                bt, bb = bank
                f = idx % 2
                P.op("act", lambda e, bt=bt, f=f: e.activation(out=ftt[f][:], in_=bt[:, :], func=AF.Exp, scale=-1.0),
                     reads=[bb], writes=[fttB[f]])
                P.op("act", lambda e, f=f: e.activation(out=ftt[f][:], in_=ftt[f][:], func=AF.Ln, bias=1.0),
                     reads=[fttB[f]], writes=[fttB[f]])
                P.op("act", lambda e, f=f: e.activation(out=ftt[f][:], in_=ftt[f][:], func=AF.Exp, scale=-1.0),
                     reads=[fttB[f]], writes=[fttB[f]])
                P.op("dve", lambda e, bt=bt, f=f, idx=idx: e.scalar_tensor_tensor(
                    out=slab[:, S_GSIL + idx, :], in0=bt[:, :], scalar=gnh[:, idx % 2:idx % 2 + 1], in1=ftt[f][:],
                    op0=ALU.mult, op1=ALU.mult), reads=[bb, gnhB, fttB[f]], writes=[slabB[S_GSIL + idx]])
                yield

    state_par = [0]

    def gla_tile(t, full):
        tok = slice(t * 128, (t + 1) * 128)
        par = t % 2
        PGLA = PGLA2[par]
        sp_t, spB, ED, EDB = sp_t2[par], spB2[par], ED2[par], EDB2[par]
        Epos, EposB, Eneg, EnegB = Epos2[par], EposB2[par], Eneg2[par], EnegB2[par]
        qt, qtB, kt, ktB, kd, kdB = qt2[par], qtB2[par], kt2[par], ktB2[par], kd2[par], kdB2[par]
        el, elB = el2[par], elB2[par]
        zt, zb = nb(PGLA)
        P.op("pe", lambda e: e.matmul(zt[:, :], lhsT=gkT[0:17, tok], rhs=wgk[0:17, :], start=True, stop=True),
             reads=[gkTB, wgkB], writes=[zb])
        P.op("act", lambda e: e.activation(out=sp_t[:], in_=zt[:, :], func=AF.Exp, scale=-1.0), reads=[zb],
             writes=[spB])
        P.op("act", lambda e: e.activation(out=sp_t[:], in_=sp_t[:], func=AF.Ln, bias=1.0), reads=[spB],
             writes=[spB])
        yield
        dt_, db = nb(PGLA)
        P.op("pe", lambda e: e.matmul(dt_[:, :], lhsT=SU2m, rhs=sp_t[:], start=True, stop=True),
             reads=[cstB, spB], writes=[db])
        P.op("act", lambda e: e.activation(out=ED[:], in_=dt_[:, :], func=AF.Exp), reads=[db], writes=[EDB])
        if not full:
            et, eb = nb(PGLA)
            P.op("pe", [(lambda e, h=h: e.matmul(et[:, h * 2:(h + 1) * 2], lhsT=sp_t[:, h * 128:(h + 1) * 128],
                                                 rhs=ind2m, start=True, stop=True)) for h in range(4)],
                 reads=[cstB, spB], writes=[eb])
            P.op("act", lambda e: e.activation(out=el[:], in_=et[:, 0:8], func=AF.Exp), reads=[eb], writes=[elB])
        if full:
            gt, gb = nb(PGLA)
            P.op("pe", [(lambda e, h=h: e.matmul(gt[:, h * 128:(h + 1) * 128], lhsT=sp_t[:, h * 128:(h + 1) * 128],
                                                 rhs=U2m, start=True, stop=True)) for h in range(4)],
                 reads=[cstB, spB], writes=[gb])
            P.op("act", lambda e: e.activation(out=Epos[:], in_=gt[:, :], func=AF.Exp), reads=[gb], writes=[EposB])
            P.op("act", lambda e: e.activation(out=Eneg[:], in_=gt[:, :], func=AF.Exp, scale=-1.0), reads=[gb],
                 writes=[EnegB])
            yield
            P.op("dve", lambda e: e.scalar_tensor_tensor(
                out=qt[:], in0=slab[:, S_QG:S_QG + 4, tok], scalar=128.0 ** -0.5,
                in1=Epos[:].rearrange("p (h c) -> p h c", h=4), op0=ALU.mult, op1=ALU.mult),
                reads=slabB[S_QG:S_QG + 4] + [EposB], writes=[qtB])
            P.op("dve", lambda e: e.tensor_tensor(
                out=kt[:], in0=slab[:, S_KG:S_KG + 4, tok], in1=Eneg[:].rearrange("p (h c) -> p h c", h=4),
                op=ALU.mult), reads=slabB[S_KG:S_KG + 4] + [EnegB], writes=[ktB])
        P.op("pool", lambda e: e.tensor_tensor(out=kd[0][0:64, :], in0=kgTM[t][0:64, :], in1=ED[0:64, :], op=ALU.mult),
             reads=[kgTMB[t], EDB], writes=[kdB[0]])
        P.op("pool", lambda e: e.tensor_tensor(out=kd[1][64:128, :], in0=kgTM[t][64:128, :], in1=ED[64:128, :],
                                               op=ALU.mult), reads=[kgTMB[t], EDB], writes=[kdB[1]])
        if not full and pending_casts:
            pending_casts.pop(0)()
        yield
        if full:
            at, ab = nb(PGLA)
            P.op("pe", [(lambda e, h=h: e.matmul(at[:, h * 128:(h + 1) * 128], lhsT=kt[:, h, :], rhs=qt[:, h, :],
                                                 start=True, stop=True)) for h in range(4)],
                 reads=[ktB, qtB], writes=[ab])
            P.op("dve", lambda e: e.tensor_tensor(out=ATs[:], in0=at[:, :].rearrange("p (h c) -> p h c", h=4),
                                                  in1=maskA.unsqueeze(1).broadcast_to([128, 4, 128]), op=ALU.mult),
                 reads=[ab, cstB], writes=[ATsB])
        sps = [None, None]

        def sps_mm(c):
            pair = []
            for hp in range(2):
                st, sbb = nb(PGLA)
                P.op("pe", [(lambda e, hh=hh, hp=hp, c=c, st=st: e.matmul(
                    st[:, hh * 256:(hh + 1) * 256], lhsT=kd[c][:, (hp * 2 + hh) * 128:(hp * 2 + hh + 1) * 128],
                    rhs=vgTM[t][:, (hp * 2 + hh) * 256:(hp * 2 + hh + 1) * 256], start=True, stop=True))
                    for hh in range(2)], reads=[kdB[c], vgTMB[t]], writes=[sbb])
                pair.append((st, sbb))
            sps[c] = pair

        def upd(c):
            for h in range(4):
                st, sbb = sps[c][h // 2]
                if full:
                    ci = h * 128 + c * 64 + 63
                    sc, scB = Epos[:, ci:ci + 1], EposB
                else:
                    sc, scB = el[:, h * 2 + c:h * 2 + c + 1], elB
                P.op("dve", lambda e, h=h, st=st, c=c, sc=sc: e.scalar_tensor_tensor(
                    out=S_st[:, h * 256:(h + 1) * 256], in0=S_st[:, h * 256:(h + 1) * 256],
                    scalar=sc, in1=st[:, (h % 2) * 256:(h % 2 + 1) * 256],
                    op0=ALU.mult, op1=ALU.add), reads=[SB_, scB, sbb], writes=[SB_])

        pa = state_par[0]
        if not full:
            sps_mm(0)
            upd(0)
            sps_mm(1)
            upd(1)
            yield
            return
        sps_mm(0)
        upd(0)
        P.op("act", lambda e: e.activation(out=Sbf[1 - pa][:], in_=S_st[:], func=AF.Copy), reads=[SB_],
             writes=[SbfB[1 - pa]])
        yield
        sps_mm(1)
        upd(1)
        obanks = [nb(PGLA), nb(PGLA)]
        for idx in range(8):
            h, dvc = idx // 2, idx % 2
            ot, ob = obanks[idx // 4]
            oc = (idx % 4) * 128
            col = h * 256 + dvc * 128
            P.op("pe", [
                lambda e, ot=ot, oc=oc, col=col, h=h: e.matmul(ot[:, oc:oc + 128], lhsT=vgTM[t][:, col:col + 128],
                                                               rhs=ATs[:, h, :], start=True, stop=False),
                lambda e, ot=ot, oc=oc, col=col, h=h: e.matmul(ot[:, oc:oc + 64], lhsT=Sbf[pa][:, col:col + 128],
                                                               rhs=qt[:, h, 0:64], start=False, stop=False),
                lambda e, ot=ot, oc=oc, col=col, h=h: e.matmul(ot[:, oc + 64:oc + 128],
                                                               lhsT=Sbf[1 - pa][:, col:col + 128],
                                                               rhs=qt[:, h, 64:128], start=False, stop=True)],
                reads=[vgTMB[t], ATsB, SbfB[0], SbfB[1], qtB], writes=[ob])
        P.op("act", lambda e: e.activation(out=Sbf[pa][:], in_=S_st[:], func=AF.Copy), reads=[SB_],
             writes=[SbfB[pa]])
        for k in range(2):
            ot, ob = obanks[k]
            P.op("act", lambda e, ot=ot, k=k: e.activation(out=sq[:, k * 4:(k + 1) * 4, :],
                                                           in_=ot[:, :].rearrange("p (i c) -> p i c", i=4),
                                                           func=AF.Square), reads=[ob], writes=[sqB])
        yield
        st_, sb_ = nb(PGLA)
        sqv = sq[:].rearrange("p (h v) c -> p h v c", v=2)
        P.op("pe", [(lambda e, v=v: e.matmul(st_[:, :], lhsT=ones_bf[:], rhs=sqv[:, :, v, :], start=(v == 0),
                                             stop=(v == 1))) for v in range(2)], reads=[sqB, onesB], writes=[sb_])
        P.op("act", lambda e: e.activation(out=lnv[:], in_=st_[:, :], func=AF.Ln, scale=1.0 / 256, bias=EPS),
             reads=[sb_], writes=[lnvB])
        P.op("act", lambda e: e.activation(out=rstd_g[:].rearrange("p h c -> p (h c)"), in_=lnv[:], func=AF.Exp,
                                           scale=-0.5), reads=[lnvB], writes=[rstdgB])
        for k in range(2):
            ot, ob = obanks[k]
            P.op("dve", lambda e, ot=ot, k=k: e.tensor_tensor(
                out=otmp[:, k * 4:(k + 1) * 4, :].rearrange("p (h v) c -> p h v c", v=2),
                in0=ot[:, :].rearrange("p (h v c) -> p h v c", h=2, v=2),
                in1=rstd_g[:, k * 2:(k + 1) * 2, :].unsqueeze(2).broadcast_to([128, 2, 2, 128]), op=ALU.mult),
                reads=[ob, rstdgB], writes=[otmpB])
        P.op("pool", lambda e: e.tensor_tensor(out=slab[:, S_OGN:S_OGN + 8, tok], in0=otmp[:],
                                               in1=slab[:, S_GSIL:S_GSIL + 8, tok], op=ALU.mult),
             reads=[otmpB] + slabB[S_GSIL:S_GSIL + 8], writes=slabB[S_OGN:S_OGN + 8])
        yield

    def attn_tile(t, first):
        tok = slice(t * 128, (t + 1) * 128)
        k = 0
        for g in range(2):
            for kb in range(2):
                bt, bb = nb(PATT)
                s_ = t + kb
                add_m0 = first and kb == 0
                fns = [lambda e, bt=bt, g=g, s_=s_: e.matmul(bt[:, :], lhsT=ka_pad[g][:, s_ * 128:(s_ + 1) * 128],
                                                             rhs=slab[:, S_QA:S_QA + 4, tok], start=True, stop=False),
                       lambda e, bt=bt, g=g, kb=kb: e.matmul(bt[:, :], lhsT=ident[:],
                                                             rhs=bias_hi[:, kb, g * 512:(g + 1) * 512], start=False,
                                                             stop=False),
                       lambda e, bt=bt, g=g, kb=kb, add_m0=add_m0: e.matmul(
                           bt[:, :], lhsT=ident[:], rhs=bias_lo[:, kb, g * 512:(g + 1) * 512], start=False,
                           stop=(not add_m0))]
                if add_m0:
                    fns.append(lambda e, bt=bt: e.matmul(bt[:, :], lhsT=ident[:],
                                                         rhs=mask0_bf[:].unsqueeze(1).broadcast_to([128, 4, 128]),
                                                         start=False, stop=True))
                P.op("pe", fns, reads=[kaB[s_], identB, biasHLB, mask0bB] + slabB[S_QA:S_QA + 4], writes=[bb])
                P.op("act", lambda e, bt=bt, k=k: e.activation(out=esT[:, k, :], in_=bt[:, :], func=AF.Exp, scale=0.125),
                     reads=[bb], writes=[esTB[k]])
                k += 1
                if k == 2:
                    yield
        yield
        ot, ob = nb(PATT)
        dt_, db = nb(PATT)
        fo, fd = [], []
        for k, (g, kb) in enumerate([(0, 0), (0, 1), (1, 0), (1, 1)]):
            s_ = t + kb
            fo.append(lambda e, g=g, s_=s_, k=k: e.matmul(ot[:, :], lhsT=va_pad[:, s_, g, :], rhs=esT[:, k, :],
                                                          start=(k == 0), stop=(k == 3)))
            fd.append(lambda e, g=g, k=k: e.matmul(dt_[:, :], lhsT=ones_pad[:, g, :], rhs=esT[:, k, :],
                                                   start=(k == 0), stop=(k == 3)))
        P.op("pe", fd, reads=[onespB] + esTB, writes=[db])
        P.op("pe", fo, reads=[vaB[t], vaB[t + 1]] + esTB, writes=[ob])
        P.op("dve", lambda e: e.tensor_tensor(out=dtmp[:].rearrange("p (j q) -> p j q", j=4),
                                              in0=dt_[:, :].rearrange("p (j q) -> p j q", j=4),
                                              in1=esink[:].unsqueeze(2).broadcast_to([128, 4, 128]), op=ALU.add),
             reads=[db, esinkB], writes=[dtmpB])
        P.op("act", lambda e: e.activation(out=dtmp[:], in_=dtmp[:], func=AF.Ln), reads=[dtmpB], writes=[dtmpB])
        P.op("act", lambda e: e.activation(out=dtmp[:], in_=dtmp[:], func=AF.Exp, scale=-1.0), reads=[dtmpB],
             writes=[dtmpB])
        yield
        P.op("dve", lambda e: e.tensor_tensor(out=slab[:, S_OA:S_OA + 4, tok],
                                              in0=ot[:, :].rearrange("p (j q) -> p j q", j=4),
                                              in1=dtmp[:].rearrange("p (j q) -> p j q", j=4), op=ALU.mult),
             reads=[ob, dtmpB], writes=slabB[S_OA:S_OA + 4])
        yield

    def run_merged(gens):
        gens = list(gens)
        while gens:
            for g in list(gens):
                try:
                    next(g)
                except StopIteration:
                    gens.remove(g)

    def run_weighted(pairs):
        pairs = [[g, w] for g, w in pairs]
        while pairs:
            for pr in list(pairs):
                for _ in range(pr[1]):
                    try:
                        next(pr[0])
                    except StopIteration:
                        pairs.remove(pr)
                        break

    def run_gla_region(full, extra, delay):
        ge = chain(gla_tile(t, full) for t in (0, 2))
        go = chain(gla_tile(t, full) for t in (1, 3))
        gens = [[ge, 0], [go, delay]] + [[g, 0] for g in extra]
        step = 0
        while gens:
            for pr in list(gens):
                if step < pr[1]:
                    continue
                try:
                    next(pr[0])
                except StopIteration:
                    gens.remove(pr)
            step += 1

    def chain(gs):
        for g in gs:
            yield from g

    NPB = NTP // 4
    hT_alt = (slab[:, 0:8, :], slabB[0:8])

    def hsel(bp):
        return hT_alt if (NPB - 1 - bp) % 2 == 0 else (hT, [hTB])

    mark('P0.norm')
    norm_to_hT(0, load_x(xp_d, 0), hsel(0))
    for bp in range(NPB):
        last = bp == NPB - 1
        src = hsel(bp)
        mark(f'P{bp}.proj')
        ld = load_x(x_d, 0) if last else load_x(xp_d, (bp + 1) * 512)
        for t in range(4):
            ld(t)
        if last:
            proj_kv_a(src)
            carry_kv()
            nxt = norm_gen(0, None, None, PPROJ)
        else:
            nxt = norm_gen(0, None, hsel(bp + 1), PPROJ)
        pg = prefix_proj_gen(src)
        next(pg)
        next(pg)
        next(pg)
        mark(f'P{bp}.gla')
        run_gla_region(False, [pg, nxt], 2)
    while pending_casts:
        pending_casts.pop(0)()
    P.op("act", lambda e: e.activation(out=Sbf[0][:], in_=S_st[:], func=AF.Copy), reads=[SB_], writes=[SbfB[0]])
    state_par[0] = 0

    psem = [P.new_sem() for _ in range(2)]
    for b in range(NT // 4):
        row0 = b * 512
        mark(f'M{b}.norm1')
        if b > 0:
            norm_to_hT(0, None)
        mark(f'M{b}.inproj')
        for j0 in (0, 2):
            srcs = []
            for g in range(2):
                for jj in range(2):
                    c0 = C_QA + g * 256 + (j0 + jj) * 64
                    src = wi_s[:, c0:c0 + 64].rearrange("(k p) c -> p k c", p=128)
                    srcs.append((lambda t, g=g, jj=jj: t[:, 0:2048].rearrange(
                        "p (k j g d) -> p k j g d", k=8, j=2, g=2)[:, :, jj, g, :], src))
            wt, wB = wload(srcs, "wiB", 8, 256)
            wv = wt[:, 0:2048].rearrange("p (k j c) -> p k j c", k=8, j=2)
            for jj in range(2):
                bt, bb = nb()
                P.op("pe", [(lambda e, kc=kc, jj=jj, bt=bt, wv=wv: e.matmul(bt[:, :], lhsT=wv[:, kc, jj, :],
                                                                            rhs=hT[:, kc, :], start=(kc == 0),
                                                                            stop=(kc == 7))) for kc in range(8)],
                     reads=[wB, hTB], writes=[bb])
                evac_copy(slab[:, S_QA + j0 + jj, :], bt[:, :], [bb], [slabB[S_QA + j0 + jj]])
        proj_kv_a()
        for (c_base, s_base) in ((C_QG, S_QG), (C_KG, S_KG)):
            for i in range(2):
                w, wB = wload_std(wi_s, wi_name(c_base + i * 256), 0, 8, c_base + i * 256, 256)
                for cc in range(2):
                    bank = nb()
                    fm_mm(bank, w, wB, cc * 128, hT, [hTB], 8)
                    evac_copy(slab[:, s_base + i * 2 + cc, :], bank[0][:, :], [bank[1]], [slabB[s_base + i * 2 + cc]])
        proj_gk()
        proj_gla_tm()
        mark(f'M{b}.attn')
        for _ in og_gen():
            pass
        run_gla_region(True, [chain(attn_tile(t, first=(b == 0 and t == 0)) for t in range(4))], 3)
        carry_kv()
        mark(f'M{b}.mixed')
        for qq in range(4):
            wga, wgaB = wload_std(wi_s, "wiC", 0, 8, C_GA + qq * 256, 256)
            wgg, wggB = wload_std(wi_s, "wiD", 0, 8, C_GG + qq * 256, 256)
            srcs = []
            for g in range(2):
                src = wpa_s[g * 256:(g + 1) * 256, qq * 256:(qq + 1) * 256].rearrange("(j p) c -> p j c", p=64)
                srcs.append((lambda t, g=g: t[g * 64:(g + 1) * 64, 0:1024].rearrange("p (j c) -> p j c", j=4), src))
            wpat, wpaB = wload(srcs, "wpa", 4, 256)
            wpa = wpat[:, 0:1024].rearrange("p (j c) -> p j c", j=4)
            wpg, wpgB = wload_std(wpg_s, "wpg", 0, 8, qq * 256, 256)
            for mm_ in range(2):
                m = qq * 2 + mm_
                bga, bgg, bya, byg = nb(), nb(), nb(), nb()
                fm_mm(bga, wga, wgaB, mm_ * 128, hT, [hTB], 8)
                P.op("act", lambda e, bt=bga[0]: e.activation(out=ta[:], in_=bt[:, :], func=AF.Tanh, scale=0.5),
                     reads=[bga[1]], writes=[taB])
                fm_mm(bgg, wgg, wggB, mm_ * 128, hT, [hTB], 8)
                P.op("act", lambda e, bt=bgg[0]: e.activation(out=tg[:], in_=bt[:, :], func=AF.Tanh, scale=0.5),
                     reads=[bgg[1]], writes=[tgB])
                fm_mm(bya, wpa, wpaB, mm_ * 128, slab[:, S_OA:S_OA + 4, :], slabB[S_OA:S_OA + 4], 4)
                fm_mm(byg, wpg, wpgB, mm_ * 128, slab[:, S_OGN:S_OGN + 8, :], slabB[S_OGN:S_OGN + 8], 8)
                P.op("dve", lambda e, bt=bya[0]: e.scalar_tensor_tensor(out=m1[:], in0=ta[:], scalar=1.0, in1=bt[:, :],
                                                                        op0=ALU.add, op1=ALU.mult),
                     reads=[taB, bya[1]], writes=[m1B])
                P.op("dve", lambda e, bt=byg[0]: e.scalar_tensor_tensor(out=m2[:], in0=tg[:], scalar=1.0, in1=bt[:, :],
                                                                        op0=ALU.add, op1=ALU.mult),
                     reads=[tgB, byg[1]], writes=[m2B])
                P.op("pool", lambda e, m=m: e.tensor_tensor(out=slab[:, S_MIX + m, :], in0=m1[:], in1=m2[:],
                                                            op=ALU.add),
                     reads=[m1B, m2B], writes=[slabB[S_MIX + m]])

        def proj_res(scr, name, nkc, act_view, actB, epi):
            for half in range(2):
                tb = [nb() for _ in range(4)]
                ngrp = (nkc + 3) // 4
                for kg in range(ngrp):
                    k0 = kg * 4
                    nk = min(4, nkc - k0)
                    w, wB = wload_std(scr, name, k0 * 128, nk, half * 512, 512)
                    for t in range(4):
                        bt, bb = tb[t]
                        P.op("pe", [(lambda e, kk=kk, t=t, bt=bt, w=w, k0=k0: e.matmul(
                            bt[:, :], lhsT=act_view[:, k0 + kk, t * 128:(t + 1) * 128], rhs=w[:, kk, :],
                            start=(k0 + kk == 0), stop=(k0 + kk == nkc - 1))) for kk in range(nk)],
                            reads=[wB] + list(actB), writes=[bb])
                for t in range(4):
                    epi(t, half, tb[t])

        def epi_half(t, half, bank):
            bt, bb = bank
            hs = slice(half * 512, (half + 1) * 512)
            P.op("dve", lambda e: e.scalar_tensor_tensor(out=xres[t][:, hs], in0=bt[:, :], scalar=0.5,
                                                         in1=xres[t][:, hs], op0=ALU.mult, op1=ALU.add),
                 reads=[bb, xresB[t]], writes=[xresB[t]])

        def epi_add(t, half, bank):
            bt, bb = bank
            hs = slice(half * 512, (half + 1) * 512)
            P.op("dve", lambda e: e.tensor_tensor(out=xres[t][:, hs], in0=bt[:, :], in1=xres[t][:, hs], op=ALU.add),
                 reads=[bb, xresB[t]], writes=[xresB[t]])

        mark(f'M{b}.wout')
        proj_res(wout_s, "wout", 8, slab[:, S_MIX:S_MIX + 8, :], slabB[S_MIX:S_MIX + 8], epi_half)

        mark(f'M{b}.norm2')
        norm_to_hT(1, None)
        mark(f'M{b}.ffnin')
        for ii in range(11):
            wg_, wgB_ = wload_std(wfi_s, "wfi", 0, 8, ii * 256, 256)
            wu_, wuB_ = wload_std(wfi_s, "wfi", 0, 8, D_FF + ii * 256, 256)
            for s in range(2):
                c = ii * 2 + s
                bg, bu = nb(), nb()
                fm_mm(bg, wg_, wgB_, s * 128, hT, [hTB], 8)
                fm_mm(bu, wu_, wuB_, s * 128, hT, [hTB], 8)
                f = c % 2
                P.op("act", lambda e, bt=bg[0], f=f: e.activation(out=ftt[f][:], in_=bt[:, :], func=AF.Tanh, scale=0.5),
                     reads=[bg[1]], writes=[fttB[f]])
                P.op("dve", lambda e, bt=bg[0], f=f: e.scalar_tensor_tensor(out=fa[f][:], in0=ftt[f][:], scalar=1.0,
                                                                           in1=bt[:, :], op0=ALU.add, op1=ALU.mult),
                     reads=[fttB[f], bg[1]], writes=[faB[f]])
                P.op("dve", lambda e, bt=bu[0], f=f, c=c: e.scalar_tensor_tensor(
                    out=slab[:, S_ACT + c, :], in0=fa[f][:], scalar=0.5, in1=bt[:, :], op0=ALU.mult, op1=ALU.mult),
                    reads=[faB[f], bu[1]], writes=[slabB[S_ACT + c]])
        mark(f'M{b}.ffnout')
        proj_res(wfo_s, "wfo", 22, slab[:, S_ACT:S_ACT + 22, :], slabB[S_ACT:S_ACT + 22], epi_add)

        mark(f'M{b}.norm3')
        norm_to_hT(2, None)
        mark(f'M{b}.ple')
        for t in range(4):
            i = t % 2
            P.dma("sp", lambda e, t=t, i=i, row0=row0: e.dma_start(out=p_f[i][:], in_=p_d[row0 + t * 128: row0 + (t + 1) * 128, :]),
                  psem[i], writes=[p_fB[i]])
            P.op("pool", lambda e, i=i: e.tensor_copy(out=p_bf[i][:], in_=p_f[i][:]), reads=[p_fB[i]],
                 writes=[p_bfB[i]])
            bt, bb = nb()
            btb = bt[:].bitcast(BF16)
            P.op("pe", [(lambda e, kc=kc, i=i, btb=btb: e.transpose(out=btb[:, kc * 128:(kc + 1) * 128],
                                                                    in_=p_bf[i][:, kc * 128:(kc + 1) * 128],
                                                                    identity=ident[:])) for kc in range(2)],
                 reads=[p_bfB[i], identB], writes=[bb])
            P.op("act", lambda e, t=t, btb=btb: e.activation(out=pT[:, :, t * 128:(t + 1) * 128],
                                                             in_=btb[:, 0:256].rearrange("p (k c) -> p k c", k=2),
                                                             func=AF.Copy), reads=[bb], writes=[pTB])

        def epi_gate(t, half, bank):
            bt, bb = bank
            P.op("act", lambda e: e.activation(out=esT[:, t, :], in_=bt[:, :], func=AF.Tanh, scale=0.5), reads=[bb],
                 writes=[tgtB[t]])

        def epi_ple(t, half, bank):
            bt, bb = bank
            hs = slice(half * 512, (half + 1) * 512)
            P.op("dve", lambda e: e.scalar_tensor_tensor(out=m1[:], in0=esT[:, t, :], scalar=1.0, in1=bt[:, :],
                                                         op0=ALU.add, op1=ALU.mult),
                 reads=[tgtB[t], bb], writes=[m1B])
            P.op("dve", lambda e: e.scalar_tensor_tensor(out=xres[t][:, hs], in0=m1[:], scalar=0.5, in1=xres[t][:, hs],
                                                         op0=ALU.mult, op1=ALU.add),
                 reads=[m1B, xresB[t]], writes=[xresB[t]])

        for half in range(2):
            tb = [nb() for _ in range(4)]
            for kg in range(2):
                w, wB = wload_std(wpgate_s, "wpgate", kg * 512, 4, half * 512, 512)
                for t in range(4):
                    bt, bb = tb[t]
                    P.op("pe", [(lambda e, kk=kk, t=t, bt=bt, w=w, kg=kg: e.matmul(
                        bt[:, :], lhsT=hT[:, kg * 4 + kk, t * 128:(t + 1) * 128], rhs=w[:, kk, :],
                        start=(kg * 4 + kk == 0), stop=(kg * 4 + kk == 7))) for kk in range(4)],
                        reads=[wB, hTB], writes=[bb])
            for t in range(4):
                epi_gate(t, half, tb[t])
            w, wB = wload_std(wple_s, "wple", 0, 2, half * 512, 512)
            for t in range(4):
                bt, bb = nb()
                P.op("pe", [(lambda e, kk=kk, t=t, bt=bt, w=w: e.matmul(bt[:, :], lhsT=pT[:, kk, t * 128:(t + 1) * 128],
                                                                        rhs=w[:, kk, :], start=(kk == 0),
                                                                        stop=(kk == 1))) for kk in range(2)],
                     reads=[wB, pTB], writes=[bb])
                epi_ple(t, half, (bt, bb))

        mark(f'M{b}.final')
        nxt_load = load_x(x_d, row0 + 512) if b + 1 < NT // 4 else None

        def out_dma(t):
            o = P.dma("sp", lambda e, t=t, row0=row0: e.dma_start(out=out_d[row0 + t * 128: row0 + (t + 1) * 128, :],
                                                              in_=ystage[t]), osem[t], reads=ystageB[t])
            if b == NT // 4 - 1:
                P.final.append(o)

        for t in range(4):
            P.op("act", lambda e, t=t: e.activation(out=junk_ap, in_=xres[t][:], func=AF.Square,
                                                   accum_out=stat[t][:, 0:1]),
                 reads=[xresB[t]], writes=[junkB, statB[t]])
            P.op("pool", lambda e, t=t: e.tensor_scalar(out=stat[t][:, 1:2], in0=stat[t][:, 0:1], scalar1=1.0 / D,
                                                       scalar2=EPS, op0=ALU.mult, op1=ALU.add),
                 reads=[statB[t]], writes=[statB[t]])
            P.op("pool", lambda e, t=t: e.tensor_tensor(out=stat[t][:, 2:3], in0=stat[t][:, 1:2], in1=mhalf[:, 0:1],
                                                       op=ALU.pow), reads=[statB[t], mhalfB], writes=[statB[t]])
            P.op("dve", lambda e, t=t: e.scalar_tensor_tensor(out=ystage[t], in0=xres[t][:],
                                                             scalar=stat[t][:, 2:3], in1=gfin[:], op0=ALU.mult,
                                                             op1=ALU.mult),
                 reads=[xresB[t], statB[t], gfinB], writes=ystageB[t])
            if nxt_load is not None:
                nxt_load(t)
            out_dma(t)

    mark('end')
    nc._marks = marks
    P.emit()
    return nc


def _t5_bucket(dist):
    max_exact = 16
    d_f = np.maximum(dist, max_exact).astype(np.float32)
    large = max_exact + (np.log(d_f / max_exact) / math.log(128 / max_exact) * (32 - max_exact)).astype(np.int32)
    large = np.minimum(large, 31)
    return np.where(dist < max_exact, dist, large)


def _consts():
    s = np.arange(128)[:, None]
    t = np.arange(128)[None, :]
    same = (s // 64) == (t // 64)
    cst = np.zeros((128, 386), np.float32)
    cst[:, 0:128] = np.where(same & (s <= t), -1.0 / 16, 0.0)
    cst[:, 128:256] = np.where(same & (s > t), -1.0 / 16, 0.0)
    cst[:, 256] = np.where(np.arange(128) < 64, -1.0 / 16, 0.0)
    cst[:, 257] = np.where(np.arange(128) >= 64, -1.0 / 16, 0.0)
    cst[:, 258:386] = np.where(same & (s <= t), 1.0, 0.0)
    k = np.arange(128)[:, None]
    q = np.arange(128)[None, :]
    maskT = np.zeros((2, 128, 128), np.float32)
    maskT[0] = np.where(k > q, 0.0, NEG)
    maskT[1] = np.where(k <= q, 0.0, NEG)
    bucket = np.zeros((2, 128, 128), np.int64)
    for kb in range(2):
        dist = q + 128 - (kb * 128 + k)
        bucket[kb] = _t5_bucket(np.maximum(dist, 0))
    return cst, maskT, bucket


def _host_inputs(x, p, w_in, w_gk2, b_gk, sinks, rel_table, w_proj_attn, w_proj_gla, gla_norm, w_out, norm_mix,
                 norm_ffn, w_ffn_in, w_ffn_out, norm_ple, w_ple_gate, w_ple, norm_final, T):
    f = lambda a: np.ascontiguousarray(np.asarray(a, dtype=np.float32))
    x, p = f(x), f(p)
    B, S, _ = x.shape
    halves = S // T
    cst, maskT, bucket = _consts()
    rel = f(rel_table)
    biasT = np.ascontiguousarray(np.transpose(rel[bucket], (0, 1, 3, 2))).reshape(2, 128, 1024)
    sk = f(sinks)[0]
    sink_l = np.zeros((128, 4), np.float32)
    sink_l[0:64, :] = sk[0:4][None, :]
    sink_l[64:128, :] = sk[4:8][None, :]
    gcols = np.concatenate([f(norm_mix)[0].reshape(8, 128).T, f(norm_ffn)[0].reshape(8, 128).T,
                            f(norm_ple)[0].reshape(8, 128).T], axis=1)
    gn_col = np.ascontiguousarray(f(gla_norm)[0].reshape(2, 128).T)
    wgk_aug = np.concatenate([f(w_gk2)[0], f(b_gk)[0][None, :]], axis=0)
    shared = {
        "w_in": f(w_in)[0], "w_proj_attn": f(w_proj_attn)[0], "w_proj_gla": f(w_proj_gla)[0], "w_out": f(w_out)[0],
        "w_ffn_in": f(w_ffn_in)[0], "w_ffn_out": f(w_ffn_out)[0], "w_ple_gate": f(w_ple_gate)[0],
        "w_ple": f(w_ple)[0], "wgk_aug": np.ascontiguousarray(wgk_aug), "gcols": np.ascontiguousarray(gcols),
        "gn_col": gn_col, "norm_final": f(norm_final), "sink_l": sink_l, "biasT": biasT, "maskT": maskT, "cst": cst,
    }
    maps = []
    for b in range(B):
        for h in range(halves):
            m = dict(shared)
            m["x"] = np.ascontiguousarray(x[b, h * T:(h + 1) * T])
            m["p"] = np.ascontiguousarray(p[0, b, h * T:(h + 1) * T])
            if h == 0:
                m["xp"] = np.zeros((T, D), np.float32)
                m["mask0"] = np.full((128, 128), NEG, np.float32)
            else:
                m["xp"] = np.ascontiguousarray(x[b, (h - 1) * T:h * T])
                m["mask0"] = np.zeros((128, 128), np.float32)
            maps.append(m)
    return maps, B, halves


def run(inputs, T):
    maps, B, halves = _host_inputs(T=T, **inputs)
    nc = build_nc(T // 128, T // 128)
    n = len(maps)
    res = run_bass_kernel_spmd(nc, maps, core_ids=list(range(n)))
    out = np.zeros((B, halves * T, D), np.float32)
    for i, r in enumerate(res.results):
        b, h = i // halves, i % halves
        out[b, h * T:(h + 1) * T] = r["out"]
    return out


def kernel(**inputs):
    return run(inputs, 4096)
```
